# Optimizing a Trainium2 kernel written in Bass

```python
import jax, jax.numpy as jnp
from jax import lax
import numpy as np

D_MODEL = 1024
BATCH = 8
SEQ = 4096
DEPTH = 1
DEC_BATCH = 8
DEC_SEQ = 16
PAST_LEN = 4096

CHUNK = 64
Q_BLOCK = 128
HG_HEADS = 4
HG_DK = 128
HG_DV = 128
MLA_HEADS = 4
MLA_NOPE = 128
MLA_ROPE = 64
MLA_V = 128
Q_LORA = 384
KV_LORA = 256
ROPE_THETA = 10000.0
MLA_SCALE = (MLA_NOPE + MLA_ROPE) ** -0.5
MIX_WIDTH = HG_HEADS * HG_DV + MLA_HEADS * MLA_V
IN_WIDTH = 2 * HG_HEADS * HG_DK + 2 * HG_HEADS * HG_DV + Q_LORA + KV_LORA + MLA_ROPE
D_FF = 2816
CONV_W = 3
EPS = 1e-6

kernel_name = 'hymba_hgrn2_mla_convglu_adaln_stream_step'


def rmsnorm(x, gain):
    xf = x.astype(jnp.float32)
    y = xf * lax.rsqrt(jnp.mean(xf * xf, axis=-1, keepdims=True) + EPS)
    return (y * gain.astype(jnp.float32)).astype(x.dtype)


def rope(x, pos):
    half = x.shape[-1] // 2
    inv_freq = ROPE_THETA ** (-jnp.arange(half, dtype=jnp.float32) / half)
    ang = pos.astype(jnp.float32)[:, None] * inv_freq[None, :]
    cos = jnp.cos(ang)[None, :, None, :]
    sin = jnp.sin(ang)[None, :, None, :]
    xf = x.astype(jnp.float32)
    x1, x2 = xf[..., :half], xf[..., half:]
    return jnp.concatenate([x1 * cos - x2 * sin, x2 * cos + x1 * sin], axis=-1).astype(x.dtype)


def hgrn_chunk(s0, q, k, v, logf):
    L = q.shape[1]
    b = jnp.cumsum(logf, axis=1)
    causal = jnp.tril(jnp.ones((L, L), dtype=bool))
    diff = b[:, :, None] - b[:, None, :]
    decay = jnp.exp(jnp.where(causal[None, :, :, None, None], diff, -jnp.inf))
    scores = jnp.einsum('bthk,bshk,btshk->bhts', q, k, decay)
    o = (jnp.einsum('bthk,bhkv->bthv', q * jnp.exp(b), s0)
         + jnp.einsum('bhts,bshv->bthv', scores, v))
    b_last = b[:, -1]
    s_new = (jnp.exp(b_last)[..., None] * s0
             + jnp.einsum('bshk,bshv->bhkv', k * jnp.exp(b_last[:, None] - b), v))
    return s_new, o


def hgrn_prompt(q, k, v, logf):
    B, S = q.shape[:2]
    nc = S // CHUNK

    def to_chunks(t):
        return t.reshape(B, nc, CHUNK, *t.shape[2:]).swapaxes(0, 1)

    s0 = jnp.zeros((B, HG_HEADS, HG_DK, HG_DV), jnp.float32)
    s_final, o = lax.scan(lambda s, xs: hgrn_chunk(s, *xs), s0,
                          (to_chunks(q), to_chunks(k), to_chunks(v), to_chunks(logf)))
    return s_final, o.swapaxes(0, 1).reshape(B, S, HG_HEADS, HG_DV)


def hgrn_mixer(hq, hf, hi, hg, lb, gain, state):
    B, L = hq.shape[:2]
    dt = hq.dtype
    f32 = jnp.float32
    q = hq.astype(f32).reshape(B, L, HG_HEADS, HG_DK) * HG_DK ** -0.5
    f = lb + (1.0 - lb) * jax.nn.sigmoid(hf.astype(f32))
    k = (1.0 - f).reshape(B, L, HG_HEADS, HG_DK)
    logf = jnp.log(f).reshape(B, L, HG_HEADS, HG_DK)
    v = hi.astype(f32).reshape(B, L, HG_HEADS, HG_DV)
    if state is None:
        s_new, o = hgrn_prompt(q, k, v, logf)
    else:
        s_new, o = hgrn_chunk(state.astype(f32), q, k, v, logf)
    o = o * lax.rsqrt(jnp.mean(o * o, axis=-1, keepdims=True) + EPS) * gain.astype(f32)
    o = o.reshape(B, L, HG_HEADS * HG_DV) * jax.nn.silu(hg.astype(f32))
    return o.astype(dt), s_new.astype(dt)


def attend(q_nope, q_pe, k_nope, k_pe, v, mask):
    s = (jnp.einsum('bqhd,bkhd->bhqk', q_nope, k_nope)
         + jnp.einsum('bqhr,bkr->bhqk', q_pe, k_pe)).astype(jnp.float32) * MLA_SCALE
    if mask is not None:
        s = jnp.where(mask, s, -jnp.inf)
    p = jax.nn.softmax(s, axis=-1).astype(v.dtype)
    return jnp.einsum('bhqk,bkhd->bqhd', p, v)


def chunk_causal_attention(q_nope, q_pe, k_nope, k_pe, v):
    B, L = q_nope.shape[:2]
    nb = L // Q_BLOCK
    key_chunk = jnp.arange(L) // CHUNK

    def blocks(t):
        return t.reshape(B, nb, Q_BLOCK, *t.shape[2:]).swapaxes(0, 1)

    def one_block(xs):
        qn, qp, bi = xs
        q_chunk = (bi * Q_BLOCK + jnp.arange(Q_BLOCK)) // CHUNK
        mask = key_chunk[None, :] <= q_chunk[:, None]
        return attend(qn, qp, k_nope, k_pe, v, mask[None, None])

    o = lax.map(one_block, (blocks(q_nope), blocks(q_pe), jnp.arange(nb)))
    return o.swapaxes(0, 1).reshape(B, L, MLA_HEADS, MLA_V)


def mla_mixer(hcq, hckv, hkpe, pos, qn_g, kvn_g, w_uq, w_uk, w_uv, cache_lat, cache_kpe):
    B, L = hcq.shape[:2]
    q = (rmsnorm(hcq, qn_g) @ w_uq).reshape(B, L, MLA_HEADS, MLA_NOPE + MLA_ROPE)
    q_nope = q[..., :MLA_NOPE]
    q_pe = rope(q[..., MLA_NOPE:], pos)
    lat = rmsnorm(hckv, kvn_g)
    kpe = rope(hkpe[:, :, None, :], pos)[:, :, 0, :]
    if cache_lat is None:
        k_lat, k_pe_all = lat, kpe
    else:
        k_lat = jnp.concatenate([cache_lat.astype(lat.dtype), lat], axis=1)
        k_pe_all = jnp.concatenate([cache_kpe.astype(kpe.dtype), kpe], axis=1)
    k_nope = jnp.einsum('bkc,chd->bkhd', k_lat, w_uk)
    v = jnp.einsum('bkc,chd->bkhd', k_lat, w_uv)
    if cache_lat is None:
        o = chunk_causal_attention(q_nope, q_pe, k_nope, k_pe_all, v)
    else:
        o = attend(q_nope, q_pe, k_nope, k_pe_all, v, None)
    return o.reshape(B, L, MLA_HEADS * MLA_V), lat, kpe


def conv_glu(h, w_up, conv_w, conv_b, w_down, buf):
    B, L = h.shape[:2]
    a, v = jnp.split(h @ w_up, 2, axis=-1)
    if buf is None:
        buf = jnp.zeros((B, CONV_W - 1, D_FF), a.dtype)
    ext = jnp.concatenate([buf.astype(a.dtype), a], axis=1)
    conv = conv_b + sum(conv_w[j] * ext[:, j:j + L] for j in range(CONV_W))
    out = (jax.nn.gelu(conv, approximate=False) * v) @ w_down
    return out, ext[:, L:]


def trunk_layer(x, c, pos, w, cache_lat, cache_kpe, st_hg, st_conv):
    (w_ada, b_ada, g_mix, w_in, lb, hg_g, qn_g, kvn_g, w_uq, w_uk, w_uv, w_out,
     g_ffn, w_up, conv_w, conv_b, w_down) = w
    mod = (jax.nn.silu(c) @ w_ada + b_ada)[:, None, :]
    sh1, sc1, g1, sh2, sc2, g2 = jnp.split(mod, 6, axis=-1)
    h = rmsnorm(x, g_mix) * (1.0 + sc1) + sh1
    proj = h @ w_in
    qk = HG_HEADS * HG_DK
    vw = HG_HEADS * HG_DV
    points = [qk, 2 * qk, 2 * qk + vw, 2 * qk + 2 * vw,
              2 * qk + 2 * vw + Q_LORA, 2 * qk + 2 * vw + Q_LORA + KV_LORA]
    hq, hf, hi, hg, hcq, hckv, hkpe = jnp.split(proj, points, axis=-1)
    o_hg, st_hg_new = hgrn_mixer(hq, hf, hi, hg, lb, hg_g, st_hg)
    o_mla, lat_new, kpe_new = mla_mixer(hcq, hckv, hkpe, pos, qn_g, kvn_g, w_uq, w_uk, w_uv,
                                        cache_lat, cache_kpe)
    x = x + g1 * (jnp.concatenate([o_hg, o_mla], axis=-1) @ w_out)
    h2 = rmsnorm(x, g_ffn) * (1.0 + sc2) + sh2
    f, conv_new = conv_glu(h2, w_up, conv_w, conv_b, w_down, st_conv)
    x = x + g2 * f
    return x, lat_new, kpe_new, st_hg_new, conv_new


def setup_inputs(seed: int = 0) -> dict:
    key = jax.random.key(seed)
    ks = jax.random.split(key, 26)
    D = D_MODEL
    L = DEPTH

    def nrm(k, shape, scale):
        return jax.random.normal(k, shape, jnp.float32) * scale

    return {
        'x_prompt': nrm(ks[0], (BATCH, SEQ, D), 1.0),
        'x_sample': nrm(ks[1], (DEC_BATCH, DEC_SEQ, D), 1.0),
        'c_prompt': nrm(ks[2], (BATCH, D), 1.0),
        'c_sample': nrm(ks[3], (DEC_BATCH, D), 1.0),
        'cache_kv_latent': nrm(ks[4], (L, DEC_BATCH, PAST_LEN, KV_LORA), 1.0),
        'cache_k_rope': nrm(ks[5], (L, DEC_BATCH, PAST_LEN, MLA_ROPE), 1.0),
        'state_hgrn': nrm(ks[6], (L, DEC_BATCH, HG_HEADS, HG_DK, HG_DV), 0.5),
        'state_ffn_conv': nrm(ks[7], (L, DEC_BATCH, CONV_W - 1, D_FF), 1.0),
        'w_ada': nrm(ks[8], (L, D, 6 * D), 0.5 * D ** -0.5),
        'b_ada': nrm(ks[9], (L, 6 * D), 0.01),
        'norm_mix_gain': 1.0 + nrm(ks[10], (L, D), 0.02),
        'w_in': nrm(ks[11], (L, D, IN_WIDTH), D ** -0.5),
        'hg_lb_logits': nrm(ks[12], (L + 1, HG_HEADS * HG_DK), 0.1),
        'hg_norm_gain': 1.0 + nrm(ks[13], (L, HG_DV), 0.02),
        'mla_q_norm_gain': 1.0 + nrm(ks[14], (L, Q_LORA), 0.02),
        'mla_kv_norm_gain': 1.0 + nrm(ks[15], (L, KV_LORA), 0.02),
        'w_uq': nrm(ks[16], (L, Q_LORA, MLA_HEADS * (MLA_NOPE + MLA_ROPE)), Q_LORA ** -0.5),
        'w_uk': nrm(ks[17], (L, KV_LORA, MLA_HEADS, MLA_NOPE), KV_LORA ** -0.5),
        'w_uv': nrm(ks[18], (L, KV_LORA, MLA_HEADS, MLA_V), KV_LORA ** -0.5),
        'w_out': nrm(ks[19], (L, MIX_WIDTH, D), MIX_WIDTH ** -0.5),
        'norm_ffn_gain': 1.0 + nrm(ks[20], (L, D), 0.02),
        'w_up': nrm(ks[21], (L, D, 2 * D_FF), D ** -0.5),
        'conv_w': nrm(ks[22], (L, CONV_W, D_FF), CONV_W ** -0.5),
        'conv_b': nrm(ks[23], (L, D_FF), 0.01),
        'w_down': nrm(ks[24], (L, D_FF, D), D_FF ** -0.5),
        'final_norm_gain': 1.0 + nrm(ks[25], (D,), 0.02),
    }


def reference(x_prompt, x_sample, c_prompt, c_sample, cache_kv_latent, cache_k_rope, state_hgrn,
              state_ffn_conv, w_ada, b_ada, norm_mix_gain, w_in, hg_lb_logits, hg_norm_gain,
              mla_q_norm_gain, mla_kv_norm_gain, w_uq, w_uk, w_uv, w_out, norm_ffn_gain, w_up,
              conv_w, conv_b, w_down, final_norm_gain):
    lower_bounds = jnp.cumsum(jax.nn.softmax(hg_lb_logits.astype(jnp.float32), axis=0), axis=0)
    pos_p = jnp.arange(x_prompt.shape[1])
    pos_s = PAST_LEN + jnp.arange(x_sample.shape[1])
    h_p, h_s = x_prompt, x_sample
    lat_p, kpe_p, hg_p, cv_p = [], [], [], []
    lat_s, kpe_s, hg_s, cv_s = [], [], [], []
    for l in range(DEPTH):
        w = (w_ada[l], b_ada[l], norm_mix_gain[l], w_in[l], lower_bounds[l], hg_norm_gain[l],
             mla_q_norm_gain[l], mla_kv_norm_gain[l], w_uq[l], w_uk[l], w_uv[l], w_out[l],
             norm_ffn_gain[l], w_up[l], conv_w[l], conv_b[l], w_down[l])
        h_p, a, b, cst, d = trunk_layer(h_p, c_prompt, pos_p, w, None, None, None, None)
        lat_p.append(a); kpe_p.append(b); hg_p.append(cst); cv_p.append(d)
        h_s, a, b, cst, d = trunk_layer(h_s, c_sample, pos_s, w, cache_kv_latent[l], cache_k_rope[l],
                                        state_hgrn[l], state_ffn_conv[l])
        lat_s.append(a); kpe_s.append(b); hg_s.append(cst); cv_s.append(d)
    y_prompt = rmsnorm(h_p, final_norm_gain)
    y_sample = rmsnorm(h_s, final_norm_gain)
    return (y_prompt, y_sample,
            jnp.stack(lat_p), jnp.stack(kpe_p), jnp.stack(hg_p), jnp.stack(cv_p),
            jnp.stack(lat_s), jnp.stack(kpe_s), jnp.stack(hg_s), jnp.stack(cv_s))
```

```python
import bisect
import numpy as np
from contextlib import ExitStack
import concourse.bass as bass
import concourse.mybir as mybir
from concourse.bass_utils import run_bass_kernel_spmd

F32 = mybir.dt.float32
BF16 = mybir.dt.bfloat16
AF = mybir.ActivationFunctionType
ALU = mybir.AluOpType

D = 1024
KC = 8
S_P = 4096
S_S = 16
PAST = 4096
NH = 4
DFF = 2816
NJ = 22
EPS = 1e-6
MLA_SCALE = float((128 + 64) ** -0.5)
NST = 2
TP = NST * 128
SAME_ENGINE_SYNC = True


class Buf:
    def __init__(self, t, name):
        self.t = t
        self.name = name
        self.w = None
        self.r = {}
        self.lsem = None
        self.lcnt = 0
        self.ssem = None
        self.scnt = 0

    def __getitem__(self, k):
        return self.t[k]


class KB:
    def __init__(self, nc, es, needed=None):
        self.nc = nc
        self.es = es
        self.needed = needed
        self.used = set()
        self.eng = {'pe': nc.tensor, 'act': nc.scalar, 'dve': nc.vector, 'pool': nc.gpsimd, 'sp': nc.sync}
        self.sem = {}
        self.cnt = {}
        for e in ['pe', 'act', 'dve', 'pool']:
            self.sem[e] = es.enter_context(nc.semaphore('s_' + e))
            self.cnt[e] = 0
        self.seen = {e: {} for e in self.eng}
        self.needed_set = {e: set(v) for e, v in needed.items()} if needed is not None else None
        self.nsem = 4
        self.allbufs = []
        self.ninst = 0

    def newsem(self, name):
        self.nsem += 1
        return self.es.enter_context(self.nc.semaphore(name))

    def sb(self, name, shape, dt=F32):
        t = self.es.enter_context(self.nc.sbuf_tensor(name, list(shape), dt))
        b = Buf(t, name)
        self.allbufs.append(b)
        return b

    def ps(self, name, shape, dt=F32):
        t = self.es.enter_context(self.nc.psum_tensor(name, list(shape), dt))
        b = Buf(t, name)
        self.allbufs.append(b)
        return b

    def virt(self, name):
        b = Buf(None, name)
        self.allbufs.append(b)
        return b

    def _waits(self, eng, reads, writes):
        need = {}

        def add(ev):
            if ev is None:
                return
            sn, sem, val, src = ev
            if src == eng and (eng == 'pe' or not SAME_ENGINE_SYNC):
                return
            if self.seen[eng].get(sn, 0) >= val:
                return
            if sn not in need or need[sn][1] < val:
                need[sn] = (sem, val)

        for b in reads:
            add(b.w)
        for b in writes:
            add(b.w)
            for ev in b.r.values():
                add(ev)
        E = self.eng[eng]
        for sn, (sem, val) in need.items():
            v = val
            if sn.startswith('s_'):
                src = sn[2:]
                self.used.add((src, val))
                if self.needed is not None:
                    v = bisect.bisect_right(self.needed[src], val)
            E.wait_ge(sem, v)
            self.seen[eng][sn] = val
            self.ninst += 1

    def _record(self, ev, reads, writes):
        for b in writes:
            b.w = ev
            b.r = {}
        for b in reads:
            if b not in writes:
                b.r[ev[0]] = ev

    def op(self, eng, fn, reads=(), writes=()):
        self._waits(eng, reads, writes)
        ins = fn(self.eng[eng])
        self.cnt[eng] += 1
        if self.needed is None or self.cnt[eng] in self.needed_set[eng]:
            ins.then_inc(self.sem[eng], 1)
        self.ninst += 1
        ev = ('s_' + eng, self.sem[eng], self.cnt[eng], eng)
        self._record(ev, reads, writes)

    def dma(self, q, out, in_, reads=(), writes=(), semof=None, store=False, nowait_w=False, **kw):
        if nowait_w:
            self._waits(q, reads, ())
        else:
            self._waits(q, reads, writes)
        ins = self.eng[q].dma_start(out=out, in_=in_, **kw)
        self.ninst += 1
        if store:
            if semof.ssem is None:
                semof.ssem = self.newsem('ss_' + semof.name)
            semof.scnt += 16
            ins.then_inc(semof.ssem, 16)
            ev = ('ss_' + semof.name, semof.ssem, semof.scnt, 'dma')
        else:
            if semof.lsem is None:
                semof.lsem = self.newsem('ls_' + semof.name)
            semof.lcnt += 16
            ins.then_inc(semof.lsem, 16)
            ev = ('ls_' + semof.name, semof.lsem, semof.lcnt, 'dma')
        self._record(ev, reads, writes)
        return ev

    def finish(self):
        for b in self.allbufs:
            if b.ssem is not None and b.scnt > 0:
                self.nc.sync.wait_ge(b.ssem, b.scnt)
            if b.lsem is not None and b.lcnt > 0:
                self.nc.sync.wait_ge(b.lsem, b.lcnt)


def build_nc(needed=None):
    nc = bass.Bass("TRN2", target_bir_lowering=False)

    def din(name, shape, dt=F32):
        return nc.dram_tensor(name, list(shape), dt, kind="ExternalInput").ap()

    def dout(name, shape, dt=F32):
        return nc.dram_tensor(name, list(shape), dt, kind="ExternalOutput").ap()

    def dscr(name, shape, dt):
        return nc.dram_tensor(name, list(shape), dt, kind="Internal").ap()

    xp = din("xp", [S_P, D])
    xs_in = din("xs", [S_S, D])
    c2 = din("c2", [128, KC, 2])
    clat = din("clat", [PAST, 256])
    ckpe = din("ckpe", [PAST, 64])
    shg = din("shg", [NH, 128, 128])
    sconv = din("sconv", [128, NJ, 2])
    w_ada = din("w_ada", [D, 6 * D])
    b_ada2 = din("b_ada2", [2, 6 * D])
    cols = din("cols", [128, 128])
    bcs = din("bcs", [4, 1024])
    cmat = din("cmat", [128, 3, 128])
    ropeT = din("ropeT", [S_P + S_S, 128])
    ropeF = din("ropeF", [64, 2, S_P + S_S])
    w_in_l = din("w_in_l", [128, KC, 2752])
    w_uq_l = din("w_uq_l", [128, 3, 1024])
    w_ukv_l = din("w_ukv_l", [128, 2, 1024])
    w_out_l = din("w_out_l", [128, 2, KC, 512])
    w_ffn_l = din("w_ffn_l", [NJ, 128, 3072])

    y_p = dout("y_p", [S_P, D])
    y_s = dout("y_s", [S_S, D])
    lat_p = dout("lat_p", [S_P, 256])
    kpe_p = dout("kpe_p", [S_P, 64])
    hg_p = dout("hg_p", [NH, 128, 128])
    cv_p = dout("cv_p", [128, NJ, 2])
    lat_s = dout("lat_s", [S_S, 256])
    kpe_s = dout("kpe_s", [S_S, 64])
    hg_s = dout("hg_s", [NH, 128, 128])
    cv_s = dout("cv_s", [128, NJ, 2])

    modscr = dscr("modscr", [2, 6 * D], F32)
    win_b = dscr("win_b", [128, KC * 2752], BF16)
    wuq_b = dscr("wuq_b", [128, 3 * 1024], BF16)
    wukv_b = dscr("wukv_b", [128, 2 * 1024], BF16)
    wout_b = dscr("wout_b", [128, 2 * KC * 512], BF16)
    wffn_b = dscr("wffn_b", [NJ * 128, 3072], BF16)

    es = ExitStack()
    with es:
        nc_ctx = es.enter_context(nc.allow_non_contiguous_dma(reason="small layout loads"))
        es.enter_context(nc.allow_low_precision(reason="bf16 matmul operands"))
        kb = KB(nc, es, needed)
        op = kb.op

        cm = kb.sb("cm", [128, 3, 128])
        colsb = kb.sb("colsb", [128, 128])
        fg_bc = kb.sb("fg_bc", [128, 1024])
        kvn_bc = kb.sb("kvn_bc", [128, 256])
        lb_bc = kb.sb("lb_bc", [128, 512])
        maskU4 = kb.sb("maskU4", [128, 4, 128], BF16)
        ones_bf = kb.sb("ones_bf", [128, 128], BF16)
        mean_bf = kb.sb("mean_bf", [128, 128], BF16)
        mhalf = kb.sb("mhalf", [128, 1])
        c2s = kb.sb("c2s", [128, KC, 2])
        c2e = kb.sb("c2e", [128, KC, 2])
        modc = [kb.sb("modc%d" % r, [128, 4, KC]) for r in range(2)]
        G1c = [kb.sb("G1c%d" % r, [128, KC]) for r in range(2)]
        G2c = [kb.sb("G2c%d" % r, [128, KC]) for r in range(2)]
        g1_bc = kb.sb("g1_bc", [128, 1024])
        g2_bc = kb.sb("g2_bc", [128, 1024])
        ident = cm.t[:, 0, :]
        Umat = cm.t[:, 1, :]
        Uxmat = cm.t[:, 2, :]
        GMIX, GFFN, QN, HGG, CW, CB = 0, 8, 16, 19, 20, 86
        KT = kb.sb("KT", [128, NH, PAST], BF16)
        Vr = kb.sb("Vr", [128, PAST // 128, 512], BF16)
        kpeT = kb.sb("kpeT", [128, PAST], BF16)
        KTs = kb.sb("KTs", [128, NH, S_S], BF16)
        Vs = kb.sb("Vs", [S_S, 512], BF16)
        kpeTs = kb.sb("kpeTs", [128, S_S], BF16)
        Sst = kb.sb("Sst", [128, NH, 128])
        Sbf = kb.sb("Sbf", [128, NH, 128], BF16)
        aprev = kb.sb("aprev", [128, NJ, 2])
        NPOOL = 4
        wpool = [kb.sb("wp%d" % i, [128, 4096], BF16) for i in range(NPOOL)]
        wpi = [0]

        def wnext():
            b = wpool[wpi[0] % NPOOL]
            wpi[0] += 1
            return b

        xbuf = [kb.sb("xb%d" % i, [128, NST, D]) for i in range(2)]
        xparts = []
        for xb_ in xbuf:
            ps_list = []
            for st_ in range(NST):
                pb_ = Buf(xb_.t, xb_.name + "s%d" % st_)
                kb.allbufs.append(pb_)
                ps_list.append(pb_)
            xparts.append(ps_list)
        xsb = kb.sb("xsb", [128, D])
        hTs = [kb.sb("hT%d" % i, [128, KC, TP], BF16) for i in range(2)]
        mixT = kb.sb("mixT", [128, KC, TP], BF16)
        stat = kb.sb("stat", [128, 8])
        tA = [kb.sb("tA%d" % i, [128, 512]) for i in range(2)]
        tB = [kb.sb("tB%d" % i, [128, 512]) for i in range(2)]
        tC = [kb.sb("tC%d" % i, [128, 512]) for i in range(2)]
        tD = kb.sb("tD", [128, 512])
        khat = kb.sb("khat", [128, NST, 512], BF16)
        vv = kb.sb("vv", [128, NST, 512], BF16)
        eb = kb.sb("eb", [128, NH, TP])
        ktT = kb.sb("ktT", [128, NH, TP], BF16)
        qtT = kb.sb("qtT", [128, NH, TP], BF16)
        scm = kb.sb("scm", [128, NH, 128], BF16)
        sqb = kb.sb("sqb", [128, 2, TP], BF16)
        sg = tA[0]
        rs = tA[1]
        cqnT = kb.sb("cqnT", [128, 3, TP], BF16)
        latT = kb.sb("latT", [128, 2, TP], BF16)
        qnT = kb.sb("qnT", [128, NH, TP], BF16)
        qpeT = kb.sb("qpeT", [128, NH, TP], BF16)
        ropeTs = kb.sb("ropeTs", [128, NST, 128])
        ropeFs = kb.sb("ropeFs", [64, 2, TP])
        lato1 = kb.sb("lato0", [128, NST, 256])
        kpeo1 = kb.sb("kpeo0", [128, NST, 64])
        lato = [lato1, lato1]
        kpeo = [kpeo1, kpeo1]
        pT = [kb.sb("pT%d" % i, [128, TP], BF16) for i in range(4)]
        aS = [kb.sb("aS%d" % i, [128, TP + 2]) for i in range(2)]
        cc = [kb.sb("cc%d" % i, [128, TP]) for i in range(2)]
        uT = [kb.sb("uT%d" % i, [128, TP], BF16) for i in range(3)]
        PSB = [kb.ps("psb%d" % i, [128, 512]) for i in range(8)]
        rot = [0]

        def pnext():
            b = PSB[rot[0] % 4]
            rot[0] += 1
            return b

        LB = PSB[4:8]
        wcvA = kb.virt("wcvA")
        wcvC = kb.virt("wcvC")
        wcv0 = kb.virt("wcv0")
        wcvB = kb.virt("wcvB")
        cst = kb.virt("cst")
        modv = kb.virt("modv")

        const_bufs = []

        def cload(buf, out_ap, in_ap):
            kb.dma('sp', out_ap, in_ap, reads=(), writes=(), semof=cst, nowait_w=True)
            const_bufs.append(buf)

        cload(cm, cm.t[:], cmat)
        cload(colsb, colsb.t[:], cols)
        cload(fg_bc, fg_bc.t[:], bcs[0, :].partition_broadcast(128))
        cload(kvn_bc, kvn_bc.t[:], bcs[1, 0:256].partition_broadcast(128))
        cload(tD, tD.t[:], bcs[2, 0:512].partition_broadcast(128))
        cload(lb_bc, lb_bc.t[:], bcs[3, 0:512].partition_broadcast(128))
        cload(c2s, c2s.t[:], c2)
        cev = ('ls_cst', cst.lsem, cst.lcnt, 'dma')
        for b in const_bufs:
            b.w = cev

        def wcast(dst2, src2, virt):
            R = dst2.shape[0]
            for r0 in range(0, R, 512):
                r1 = min(R, r0 + 512)
                kb.dma('pool', dst2[r0:r1, :], src2[r0:r1, :], reads=(), writes=(), semof=virt, nowait_w=True)

        wcast(wukv_b.rearrange("p (a b) -> (p a) b", b=1024), w_ukv_l.rearrange("p k c -> (p k) c"), wcv0)
        wcv0.w = ('ls_wcv0', wcv0.lsem, wcv0.lcnt, 'dma')
        wcast(win_b.rearrange("p (a b) -> (p a) b", b=1376), w_in_l.rearrange("p k c -> p (k c)").rearrange("p (a b) -> (p a) b", b=1376), wcvA)
        wcvA.w = ('ls_wcvA', wcvA.lsem, wcvA.lcnt, 'dma')
        op('pool', lambda e: e.memset(ones_bf.t[:], 1.0), (), (ones_bf,))
        op('pool', lambda e: e.memset(mean_bf.t[:], 1.0 / 128.0), (), (mean_bf,))
        op('pool', lambda e: e.memset(mhalf.t[:], -0.5), (), (mhalf,))
        op('pool', lambda e: e.memset(kpeT.t[64:128, :], 0.0), (), (kpeT,))
        op('pool', lambda e: e.memset(kpeTs.t[64:128, :], 0.0), (), (kpeTs,))
        op('pool', lambda e: e.memset(qpeT.t[64:128, :, :], 0.0), (), (qpeT,))
        for h in range(NH):
            op('pool', lambda e, h=h: e.tensor_copy(maskU4.t[:, h, :], Umat), (cm,), (maskU4,))
        op('dve', lambda e: e.tensor_tensor(lb_bc.t[:], lb_bc.t[:], tD.t[:], ALU.subtract), (lb_bc, tD), (lb_bc,))
        op('act', lambda e: e.activation(lb_bc.t[:], lb_bc.t[:], AF.Exp), (lb_bc,), (lb_bc,))
        op('dve', lambda e: e.tensor_scalar(lb_bc.t[:], lb_bc.t[:], 1.0, None, ALU.add), (lb_bc,), (lb_bc,))
        op('dve', lambda e: e.reciprocal(lb_bc.t[:], lb_bc.t[:]), (lb_bc,), (lb_bc,))
        op('act', lambda e: e.activation(c2e.t[:], c2s.t[:], AF.Exp, scale=-1.0), (c2s,), (c2e,))
        op('dve', lambda e: e.tensor_scalar(c2e.t[:], c2e.t[:], 1.0, None, ALU.add), (c2e,), (c2e,))
        op('dve', lambda e: e.reciprocal(c2e.t[:], c2e.t[:]), (c2e,), (c2e,))
        op('dve', lambda e: e.tensor_tensor(c2e.t[:], c2e.t[:], c2s.t[:], ALU.mult), (c2e, c2s), (c2e,))
        mod_steps = []

        def mk_mod_step(cb, k):
            def step():
                if k == 0:
                    kb.dma('sp', tA[0].t[0:2, :], b_ada2[:, cb * 1024:cb * 1024 + 512], (), (tA[0],), semof=tA[0])
                    kb.dma('sp', tA[1].t[0:2, :], b_ada2[:, cb * 1024 + 512:(cb + 1) * 1024], (), (tA[1],), semof=tA[1])
                wb = wnext()
                wv = wb.t[:, 0:2048].bitcast(F32)
                kb.dma('sp', wv, w_ada[k * 128:(k + 1) * 128, cb * 1024:(cb + 1) * 1024], (), (wb,), semof=wb)
                for i in range(2):
                    op('pe', lambda e, i=i: e.matmul(LB[i].t[0:2, :], c2e.t[:, k, :], wv[:, i * 512:(i + 1) * 512],
                                                     start=(k == 0), stop=(k == KC - 1)),
                       (c2e, wb), (LB[i],))
                if k == KC - 1:
                    for i in range(2):
                        op('dve', lambda e, i=i: e.tensor_tensor(xsb.t[0:2, i * 512:(i + 1) * 512], LB[i].t[0:2, :], tA[i].t[0:2, :], ALU.add),
                           (LB[i], tA[i]), (xsb,))
                    kb.dma('sp', modscr[:, cb * 1024:(cb + 1) * 1024], xsb.t[0:2, :], (xsb,), (modv,), semof=modv, nowait_w=True)
            return step

        for cb in range(6):
            for k in range(KC):
                mod_steps.append(mk_mod_step(cb, k))

        def mod_finish():
            mt = tA[0]
            kb.dma('sp', mt.t[0:96, 0:128], modscr.rearrange("r (m k p) -> (r m k) p", m=6, k=KC), (modv,), (mt,), semof=mt)
            pb = LB[0]
            op('pe', lambda e: e.transpose(pb.t[:, 0:96], mt.t[0:96, 0:128], ident[0:96, 0:96]), (mt, cm), (pb,))
            for r in range(2):
                pv4 = pb.t[:, r * 48:(r + 1) * 48].rearrange("p (m k) -> p m k", m=6)
                op('act', lambda e, r=r, pv4=pv4: e.activation(modc[r].t[:, 0:2, :], pv4[:, 0:2, :], AF.Copy), (pb,), (modc[r],))
                op('act', lambda e, r=r, pv4=pv4: e.activation(modc[r].t[:, 2:4, :], pv4[:, 3:5, :], AF.Copy), (pb,), (modc[r],))
                op('dve', lambda e, r=r: e.scalar_tensor_tensor(G1c[r].t[:], modc[r].t[:, 1, :], 1.0, colsb.t[:, GMIX:GMIX + 8], ALU.add, ALU.mult),
                   (modc[r], colsb), (G1c[r],))
                op('dve', lambda e, r=r: e.scalar_tensor_tensor(G2c[r].t[:], modc[r].t[:, 3, :], 1.0, colsb.t[:, GFFN:GFFN + 8], ALU.add, ALU.mult),
                   (modc[r], colsb), (G2c[r],))

        def load_gbc(r, P):
            kb.dma('sp', g1_bc.t[0:P, :], modscr[r, 2048:3072].partition_broadcast(P), (modv,), (g1_bc,), semof=g1_bc)
            kb.dma('sp', g2_bc.t[0:P, :], modscr[r, 5120:6144].partition_broadcast(P), (modv,), (g2_bc,), semof=g2_bc)

        def wload(src_ap, ncols, virt):
            wb = wnext()
            kb.dma('sp', wb.t[:, 0:ncols], src_ap, (virt,), (wb,), semof=wb)
            return wb

        win3 = win_b.rearrange("p (k c) -> p k c", k=KC)

        def load_win(c0, c1):
            wb = wnext()
            n = c1 - c0
            kb.dma('sp', wb.t[:, 0:KC * n].rearrange("p (k c) -> p k c", k=KC), win3[:, :, c0:c1], (wcvA,), (wb,), semof=wb)
            return wb, wb.t[:, 0:KC * n].rearrange("p (k c) -> p k c", k=KC)

        def rstd_from(col, P, N):
            op('pool', lambda e: e.tensor_scalar(stat.t[0:P, col:col + 1], stat.t[0:P, col:col + 1], 1.0 / N, EPS, ALU.mult, ALU.add),
               (stat,), (stat,))
            op('pool', lambda e: e.tensor_tensor(stat.t[0:P, col:col + 1], stat.t[0:P, col:col + 1], mhalf.t[0:P, 0:1], ALU.pow),
               (stat, mhalf), (stat,))

        def norm_T(xb, st, P, Gc, shc_buf, shc_idx, dst, T0, bank=None):
            norm_T_a(xb, st, P)
            norm_T_b(st, P, Gc, shc_buf, shc_idx, dst, T0, bank)

        def xs_view(xs_):
            if xs_ is xsb:
                return xsb.t[:, :]
            return xs_.t[:, :, :].rearrange("p a b -> p (a b)").bitcast(F32)

        def norm_T_a(xb, st, P, use_pool=False, scol=0, xs_=None):
            xs_ = xsb if xs_ is None else xs_
            xsv = xs_view(xs_)
            xv = xb.t[0:P, st, :]
            op('act', lambda e: e.activation(xsv[0:P, :], xv, AF.Square, accum_out=stat.t[0:P, scol:scol + 1]), (xb,), (xs_, stat))
            rstd_from(scol, P, D)
            op('act', lambda e: e.activation(xsv[0:P, :], xv, AF.Copy, scale=stat.t[0:P, scol:scol + 1]), (xb, stat), (xs_,))

        def norm_T_b(st, P, Gc, shc_buf, shc_idx, dst, T0, bank=None, halves=(0, 1), part=None, xs_=None):
            xs_ = xsb if xs_ is None else xs_
            xsv = xs_view(xs_)
            for half in halves:
                pb = pnext() if bank is None else bank
                if part in (None, 'pe'):
                    for kk in range(4):
                        k = half * 4 + kk
                        op('pe', lambda e, k=k, kk=kk, pb=pb: e.transpose(pb.t[:, kk * P:(kk + 1) * P], xsv[0:P, k * 128:(k + 1) * 128], ident[0:P, 0:P]),
                           (xs_, cm), (pb,))
                if part == 'pe':
                    continue
                for kk in range(4):
                    k = half * 4 + kk
                    eng = 'act' if kk % 2 == 0 else 'dve'
                    if eng == 'act':
                        op('act', lambda e, k=k, kk=kk, pb=pb: e.activation(dst.t[:, k, T0:T0 + P], pb.t[:, kk * P:(kk + 1) * P], AF.Identity,
                                                                         scale=Gc.t[:, k:k + 1], bias=shc_buf.t[:, shc_idx, k:k + 1]),
                           (pb, Gc, shc_buf), (dst,))
                    else:
                        op('dve', lambda e, k=k, kk=kk, pb=pb: e.tensor_scalar(dst.t[:, k, T0:T0 + P], pb.t[:, kk * P:(kk + 1) * P],
                                                                            Gc.t[:, k:k + 1], shc_buf.t[:, shc_idx, k:k + 1], ALU.mult, ALU.add),
                           (pb, Gc, shc_buf), (dst,))

        def kv_project(T, wkv3, wkvb, KT_dst, V_dst_fn, nst, P):
            for hp in range(2):
                pb = pnext()
                for hh in range(2):
                    h = hp * 2 + hh
                    for kc in range(2):
                        op('pe', lambda e, h=h, hh=hh, kc=kc, pb=pb: e.matmul(pb.t[:, hh * T:(hh + 1) * T], wkv3[:, kc, h * 128:(h + 1) * 128], latT.t[:, kc, 0:T],
                                                                          start=(kc == 0), stop=(kc == 1)),
                           (wkvb, latT), (pb,))
                for hh in range(2):
                    h = hp * 2 + hh
                    dst_ap, dst_buf = KT_dst(h)
                    op('act', lambda e, hh=hh, pb=pb, dst_ap=dst_ap: e.activation(dst_ap, pb.t[:, hh * T:(hh + 1) * T], AF.Copy), (pb,), (dst_buf,))
            for st in range(nst):
                pb = pnext()
                for kc in range(2):
                    op('pe', lambda e, kc=kc, pb=pb, st=st: e.matmul(pb.t[0:P, :], latT.t[:, kc, st * P:(st + 1) * P], wkv3[:, kc, 512:1024],
                                                                 start=(kc == 0), stop=(kc == 1)),
                       (wkvb, latT), (pb,))
                dst_ap, dst_buf = V_dst_fn(st)
                op('dve', lambda e, pb=pb, dst_ap=dst_ap: e.tensor_copy(dst_ap, pb.t[0:P, :]), (pb,), (dst_buf,))

        def attend(NQ, keyblocks):
            nb = len(keyblocks)
            items = [(h, bi) for h in range(NH) for bi in range(nb)]
            sTs = {}
            ps_ = {}

            def scores(i):
                h, bi = items[i]
                KT_fn, kpe_ap, kpe_buf, V_fn, nk, c0, maskq = keyblocks[bi]
                sT = pnext()
                kt_ap, kt_buf = KT_fn(h)
                n = NQ - c0
                op('pe', lambda e: e.matmul(sT.t[0:nk, 0:n], kt_ap, qnT.t[:, h, c0:NQ], start=True, stop=False), (kt_buf, qnT), (sT,))
                op('pe', lambda e: e.matmul(sT.t[0:nk, 0:n], kpe_ap, qpeT.t[:, h, c0:NQ], start=False, stop=True), (kpe_buf, qpeT), (sT,))
                sTs[i] = sT

            def expo(i):
                h, bi = items[i]
                KT_fn, kpe_ap, kpe_buf, V_fn, nk, c0, maskq = keyblocks[bi]
                n = NQ - c0
                sT = sTs.pop(i)
                p = pT[i % len(pT)]
                op('act', lambda e: e.activation(p.t[0:nk, 0:n], sT.t[0:nk, 0:n], AF.Exp, scale=MLA_SCALE), (sT,), (p,))
                if maskq:
                    op('pool', lambda e: e.memset(p.t[64:128, 0:64], 0.0), (), (p,))
                ps_[i] = p

            def pv(i):
                h, bi = items[i]
                KT_fn, kpe_ap, kpe_buf, V_fn, nk, c0, maskq = keyblocks[bi]
                n = NQ - c0
                p = ps_.pop(i)
                oT = LB[0 + 2 * (h % 2)]
                sm = LB[1 + 2 * (h % 2)]
                v_ap, v_buf = V_fn(h)
                op('pe', lambda e: e.matmul(oT.t[:, c0:NQ], v_ap, p.t[0:nk, 0:n], start=(bi == 0), stop=(bi == nb - 1)), (v_buf, p), (oT,))
                op('pe', lambda e: e.matmul(sm.t[:, c0:NQ], ones_bf.t[0:nk, :], p.t[0:nk, 0:n], start=(bi == 0), stop=(bi == nb - 1)), (ones_bf, p), (sm,))
                if bi == nb - 1:
                    rc = tD
                    op('dve', lambda e: e.reciprocal(rc.t[:, 0:NQ], sm.t[:, 0:NQ]), (sm,), (rc,))
                    op('dve', lambda e: e.tensor_tensor(mixT.t[:, 4 + h, 0:NQ], oT.t[:, 0:NQ], rc.t[:, 0:NQ], ALU.mult), (oT, rc), (mixT,))

            NI = len(items)
            AHEAD = 3
            for i in range(min(AHEAD, NI)):
                scores(i)
            for i in range(NI):
                expo(i)
                if i + AHEAD < NI:
                    scores(i + AHEAD)
                pv(i)

        prefetched = set()

        def prefetch(seq, ti, g, bank=None, step=None):
            P = seq['P']
            nst = seq['nst']
            T = P * nst
            r = seq['r']
            xb = xbuf[g % 2]
            hT = hTs[g % 2]
            tok0 = ti * T
            pos0 = seq['pos0'] + tok0
            if step is None or step == 'dma':
                kb.dma('sp', xb.t[0:P, 0:nst, :], seq['x'][tok0:tok0 + T, :].rearrange("(n p) d -> p n d", p=P), (), tuple(xparts[g % 2]), semof=xb)
                kb.dma('sp', ropeTs.t[0:P, 0:nst, :], ropeT[pos0:pos0 + T, :].rearrange("(n p) c -> p n c", p=P), (), (ropeTs,), semof=ropeTs)
                kb.dma('sp', ropeFs.t[:, :, 0:T], ropeF[:, :, pos0:pos0 + T], (), (ropeFs,), semof=ropeFs)
            for st in range(nst):
                if step is None or step == ('a', st):
                    norm_T_a(xparts[g % 2][st], st, P, use_pool=False)
                if step is None or step == ('b', st):
                    norm_T_b(st, P, G1c[r], modc[r], 0, hT, st * P, bank)
                for hf_ in range(2):
                    for part in ('pe', 'ev'):
                        if step == ('bh', st, hf_, part):
                            norm_T_b(st, P, G1c[r], modc[r], 0, hT, st * P, bank, halves=(hf_,), part=part)
            if step is None or step == ('b', nst - 1) or step == ('bh', nst - 1, 1, 'ev'):
                prefetched.add(g)

        preloaded_w = {}
        pending_tail = []

        def run_tile(seq, ti, g, nxt=None):
            P = seq['P']
            nst = seq['nst']
            T = P * nst
            C = seq['C']
            nch = P // C
            r = seq['r']
            xb = xbuf[g % 2]
            xps = xparts[g % 2]
            hT = hTs[g % 2]
            tok0 = ti * T
            pos0 = seq['pos0'] + tok0
            last = (ti == seq['ntiles'] - 1)
            if g not in prefetched:
                prefetch(seq, ti, g)
            if g in preloaded_w:
                (wb_f, wf3), (wb_i, wi3), (wb_c, wc3), (wb_c2, wc3b) = preloaded_w.pop(g)
            else:
                wb_f, wf3 = load_win(512, 1024)
                wb_i, wi3 = load_win(1024, 1536)
                wb_c, wc3 = load_win(2048, 2432)
                wb_c2, wc3b = load_win(2432, 2752)
            lo = lato[0]
            ko = kpeo[0]

            def rotator(banks):
                cnt_ = [0]

                def nx():
                    b = banks[cnt_[0] % len(banks)]
                    cnt_[0] += 1
                    return b
                return nx

            def run_chains(chains):
                n = max(len(c) for c in chains)
                for s_ in range(n):
                    for c in chains:
                        if s_ < len(c):
                            c[s_]()

            def H_chain(st, banks):
                a, b_, c_ = tA[st], tB[st], tC[st]
                pn = rotator(banks)
                S = {}

                def s0():
                    pf = pn()
                    for k in range(KC):
                        op('pe', lambda e, k=k: e.matmul(pf.t[0:P, :], hT.t[:, k, st * P:(st + 1) * P], wf3[:, k, :], start=(k == 0), stop=(k == KC - 1)),
                           (hT, wb_f), (pf,))
                    op('act', lambda e: e.activation(a.t[0:P, :], pf.t[0:P, :], AF.Exp, scale=-1.0), (pf,), (a,))
                    pi = S['pi'] = pn()
                    for k in range(KC):
                        op('pe', lambda e, k=k: e.matmul(pi.t[0:P, :], hT.t[:, k, st * P:(st + 1) * P], wi3[:, k, :], start=(k == 0), stop=(k == KC - 1)),
                           (hT, wb_i), (pi,))

                def s1():
                    pi = S['pi']
                    op('pool', lambda e: e.tensor_tensor(b_.t[0:P, :], a.t[0:P, :], lb_bc.t[0:P, :], ALU.mult), (a, lb_bc), (b_,))
                    op('act', lambda e: e.activation(vv.t[0:P, st, :], pi.t[0:P, :], AF.Copy), (pi,), (vv,))

                def s2():
                    op('act', lambda e: e.activation(b_.t[0:P, :], b_.t[0:P, :], AF.Ln, bias=1.0), (b_,), (b_,))
                    op('act', lambda e: e.activation(a.t[0:P, :], a.t[0:P, :], AF.Ln, bias=1.0), (a,), (a,))

                def s3():
                    op('dve', lambda e: e.tensor_tensor(c_.t[0:P, :], b_.t[0:P, :], a.t[0:P, :], ALU.subtract), (a, b_), (c_,))

                def s4():
                    op('act', lambda e: e.activation(a.t[0:P, :], c_.t[0:P, :], AF.Exp), (c_,), (a,))
                    pbb = S['pbb'] = pn()
                    op('pe', lambda e: e.matmul(pbb.t[0:P, :], Umat[0:P, 0:P], c_.t[0:P, :], start=True, stop=True), (cm, c_), (pbb,))
                    pdd = S['pdd'] = pn()
                    op('pe', lambda e: e.matmul(pdd.t[0:P, :], Uxmat[0:P, 0:P], c_.t[0:P, :], start=True, stop=True), (cm, c_), (pdd,))

                def s5():
                    pbb, pdd = S['pbb'], S['pdd']
                    op('pool', lambda e: e.tensor_scalar(a.t[0:P, :], a.t[0:P, :], -1.0, 1.0, ALU.mult, ALU.add), (a,), (a,))
                    op('act', lambda e: e.activation(b_.t[0:P, :], pbb.t[0:P, :], AF.Exp, scale=-1.0), (pbb,), (b_,))
                    op('act', lambda e: e.activation(pdd.t[0:P, :], pdd.t[0:P, :], AF.Exp), (pdd,), (pdd,))

                def s6():
                    pdd = S['pdd']
                    op('dve', lambda e: e.tensor_tensor(b_.t[0:P, :], b_.t[0:P, :], a.t[0:P, :], ALU.mult), (a, b_), (b_,))
                    op('dve', lambda e: e.tensor_tensor(khat.t[0:P, st, :], pdd.t[0:P, :], a.t[0:P, :], ALU.mult), (a, pdd), (khat,))
                    pbt = S['pbt'] = pn()
                    for h in range(NH):
                        op('pe', lambda e, h=h: e.matmul(pbt.t[:, h * P:(h + 1) * P], c_.t[0:P, h * 128:(h + 1) * 128], Umat[0:P, 0:P], start=True, stop=True),
                           (c_, cm), (pbt,))

                def s7():
                    pbt = S['pbt']
                    op('act', lambda e: e.activation(eb.t[:, :, st * P:(st + 1) * P], pbt.t[:, 0:NH * P].rearrange("p (h t) -> p h t", h=NH), AF.Exp),
                       (pbt,), (eb,))
                    pkt = S['pkt'] = pn()
                    for h in range(NH):
                        op('pe', lambda e, h=h: e.transpose(pkt.t[:, h * P:(h + 1) * P], b_.t[0:P, h * 128:(h + 1) * 128], ident[0:P, 0:P]),
                           (b_, cm), (pkt,))

                def s8():
                    pkt = S['pkt']
                    op('dve', lambda e: e.tensor_copy(ktT.t[:, :, st * P:(st + 1) * P], pkt.t[:, 0:NH * P].rearrange("p (h t) -> p h t", h=NH)),
                       (pkt,), (ktT,))

                return [s0, s1, s2, s3, s4, s5, s6, s7, s8]

            def M_chain(st, banks, xs_buf, c1, c2):
                pn = rotator(banks)
                S = {}
                tcs = tC[st]
                tc_, ts_ = cc[0], cc[1]

                def m0():
                    pc1 = S['pc1'] = pn()
                    for k in range(KC):
                        op('pe', lambda e, k=k: e.matmul(pc1.t[0:P, 0:384], hT.t[:, k, st * P:(st + 1) * P], wc3[:, k, 0:384], start=(k == 0), stop=(k == KC - 1)),
                           (hT, wb_c), (pc1,))
                    op('act', lambda e: e.activation(sqb.t[0:P, :, :].rearrange("p h t -> p (h t)")[:, 0:384], pc1.t[0:P, 0:384], AF.Square, accum_out=stat.t[0:P, c1:c1 + 1]),
                       (pc1,), (sqb, stat))
                    pc2 = S['pc2'] = pn()
                    for k in range(KC):
                        op('pe', lambda e, k=k: e.matmul(pc2.t[0:P, 0:320], hT.t[:, k, st * P:(st + 1) * P], wc3b[:, k, 0:320], start=(k == 0), stop=(k == KC - 1)),
                           (hT, wb_c2), (pc2,))

                def m1():
                    rstd_from(c1, P, 384)

                def m2():
                    pc1 = S['pc1']
                    op('act', lambda e: e.activation(xs_buf.t[0:P, 0:384], pc1.t[0:P, 0:384], AF.Copy, scale=stat.t[0:P, c1:c1 + 1]), (pc1, stat), (xs_buf,))
                    pc2 = S['pc2']
                    op('act', lambda e: e.activation(sqb.t[0:P, :, :].rearrange("p h t -> p (h t)")[:, 0:256], pc2.t[0:P, 0:256], AF.Square, accum_out=stat.t[0:P, c2:c2 + 1]),
                       (pc2,), (sqb, stat))

                def m3():
                    rstd_from(c2, P, 256)
                    pt = S['pt'] = pn()
                    for kc in range(3):
                        op('pe', lambda e, kc=kc: e.transpose(pt.t[:, kc * P:(kc + 1) * P], xs_buf.t[0:P, kc * 128:(kc + 1) * 128], ident[0:P, 0:P]), (xs_buf, cm), (pt,))

                def m4():
                    pt, pc2 = S['pt'], S['pc2']
                    for kc in range(3):
                        op('act', lambda e, kc=kc: e.activation(cqnT.t[:, kc, st * P:(st + 1) * P], pt.t[:, kc * P:(kc + 1) * P], AF.Copy, scale=colsb.t[:, QN + kc:QN + kc + 1]),
                           (pt, colsb), (cqnT,))
                    op('dve', lambda e: e.scalar_tensor_tensor(lo.t[0:P, st, :], pc2.t[0:P, 0:256], stat.t[0:P, c2:c2 + 1], kvn_bc.t[0:P, :], ALU.mult, ALU.mult),
                       (pc2, stat, kvn_bc), (lo,))
                    o0 = st * 128
                    op('dve', lambda e: e.tensor_tensor(tc_.t[0:P, o0:o0 + 64], pc2.t[0:P, 256:320], ropeTs.t[0:P, st, 0:64], ALU.mult), (pc2, ropeTs), (tc_,))
                    op('dve', lambda e: e.tensor_tensor(ts_.t[0:P, o0:o0 + 64], pc2.t[0:P, 256:320], ropeTs.t[0:P, st, 64:128], ALU.mult), (pc2, ropeTs), (ts_,))

                def m5():
                    o0 = st * 128
                    pt2 = S['pt2'] = pn()
                    for kc in range(2):
                        op('pe', lambda e, kc=kc: e.transpose(pt2.t[:, kc * P:(kc + 1) * P], lo.t[0:P, st, kc * 128:(kc + 1) * 128], ident[0:P, 0:P]), (lo, cm), (pt2,))
                    op('pool', lambda e: e.tensor_tensor(ko.t[0:P, st, 0:32], tc_.t[0:P, o0:o0 + 32], ts_.t[0:P, o0 + 32:o0 + 64], ALU.subtract), (tc_, ts_), (ko,))
                    op('pool', lambda e: e.tensor_tensor(ko.t[0:P, st, 32:64], tc_.t[0:P, o0 + 32:o0 + 64], ts_.t[0:P, o0:o0 + 32], ALU.add), (tc_, ts_), (ko,))

                def m6():
                    pt2 = S['pt2']
                    op('act', lambda e: e.activation(latT.t[:, :, st * P:(st + 1) * P], pt2.t[:, 0:2 * P].rearrange("p (k t) -> p k t", k=2), AF.Copy), (pt2,), (latT,))
                    pt3 = S['pt3'] = pn()
                    op('pe', lambda e: e.transpose(pt3.t[0:64, 0:P], ko.t[0:P, st, :], ident[0:P, 0:P]), (ko, cm), (pt3,))

                def m7():
                    pt3 = S['pt3']
                    kdst_ap, kdst_buf = seq['kpe_dst'](tok0 + st * P, P)
                    op('act', lambda e: e.activation(kdst_ap, pt3.t[0:64, 0:P], AF.Copy), (pt3,), (kdst_buf,))

                return [m0, m1, m2, m3, m4, m5, m6, m7]

            chains = []
            ptail = pending_tail.pop() if pending_tail else None
            if ptail is not None:
                chains.append([lambda: None, ptail])
            mchains = []
            for st in range(nst):
                chains.append(H_chain(st, [PSB[2 * st], PSB[2 * st + 1]]))
                mch = M_chain(st, [PSB[4 + 2 * st], PSB[5 + 2 * st]], xsb if st == 0 else tD, 1 + 3 * st, 2 + 3 * st)
                if ptail is not None:
                    mch = [lambda: None] + mch
                mchains.append(mch)
            run_chains(chains + mchains)
            kb.dma('sp', seq['lat_out'][tok0:tok0 + T, :].rearrange("(n p) c -> p n c", p=P), lo.t[0:P, 0:nst, :], (lo,), (), semof=lo, store=True)
            kb.dma('sp', seq['kpe_out'][tok0:tok0 + T, :].rearrange("(n p) c -> p n c", p=P), ko.t[0:P, 0:nst, :], (ko,), (), semof=ko, store=True)

            wb_q, wq3 = load_win(0, 512)
            RB = rotator([PSB[2], PSB[3]])
            for hp in range(2):
                pq = RB()
                for hh in range(2):
                    h = hp * 2 + hh
                    for k in range(KC):
                        op('pe', lambda e, k=k, h=h, hh=hh: e.matmul(pq.t[:, hh * T:(hh + 1) * T], wq3[:, k, h * 128:(h + 1) * 128], hT.t[:, k, 0:T],
                                                                  start=(k == 0), stop=(k == KC - 1)),
                           (hT, wb_q), (pq,))
                op('dve', lambda e: e.scalar_tensor_tensor(qtT.t[:, hp * 2:hp * 2 + 2, 0:T], pq.t[:, 0:2 * T].rearrange("p (h t) -> p h t", h=2),
                                                           float(128 ** -0.5), eb.t[:, hp * 2:hp * 2 + 2, 0:T], ALU.mult, ALU.mult),
                   (pq, eb), (qtT,))

            wb_g, wg3 = load_win(1536, 2048)
            wb_u = wload(wuq_b, 3072, wcvC)
            wu3 = wb_u.t[:, 0:3072].rearrange("p (k c) -> p k c", k=3)
            wb_kv = wload(wukv_b, 2048, wcvA)
            wkv3 = wb_kv.t[:, 0:2048].rearrange("p (k c) -> p k c", k=2)
            oTb = [PSB[0], PSB[1]]
            sgX = [tA[0], tA[1]]
            rsX = [tC[0], tC[1]]

            def SEQ_chain():
                stages = []
                for st in range(nst):
                    def q0(st=st):
                        psc = PSB[2]
                        for h in range(NH):
                            op('pe', lambda e, h=h: e.matmul(psc.t[0:P, h * P:(h + 1) * P], ktT.t[:, h, st * P:(st + 1) * P], qtT.t[:, h, st * P:(st + 1) * P],
                                                             start=True, stop=True), (ktT, qtT), (psc,))
                        op('dve', lambda e: e.tensor_tensor(scm.t[0:P, :, 0:P], psc.t[0:P, 0:NH * P].rearrange("p (h t) -> p h t", h=NH), maskU4.t[0:P, :, 0:P], ALU.mult),
                           (psc, maskU4), (scm,))
                    stages.append(q0)
                    for ch in range(nch):
                        def cA(st=st, ch=ch):
                            r0 = ch * C
                            t0 = st * P + ch * C
                            pkv = PSB[3]
                            for h in range(NH):
                                op('pe', lambda e, h=h: e.matmul(pkv.t[:, h * 128:(h + 1) * 128], khat.t[r0:r0 + C, st, h * 128:(h + 1) * 128], vv.t[r0:r0 + C, st, h * 128:(h + 1) * 128],
                                                                 start=True, stop=True), (khat, vv), (pkv,))
                            for h in range(NH):
                                ob = oTb[h // 2]
                                oc = (h % 2) * T + t0
                                op('pe', lambda e, h=h, ob=ob, oc=oc: e.matmul(ob.t[:, oc:oc + C], Sbf.t[:, h, :], qtT.t[:, h, t0:t0 + C], start=True, stop=False),
                                   (Sbf, qtT), (ob,))
                                op('pe', lambda e, h=h, ob=ob, oc=oc: e.matmul(ob.t[:, oc:oc + C], vv.t[r0:r0 + C, st, h * 128:(h + 1) * 128], scm.t[r0:r0 + C, h, r0:r0 + C],
                                                                            start=False, stop=True), (vv, scm), (ob,))
                            dec = eb.t[:, :, t0 + C - 1:t0 + C].to_broadcast([128, NH, 128])
                            op('dve', lambda e: e.tensor_tensor(Sst.t[:], Sst.t[:], dec, ALU.mult), (Sst, eb), (Sst,))

                        def cB(st=st, ch=ch):
                            pkv = PSB[3]
                            pk3 = pkv.t[:].rearrange("p (h v) -> p h v", h=NH)
                            op('dve', lambda e: e.tensor_tensor(Sbf.t[:], Sst.t[:], pk3, ALU.add), (Sst, pkv), (Sbf,))
                            op('dve', lambda e: e.tensor_tensor(Sst.t[:], Sst.t[:], pk3, ALU.add), (Sst, pkv), (Sst,))
                        stages += [cA, cB]
                return stages

            def G_chain():
                stages = []
                for hp in range(2):
                    sgb = sgX[hp]
                    sg3 = sgb.t[:, 0:2 * T].rearrange("p (h t) -> p h t", h=2)

                    def g0(hp=hp, sgb=sgb, sg3=sg3):
                        pg = PSB[4]
                        for hh in range(2):
                            h = hp * 2 + hh
                            for k in range(KC):
                                op('pe', lambda e, k=k, h=h, hh=hh: e.matmul(pg.t[:, hh * T:(hh + 1) * T], wg3[:, k, h * 128:(h + 1) * 128], hT.t[:, k, 0:T],
                                                                          start=(k == 0), stop=(k == KC - 1)),
                                   (hT, wb_g), (pg,))
                        pg3 = pg.t[:, 0:2 * T].rearrange("p (h t) -> p h t", h=2)
                        op('act', lambda e: e.activation(sg3, pg3, AF.Tanh, scale=0.5), (pg,), (sgb,))

                    def g1(hp=hp, sgb=sgb, sg3=sg3):
                        pg = PSB[4]
                        pg3 = pg.t[:, 0:2 * T].rearrange("p (h t) -> p h t", h=2)
                        op('dve', lambda e: e.scalar_tensor_tensor(sg3, sg3, 1.0, pg3, ALU.add, ALU.mult), (sgb, pg), (sgb,))
                    stages += [g0, g1]
                return stages

            def Q_chain():
                stages = []
                S = {}
                QB = rotator([PSB[5], PSB[6]])
                for hp in range(2):
                    def q0(hp=hp):
                        pq = QB()
                        for hh in range(2):
                            h = hp * 2 + hh
                            for kc in range(3):
                                op('pe', lambda e, kc=kc, h=h, hh=hh: e.matmul(pq.t[:, hh * T:(hh + 1) * T], wu3[:, kc, h * 192:h * 192 + 128], cqnT.t[:, kc, 0:T],
                                                                            start=(kc == 0), stop=(kc == 2)), (wb_u, cqnT), (pq,))
                        op('act', lambda e: e.activation(qnT.t[:, hp * 2:hp * 2 + 2, 0:T], pq.t[:, 0:2 * T].rearrange("p (h t) -> p h t", h=2), AF.Copy), (pq,), (qnT,))

                    def q1(hp=hp):
                        pp = S['pp'] = QB()
                        for hh in range(2):
                            h = hp * 2 + hh
                            for kc in range(3):
                                op('pe', lambda e, kc=kc, h=h, hh=hh: e.matmul(pp.t[0:64, hh * T:(hh + 1) * T], wu3[:, kc, h * 192 + 128:h * 192 + 192], cqnT.t[:, kc, 0:T],
                                                                            start=(kc == 0), stop=(kc == 2)), (wb_u, cqnT), (pp,))

                    def q2(hp=hp):
                        ps_ = S['ps'] = QB()
                        for hh in range(2):
                            h = hp * 2 + hh
                            for kc in range(3):
                                op('pe', lambda e, kc=kc, h=h, hh=hh: e.matmul(ps_.t[0:64, hh * T:(hh + 1) * T], wu3[:, kc, 768 + h * 64:768 + (h + 1) * 64], cqnT.t[:, kc, 0:T],
                                                                            start=(kc == 0), stop=(kc == 2)), (wb_u, cqnT), (ps_,))

                    def q3(hp=hp):
                        pp, ps_ = S['pp'], S['ps']
                        op('act', lambda e: e.activation(tB[0].t[0:64, 0:2 * T], pp.t[0:64, 0:2 * T], AF.Copy), (pp,), (tB[0],))
                        op('act', lambda e: e.activation(tB[1].t[0:64, 0:2 * T], ps_.t[0:64, 0:2 * T], AF.Copy), (ps_,), (tB[1],))

                    def q4(hp=hp):
                        cosb = ropeFs.t[:, 0:1, 0:T].to_broadcast([64, 2, T])
                        sinb = ropeFs.t[:, 1:2, 0:T].to_broadcast([64, 2, T])
                        t0v = tB[0].t[0:64, 0:2 * T].rearrange("p (h t) -> p h t", h=2)
                        t1v = tB[1].t[0:64, 0:2 * T].rearrange("p (h t) -> p h t", h=2)
                        op('pool', lambda e: e.tensor_tensor(t0v, t0v, cosb, ALU.mult), (tB[0], ropeFs), (tB[0],))
                        op('pool', lambda e: e.tensor_tensor(t1v, t1v, sinb, ALU.mult), (tB[1], ropeFs), (tB[1],))
                        op('pool', lambda e: e.tensor_tensor(qpeT.t[0:64, hp * 2:hp * 2 + 2, 0:T], t0v, t1v, ALU.add), (tB[0], tB[1]), (qpeT,))
                    stages += [q0, q1, q2, q3, q4]
                return stages

            def KV_chain():
                stages = []
                pb = PSB[7]
                for hp in range(2):
                    def k0(hp=hp):
                        for hh in range(2):
                            h = hp * 2 + hh
                            for kc in range(2):
                                op('pe', lambda e, h=h, hh=hh, kc=kc: e.matmul(pb.t[:, hh * T:(hh + 1) * T], wkv3[:, kc, h * 128:(h + 1) * 128], latT.t[:, kc, 0:T],
                                                                            start=(kc == 0), stop=(kc == 1)),
                                   (wb_kv, latT), (pb,))
                        for hh in range(2):
                            h = hp * 2 + hh
                            dst_ap, dst_buf = seq['KT_dst'](h, tok0, T)
                            op('act', lambda e, hh=hh, dst_ap=dst_ap: e.activation(dst_ap, pb.t[:, hh * T:(hh + 1) * T], AF.Copy), (pb,), (dst_buf,))
                    stages.append(k0)
                for st in range(nst):
                    def v0(st=st):
                        for kc in range(2):
                            op('pe', lambda e, kc=kc: e.matmul(pb.t[0:P, :], latT.t[:, kc, st * P:(st + 1) * P], wkv3[:, kc, 512:1024],
                                                             start=(kc == 0), stop=(kc == 1)),
                               (wb_kv, latT), (pb,))
                        dst_ap, dst_buf = seq['V_dst'](tok0, st, P)
                        op('dve', lambda e: e.tensor_copy(dst_ap, pb.t[0:P, :]), (pb,), (dst_buf,))
                    stages.append(v0)
                return stages

            run_chains([SEQ_chain(), G_chain(), Q_chain(), KV_chain()])
            if last:
                kb.dma('sp', seq['hg_out'].rearrange("h k v -> k h v"), Sst.t[:], (Sst,), (), semof=Sst, store=True)

            def R_chain(hp):
                ob = oTb[hp]
                ob3 = ob.t[:, 0:2 * T].rearrange("p (h t) -> p h t", h=2)
                sgb = sgX[hp]
                sg3 = sgb.t[:, 0:2 * T].rearrange("p (h t) -> p h t", h=2)
                rsb = rsX[hp]
                rs3 = rsb.t[:, 0:2 * T].rearrange("p (h t) -> p h t", h=2)
                sq3 = sqb.t[:, :, 0:T] if hp == 0 else khat.t[:, 0, 0:2 * T].rearrange("p (h t) -> p h t", h=2)
                sqbuf = sqb if hp == 0 else khat
                sq2 = sq3
                S = {}

                def r0():
                    op('act', lambda e: e.activation(sq3, ob3, AF.Square), (ob,), (sqbuf,))
                    pm = S['pm'] = RB()
                    op('pe', lambda e: e.matmul(pm.t[:, 0:2 * T], mean_bf.t[:], sq2, start=True, stop=True), (mean_bf, sqbuf), (pm,))

                def r1():
                    pm = S['pm']
                    pm3 = pm.t[:, 0:2 * T].rearrange("p (h t) -> p h t", h=2)
                    op('act', lambda e: e.activation(rs3, pm3, AF.Ln, bias=EPS), (pm,), (rsb,))

                def r2():
                    op('act', lambda e: e.activation(rs3, rs3, AF.Exp, scale=-0.5), (rsb,), (rsb,))

                def r3():
                    op('dve', lambda e: e.scalar_tensor_tensor(rs3, rs3, 0.5, ob3, ALU.mult, ALU.mult), (rsb, ob), (rsb,))

                def r4():
                    op('dve', lambda e: e.scalar_tensor_tensor(mixT.t[:, hp * 2:hp * 2 + 2, 0:T], rs3, colsb.t[:, HGG:HGG + 1], sg3, ALU.mult, ALU.mult),
                       (rsb, sgb, colsb), (mixT,))
                return [r0, r1, r2, r3, r4]

            run_chains([R_chain(0), R_chain(1)])
            attend(T, seq['keyblocks'](ti))
            wbs = {}

            wffn3 = wffn_b.rearrange("(j p) c -> p j c", p=128)
            dnbufs = [tA[0], tA[1], tC[0], tC[1], tD]
            wdns = {}

            def ffn_load_up(q):
                wb = wnext()
                kb.dma('sp', wb.t[:, 0:4096].rearrange("p (j c) -> p j c", j=2), wffn3[:, 2 * q:2 * q + 2, 0:2048], (wcvB,), (wb,), semof=wb)
                wbs[2 * q] = (wb, wb.t[:, 0:2048])
                wbs[2 * q + 1] = (wb, wb.t[:, 2048:4096])

            def ffn_load_dn(j):
                db = dnbufs[j % 5]
                dv = db.t[:, :].bitcast(BF16)
                kb.dma('sp', dv, wffn_b[j * 128:(j + 1) * 128, 2048:3072], (wcvB,), (db,), semof=db)
                wdns[j] = (db, dv)

            wos = []
            for half in range(2):
                wb_o = wload(wout_b[:, half * 4096:(half + 1) * 4096], 4096, wcvC)
                wos.append((wb_o, wb_o.t[:, 0:4096].rearrange("p (k c) -> p k c", k=KC)))
            ffn_load_up(0)
            ffn_load_up(1)
            for j in range(5):
                ffn_load_dn(j)
            for st in range(nst):
                for half in range(2):
                    wb_o, wo3 = wos[half]
                    po = pnext()
                    for c in range(KC):
                        op('pe', lambda e, c=c: e.matmul(po.t[0:P, :], mixT.t[:, c, st * P:(st + 1) * P], wo3[:, c, :], start=(c == 0), stop=(c == KC - 1)),
                           (mixT, wb_o), (po,))
                    tt = tB[st % 2]
                    op('dve', lambda e: e.tensor_tensor(tt.t[0:P, :], po.t[0:P, :], g1_bc.t[0:P, half * 512:(half + 1) * 512], ALU.mult), (po, g1_bc), (tt,))
                    op('pool', lambda e: e.tensor_tensor(xb.t[0:P, st, half * 512:(half + 1) * 512], xb.t[0:P, st, half * 512:(half + 1) * 512], tt.t[0:P, :], ALU.add),
                       (xps[st], tt), (xps[st],))
            for st in range(nst):
                norm_T_a(xps[st], st, P, scol=4 + st, xs_=(xsb if st == 0 else mixT))
            for st in range(nst):
                norm_T_b(st, P, G2c[r], modc[r], 2, hT, st * P, xs_=(xsb if st == 0 else mixT))
            acc = [[LB[st * 2 + half] for half in range(2)] for st in range(nst)]
            pavs = {}

            def ffn_up(j):
                wb, wv_ = wbs[j]
                wup = wv_.rearrange("p (k a c) -> p k a c", k=KC, a=2)
                pav = bank_of[('c', j)]
                for a_ in range(2):
                    for k in range(KC):
                        op('pe', lambda e, k=k, a_=a_: e.matmul(pav.t[:, a_ * T:(a_ + 1) * T], wup[:, k, a_, :], hT.t[:, k, 0:T], start=(k == 0), stop=(k == KC - 1)),
                           (wb, hT), (pav,))
                pavs[j] = pav

            def ffn_elem(j):
                pav = pavs[j]
                as_ = aS[j % 2]
                c1 = cc[j % 2]
                u = uT[j % 3]
                cwb = CW + j * 3
                op('pool', lambda e: e.tensor_copy(as_.t[:, 0:2], aprev.t[:, j, :]), (aprev,), (as_,))
                op('act', lambda e: e.activation(as_.t[:, 2:T + 2], pav.t[:, 0:T], AF.Copy), (pav,), (as_,))
                op('act', lambda e: e.activation(c1.t[:, 0:T], pav.t[:, 0:T], AF.Identity, scale=colsb.t[:, cwb + 2:cwb + 3], bias=colsb.t[:, CB + j:CB + j + 1]),
                   (pav, colsb), (c1,))
                op('pool', lambda e: e.tensor_copy(aprev.t[:, j, :], as_.t[:, T:T + 2]), (as_,), (aprev,))
                op('dve', lambda e: e.scalar_tensor_tensor(c1.t[:, 0:T], as_.t[:, 1:T + 1], colsb.t[:, cwb + 1:cwb + 2], c1.t[:, 0:T], ALU.mult, ALU.add),
                   (as_, colsb, c1), (c1,))
                op('dve', lambda e: e.scalar_tensor_tensor(c1.t[:, 0:T], as_.t[:, 0:T], colsb.t[:, cwb:cwb + 1], c1.t[:, 0:T], ALU.mult, ALU.add),
                   (as_, colsb, c1), (c1,))
                op('act', lambda e: e.activation(c1.t[:, 0:T], c1.t[:, 0:T], AF.Gelu), (c1,), (c1,))
                op('dve', lambda e: e.tensor_tensor(u.t[:, 0:T], c1.t[:, 0:T], pav.t[:, T:2 * T], ALU.mult), (c1, pav), (u,))

            def ffn_down(j):
                wbs.pop(j)
                wb, wdn = wdns.pop(j)
                u = uT[j % 3]
                for st in range(nst):
                    for half in range(2):
                        ab = acc[st][half]
                        op('pe', lambda e, st=st, half=half, ab=ab: e.matmul(ab.t[0:P, :], u.t[:, st * P:(st + 1) * P], wdn[:, half * 512:(half + 1) * 512],
                                                                          start=(j == 0), stop=(j == NJ - 1)), (u, wb), (ab,))

            ffn_load_up(2)
            ffn_load_up(3)
            items = [('c', j) for j in range(NJ)]
            post = {}
            if nxt is not None:
                nn = nxt[0]['nst']
                ins_at = [(8, 0)] + ([(16, 1)] if nn > 1 else [])
                for (cj, st_) in reversed(ins_at):
                    pos = items.index(('c', cj))
                    items[pos:pos] = [('p', st_, 0), ('p', st_, 1)]
                post[('c', 1)] = ('a', 0)
                if nn > 1:
                    post[('c', 8)] = ('a', 1)
                prefetch(nxt[0], nxt[1], g + 1, None, 'dma')
            bank_of = {it: PSB[i % 4] for i, it in enumerate(items)}
            AH = 3

            def stage1(it):
                if it[0] == 'c':
                    ffn_up(it[1])
                else:
                    prefetch(nxt[0], nxt[1], g + 1, bank_of[it], ('bh', it[1], it[2], 'pe'))

            def stage2(it):
                if it[0] == 'c':
                    ffn_elem(it[1])
                else:
                    prefetch(nxt[0], nxt[1], g + 1, bank_of[it], ('bh', it[1], it[2], 'ev'))

            for i in range(min(AH, len(items))):
                stage1(items[i])
            for i, it in enumerate(items):
                if it[0] == 'c':
                    j = it[1]
                    if j % 2 == 0 and j // 2 + 4 < NJ // 2:
                        ffn_load_up(j // 2 + 4)
                stage2(it)
                if it in post:
                    prefetch(nxt[0], nxt[1], g + 1, None, post[it])
                if i + AH < len(items):
                    stage1(items[i + AH])
                if it[0] == 'c':
                    ffn_down(it[1])
                    if it[1] + 5 < NJ:
                        ffn_load_dn(it[1] + 5)
            if nxt is not None:
                preloaded_w[g + 1] = (load_win(512, 1024), load_win(1024, 1536), load_win(2048, 2432), load_win(2432, 2752))
            def tail():
                if last:
                    kb.dma('sp', seq['cv_out'], aprev.t[:], (aprev,), (), semof=aprev, store=True)
                for st in range(nst):
                    for half in range(2):
                        ab = acc[st][half]
                        tt = tB[(st * 2 + half) % 2]
                        op('dve', lambda e: e.tensor_tensor(tt.t[0:P, :], ab.t[0:P, :], g2_bc.t[0:P, half * 512:(half + 1) * 512], ALU.mult), (ab, g2_bc), (tt,))
                        op('pool', lambda e: e.tensor_tensor(xb.t[0:P, st, half * 512:(half + 1) * 512], xb.t[0:P, st, half * 512:(half + 1) * 512], tt.t[0:P, :], ALU.add),
                           (xps[st], tt), (xps[st],))
                for st in range(nst):
                    xv = xb.t[0:P, st, :]
                    op('act', lambda e: e.activation(xsb.t[0:P, :], xv, AF.Square, accum_out=stat.t[0:P, 3:4]), (xps[st],), (xsb, stat))
                    rstd_from(3, P, D)
                    op('dve', lambda e: e.scalar_tensor_tensor(xv, xv, stat.t[0:P, 3:4], fg_bc.t[0:P, :], ALU.mult, ALU.mult), (xps[st], stat, fg_bc), (xps[st],))
                kb.dma('sp', seq['y_out'][tok0:tok0 + T, :].rearrange("(n p) d -> p n d", p=P), xb.t[0:P, 0:nst, :], tuple(xps[0:nst]), (), semof=xb, store=True)

            if (not last) and P == 128:
                pending_tail.append(tail)
            else:
                tail()

        kb.dma('sp', mixT.t[:, :, :].rearrange("p a b -> p (a b)"), wukv_b, (wcv0,), (mixT,), semof=mixT)
        wkv3c = mixT.t[:, :, :].rearrange("p a b -> p (a b)").rearrange("p (k c) -> p k c", k=2)
        msi = 0
        for g in range(PAST // TP):
            for _ in range(3):
                if msi < len(mod_steps):
                    mod_steps[msi]()
                    msi += 1
            cl = xparts[g % 2][0]
            clv = cl.t[:, 0, 0:640].rearrange("p (n c) -> p n c", n=2)
            kb.dma('sp', clv[:, :, 0:256], clat[g * TP:(g + 1) * TP, :].rearrange("(n p) c -> p n c", p=128), (), (cl,), semof=xbuf[g % 2])
            kb.dma('sp', clv[:, :, 256:320], ckpe[g * TP:(g + 1) * TP, :].rearrange("(n p) c -> p n c", p=128), (), (cl,), semof=xbuf[g % 2])
            for st in range(NST):
                pt2 = pnext()
                for kc in range(2):
                    op('pe', lambda e, kc=kc: e.transpose(pt2.t[:, kc * 128:(kc + 1) * 128], clv[:, st, kc * 128:(kc + 1) * 128], ident), (cl, cm), (pt2,))
                op('act', lambda e: e.activation(latT.t[:, :, st * 128:(st + 1) * 128], pt2.t[:, 0:256].rearrange("p (k t) -> p k t", k=2), AF.Copy), (pt2,), (latT,))
                pt3 = pnext()
                op('pe', lambda e: e.transpose(pt3.t[0:64, 0:128], clv[:, st, 256:320], ident), (cl, cm), (pt3,))
                c0 = g * TP + st * 128
                op('dve', lambda e: e.tensor_copy(kpeT.t[0:64, c0:c0 + 128], pt3.t[0:64, 0:128]), (pt3,), (kpeT,))
            kv_project(TP, wkv3c, mixT, lambda h, g=g: (KT.t[:, h, g * TP:(g + 1) * TP], KT),
                       lambda st, g=g: (Vr.t[:, g * NST + st, :], Vr), NST, 128)
        while msi < len(mod_steps):
            mod_steps[msi]()
            msi += 1
        mod_finish()
        kb.dma('pool', wuq_b.rearrange("p (a b) -> (p a) b", b=1024), w_uq_l.rearrange("p k c -> (p k) c"), reads=(modv,), semof=wcvC, nowait_w=True)
        kb.dma('pool', wout_b.rearrange("p (a b) -> (p a) b", b=2048), w_out_l.rearrange("p h k c -> p (h k c)").rearrange("p (a b) -> (p a) b", b=2048),
               reads=(modv,), semof=wcvC, nowait_w=True)
        wcvC.w = ('ls_wcvC', wcvC.lsem, wcvC.lcnt, 'dma')
        wsrc = w_ffn_l.rearrange("j p c -> (j p) c")
        RH = NJ * 128 // 2
        for r0 in range(0, NJ * 128, RH):
            kb.dma('pool', wffn_b[r0:r0 + RH, 0:2048], wsrc[r0:r0 + RH, 0:2048], reads=(modv,), semof=wcvB, nowait_w=True)
            kb.dma('pool', wffn_b[r0:r0 + RH, 2048:3072], wsrc[r0:r0 + RH, 2048:3072], reads=(modv,), semof=wcvB, nowait_w=True)
        wcvB.w = ('ls_wcvB', wcvB.lsem, wcvB.lcnt, 'dma')


        def full_block(kt):
            return (lambda h: (KT.t[:, h, kt * 128:(kt + 1) * 128], KT), kpeT.t[:, kt * 128:(kt + 1) * 128], kpeT,
                    lambda h: (Vr.t[:, kt, h * 128:(h + 1) * 128], Vr), 128)

        def sample_blocks(ti):
            bl = []
            for kt in range(PAST // 128):
                bl.append(full_block(kt) + (0, False))
            bl.append((lambda h: (KTs.t[:, h, :], KTs), kpeTs.t[:, :], kpeTs, lambda h: (Vs.t[:, h * 128:(h + 1) * 128], Vs), S_S, 0, False))
            return bl

        seq_s = dict(P=S_S, nst=1, C=S_S, r=1, ntiles=1, pos0=PAST, x=xs_in, y_out=y_s, lat_out=lat_s, kpe_out=kpe_s, hg_out=hg_s, cv_out=cv_s,
                     kpe_dst=lambda t0, P: (kpeTs.t[0:64, t0:t0 + P], kpeTs),
                     KT_dst=lambda h, t0, T: (KTs.t[:, h, t0:t0 + T], KTs),
                     V_dst=lambda t0, st, P: (Vs.t[0:P, :], Vs),
                     keyblocks=sample_blocks)
        def prompt_blocks(ti):
            bl = []
            for kt in range(ti * NST):
                bl.append(full_block(kt) + (0, False))
            for j in range(NST):
                bl.append(full_block(ti * NST + j) + (j * 128, True))
            return bl

        seq_p = dict(P=128, nst=NST, C=64, r=0, ntiles=S_P // TP, pos0=0, x=xp, y_out=y_p, lat_out=lat_p, kpe_out=kpe_p, hg_out=hg_p, cv_out=cv_p,
                     kpe_dst=lambda t0, P: (kpeT.t[0:64, t0:t0 + P], kpeT),
                     KT_dst=lambda h, t0, T: (KT.t[:, h, t0:t0 + T], KT),
                     V_dst=lambda t0, st, P: (Vr.t[:, t0 // 128 + st, :], Vr),
                     keyblocks=prompt_blocks)
        kb.dma('sp', Sst.t[:], shg.rearrange("h k v -> k h v"), (), (Sst,), semof=Sst)
        op('act', lambda e: e.activation(Sbf.t[:], Sst.t[:], AF.Copy), (Sst,), (Sbf,))
        kb.dma('sp', aprev.t[:], sconv, (), (aprev,), semof=aprev)
        load_gbc(1, S_S)
        run_tile(seq_s, 0, 0, (seq_p, 0))

        op('pool', lambda e: e.memset(Sst.t[:], 0.0), (), (Sst,))
        op('pool', lambda e: e.memset(Sbf.t[:], 0.0), (), (Sbf,))
        op('pool', lambda e: e.memset(aprev.t[:], 0.0), (), (aprev,))
        load_gbc(0, 128)

        for ti in range(seq_p['ntiles']):
            run_tile(seq_p, ti, 1 + ti, (seq_p, ti + 1) if ti + 1 < seq_p['ntiles'] else None)
        kb.finish()
        build_nc.stats = (kb.ninst, kb.nsem, dict(kb.cnt))
        build_nc.used = kb.used
        build_nc.sbuf_left = nc.sbuf_bytes_remaining
    return nc


def _rope_tables():
    half = 32
    inv_freq = (np.float32(10000.0) ** (-np.arange(half, dtype=np.float32) / np.float32(half))).astype(np.float32)
    pos = np.arange(S_P + S_S, dtype=np.float32)
    ang = (pos[:, None] * inv_freq[None, :]).astype(np.float32)
    cos = np.cos(ang).astype(np.float32)
    sin = np.sin(ang).astype(np.float32)
    ropeT = np.concatenate([cos, cos, sin, sin], axis=1).astype(np.float32)
    ropeF = np.stack([np.concatenate([cos.T, cos.T], 0), np.concatenate([-sin.T, sin.T], 0)], axis=1)
    return np.ascontiguousarray(ropeT), np.ascontiguousarray(ropeF.astype(np.float32))


def _const_mats():
    idx = np.arange(128)
    same = (idx[:, None] // 64) == (idx[None, :] // 64)
    U = (same & (idx[:, None] <= idx[None, :])).astype(np.float32)
    Ux = (same & (idx[:, None] > idx[None, :])).astype(np.float32)
    return np.ascontiguousarray(np.stack([np.eye(128, dtype=np.float32), U, Ux], axis=1))


def kernel(x_prompt, x_sample, c_prompt, c_sample, cache_kv_latent, cache_k_rope, state_hgrn,
           state_ffn_conv, w_ada, b_ada, norm_mix_gain, w_in, hg_lb_logits, hg_norm_gain,
           mla_q_norm_gain, mla_kv_norm_gain, w_uq, w_uk, w_uv, w_out, norm_ffn_gain, w_up,
           conv_w, conv_b, w_down, final_norm_gain):
    f = lambda a: np.ascontiguousarray(np.asarray(a, dtype=np.float32))
    x_prompt, x_sample, c_prompt, c_sample = f(x_prompt), f(x_sample), f(c_prompt), f(c_sample)
    n = 8
    ropeT, ropeF = _rope_tables()
    cmat = _const_mats()
    cols = np.zeros((128, 128), np.float32)
    cols[:, 0:8] = f(norm_mix_gain)[0].reshape(8, 128).T
    cols[:, 8:16] = f(norm_ffn_gain)[0].reshape(8, 128).T
    cols[:, 16:19] = f(mla_q_norm_gain)[0].reshape(3, 128).T
    cols[:, 19] = f(hg_norm_gain)[0]
    cw = f(conv_w)[0].reshape(3, NJ, 128)
    cols[:, 20:20 + 66] = cw.transpose(2, 1, 0).reshape(128, 66)
    cols[:, 86:86 + NJ] = f(conv_b)[0].reshape(NJ, 128).T
    bcs = np.zeros((4, 1024), np.float32)
    bcs[0] = f(final_norm_gain)
    bcs[1, 0:256] = f(mla_kv_norm_gain)[0]
    bcs[2, 0:512] = f(hg_lb_logits)[0]
    bcs[3, 0:512] = f(hg_lb_logits)[1]
    w_in_l = np.ascontiguousarray(f(w_in)[0].reshape(KC, 128, 2752).transpose(1, 0, 2))
    wq = f(w_uq)[0]
    sw = []
    for h in range(NH):
        sw.append(wq[:, h * 192 + 160:h * 192 + 192])
        sw.append(wq[:, h * 192 + 128:h * 192 + 160])
    wq_ext = np.concatenate([wq] + sw, axis=1)
    w_uq_l = np.ascontiguousarray(wq_ext.reshape(3, 128, 1024).transpose(1, 0, 2))
    wkv = np.concatenate([f(w_uk)[0].reshape(256, 512), f(w_uv)[0].reshape(256, 512)], axis=1)
    w_ukv_l = np.ascontiguousarray(wkv.reshape(2, 128, 1024).transpose(1, 0, 2))
    wo = f(w_out)[0]
    w_out_l = np.ascontiguousarray(wo.reshape(KC, 128, 2, 512).transpose(1, 2, 0, 3))
    wu = f(w_up)[0]
    wa = wu[:, :DFF].reshape(KC, 128, NJ, 128)
    wv = wu[:, DFF:].reshape(KC, 128, NJ, 128)
    wup_l = np.stack([wa, wv], axis=3)
    wup_l = wup_l.transpose(2, 1, 0, 3, 4).reshape(NJ, 128, 2048)
    wd_l = f(w_down)[0].reshape(NJ, 128, 1024)
    w_ffn_l = np.ascontiguousarray(np.concatenate([wup_l, wd_l], axis=2))
    b_ada2 = np.ascontiguousarray(np.broadcast_to(f(b_ada)[0][None, :], (2, 6 * D)))
    shared = dict(w_ada=f(w_ada)[0], b_ada2=b_ada2, cols=cols, bcs=bcs, cmat=cmat, ropeT=ropeT, ropeF=ropeF,
                  w_in_l=w_in_l, w_uq_l=w_uq_l, w_ukv_l=w_ukv_l, w_out_l=w_out_l, w_ffn_l=w_ffn_l)
    in_maps = []
    for b in range(n):
        c2 = np.stack([c_prompt[b], c_sample[b]], axis=0)
        c2l = np.ascontiguousarray(c2.reshape(2, KC, 128).transpose(2, 1, 0))
        sc = f(state_ffn_conv)[0, b]
        scl = np.ascontiguousarray(sc.reshape(2, NJ, 128).transpose(2, 1, 0))
        m = dict(shared)
        m.update(xp=x_prompt[b], xs=x_sample[b], c2=c2l, clat=f(cache_kv_latent)[0, b], ckpe=f(cache_k_rope)[0, b],
                 shg=f(state_hgrn)[0, b], sconv=scl)
        in_maps.append(m)
    build_nc()
    needed = {e: [] for e in ['pe', 'act', 'dve', 'pool']}
    for (e, i) in build_nc.used:
        needed[e].append(i)
    for e in needed:
        needed[e].sort()
    nc = build_nc(needed)
    res = run_bass_kernel_spmd(nc, in_maps, core_ids=list(range(n)))
    R = res.results

    def st(name):
        return np.stack([np.asarray(R[b][name], dtype=np.float32) for b in range(n)], axis=0)

    def cvfix(a):
        return np.ascontiguousarray(a.transpose(0, 3, 2, 1).reshape(n, 2, DFF))

    y_prompt = st("y_p")
    y_sample = st("y_s")
    return (y_prompt, y_sample, st("lat_p")[None], st("kpe_p")[None], st("hg_p")[None], cvfix(st("cv_p"))[None],
            st("lat_s")[None], st("kpe_s")[None], st("hg_s")[None], cvfix(st("cv_s"))[None])
```

```python
import bisect
import numpy as np
from contextlib import ExitStack
import concourse.bass as bass
import concourse.mybir as mybir
from concourse.bass_utils import run_bass_kernel_spmd

F32 = mybir.dt.float32
BF16 = mybir.dt.bfloat16
AF = mybir.ActivationFunctionType
ALU = mybir.AluOpType

D = 1024
KC = 8
S_P = 4096
S_S = 16
PAST = 4096
NH = 4
DFF = 2816
NJ = 22
EPS = 1e-6
MLA_SCALE = float((128 + 64) ** -0.5)
NST = 2
TP = NST * 128
SAME_ENGINE_SYNC = True


class Buf:
    def __init__(self, t, name):
        self.t = t
        self.name = name
        self.w = None
        self.r = {}
        self.lsem = None
        self.lcnt = 0
        self.ssem = None
        self.scnt = 0

    def __getitem__(self, k):
        return self.t[k]


class KB:
    def __init__(self, nc, es, needed=None):
        self.nc = nc
        self.es = es
        self.needed = needed
        self.used = set()
        self.eng = {'pe': nc.tensor, 'act': nc.scalar, 'dve': nc.vector, 'pool': nc.gpsimd, 'sp': nc.sync}
        self.sem = {}
        self.cnt = {}
        for e in ['pe', 'act', 'dve', 'pool']:
            self.sem[e] = es.enter_context(nc.semaphore('s_' + e))
            self.cnt[e] = 0
        self.seen = {e: {} for e in self.eng}
        self.needed_set = {e: set(v) for e, v in needed.items()} if needed is not None else None
        self.nsem = 4
        self.allbufs = []
        self.ninst = 0

    def newsem(self, name):
        self.nsem += 1
        return self.es.enter_context(self.nc.semaphore(name))

    def sb(self, name, shape, dt=F32):
        t = self.es.enter_context(self.nc.sbuf_tensor(name, list(shape), dt))
        b = Buf(t, name)
        self.allbufs.append(b)
        return b

    def ps(self, name, shape, dt=F32):
        t = self.es.enter_context(self.nc.psum_tensor(name, list(shape), dt))
        b = Buf(t, name)
        self.allbufs.append(b)
        return b

    def virt(self, name):
        b = Buf(None, name)
        self.allbufs.append(b)
        return b

    def _waits(self, eng, reads, writes):
        need = {}

        def add(ev):
            if ev is None:
                return
            sn, sem, val, src = ev
            if src == eng and (eng == 'pe' or not SAME_ENGINE_SYNC):
                return
            if self.seen[eng].get(sn, 0) >= val:
                return
            if sn not in need or need[sn][1] < val:
                need[sn] = (sem, val)

        for b in reads:
            add(b.w)
        for b in writes:
            add(b.w)
            for ev in b.r.values():
                add(ev)
        E = self.eng[eng]
        for sn, (sem, val) in need.items():
            v = val
            if sn.startswith('s_'):
                src = sn[2:]
                self.used.add((src, val))
                if self.needed is not None:
                    v = bisect.bisect_right(self.needed[src], val)
            E.wait_ge(sem, v)
            self.seen[eng][sn] = val
            self.ninst += 1

    def _record(self, ev, reads, writes):
        for b in writes:
            b.w = ev
            b.r = {}
        for b in reads:
            if b not in writes:
                b.r[ev[0]] = ev

    def op(self, eng, fn, reads=(), writes=()):
        self._waits(eng, reads, writes)
        ins = fn(self.eng[eng])
        self.cnt[eng] += 1
        if self.needed is None or self.cnt[eng] in self.needed_set[eng]:
            ins.then_inc(self.sem[eng], 1)
        self.ninst += 1
        ev = ('s_' + eng, self.sem[eng], self.cnt[eng], eng)
        self._record(ev, reads, writes)

    def dma(self, q, out, in_, reads=(), writes=(), semof=None, store=False, nowait_w=False, **kw):
        if nowait_w:
            self._waits(q, reads, ())
        else:
            self._waits(q, reads, writes)
        ins = self.eng[q].dma_start(out=out, in_=in_, **kw)
        self.ninst += 1
        if store:
            if semof.ssem is None:
                semof.ssem = self.newsem('ss_' + semof.name)
            semof.scnt += 16
            ins.then_inc(semof.ssem, 16)
            ev = ('ss_' + semof.name, semof.ssem, semof.scnt, 'dma')
        else:
            if semof.lsem is None:
                semof.lsem = self.newsem('ls_' + semof.name)
            semof.lcnt += 16
            ins.then_inc(semof.lsem, 16)
            ev = ('ls_' + semof.name, semof.lsem, semof.lcnt, 'dma')
        self._record(ev, reads, writes)
        return ev

    def finish(self):
        for b in self.allbufs:
            if b.ssem is not None and b.scnt > 0:
                self.nc.sync.wait_ge(b.ssem, b.scnt)
            if b.lsem is not None and b.lcnt > 0:
                self.nc.sync.wait_ge(b.lsem, b.lcnt)


def build_nc(needed=None):
    nc = bass.Bass("TRN2", target_bir_lowering=False)

    def din(name, shape, dt=F32):
        return nc.dram_tensor(name, list(shape), dt, kind="ExternalInput").ap()

    def dout(name, shape, dt=F32):
        return nc.dram_tensor(name, list(shape), dt, kind="ExternalOutput").ap()

    def dscr(name, shape, dt):
        return nc.dram_tensor(name, list(shape), dt, kind="Internal").ap()

    xp = din("xp", [S_P, D])
    xs_in = din("xs", [S_S, D])
    c2 = din("c2", [128, KC, 2])
    clat = din("clat", [PAST, 256])
    ckpe = din("ckpe", [PAST, 64])
    shg = din("shg", [NH, 128, 128])
    sconv = din("sconv", [128, NJ, 2])
    w_ada = din("w_ada", [D, 6 * D])
    b_ada2 = din("b_ada2", [2, 6 * D])
    cols = din("cols", [128, 128])
    bcs = din("bcs", [4, 1024])
    cmat = din("cmat", [128, 3, 128])
    ropeT = din("ropeT", [S_P + S_S, 128])
    ropeF = din("ropeF", [64, 2, S_P + S_S])
    w_in_l = din("w_in_l", [128, KC, 2752])
    w_uq_l = din("w_uq_l", [128, 3, 1024])
    w_ukv_l = din("w_ukv_l", [128, 2, 1024])
    w_out_l = din("w_out_l", [128, 2, KC, 512])
    w_ffn_l = din("w_ffn_l", [NJ, 128, 3072])

    y_p = dout("y_p", [S_P, D])
    y_s = dout("y_s", [S_S, D])
    lat_p = dout("lat_p", [S_P, 256])
    kpe_p = dout("kpe_p", [S_P, 64])
    hg_p = dout("hg_p", [NH, 128, 128])
    cv_p = dout("cv_p", [128, NJ, 2])
    lat_s = dout("lat_s", [S_S, 256])
    kpe_s = dout("kpe_s", [S_S, 64])
    hg_s = dout("hg_s", [NH, 128, 128])
    cv_s = dout("cv_s", [128, NJ, 2])

    modscr = dscr("modscr", [2, 6 * D], F32)
    win_b = dscr("win_b", [128, KC * 2752], BF16)
    wuq_b = dscr("wuq_b", [128, 3 * 1024], BF16)
    wukv_b = dscr("wukv_b", [128, 2 * 1024], BF16)
    wout_b = dscr("wout_b", [128, 2 * KC * 512], BF16)
    wffn_b = dscr("wffn_b", [NJ * 128, 3072], BF16)

    es = ExitStack()
    with es:
        nc_ctx = es.enter_context(nc.allow_non_contiguous_dma(reason="small layout loads"))
        es.enter_context(nc.allow_low_precision(reason="bf16 matmul operands"))
        kb = KB(nc, es, needed)
        op = kb.op

        cm = kb.sb("cm", [128, 3, 128])
        colsb = kb.sb("colsb", [128, 128])
        fg_bc = kb.sb("fg_bc", [128, 1024])
        kvn_bc = kb.sb("kvn_bc", [128, 256])
        lb_bc = kb.sb("lb_bc", [128, 512])
        maskU4 = kb.sb("maskU4", [128, 4, 128], BF16)
        ones_bf = kb.sb("ones_bf", [128, 128], BF16)
        mean_bf = kb.sb("mean_bf", [128, 128], BF16)
        mhalf = kb.sb("mhalf", [128, 1])
        c2s = kb.sb("c2s", [128, KC, 2])
        c2e = kb.sb("c2e", [128, KC, 2])
        modc = [kb.sb("modc%d" % r, [128, 4, KC]) for r in range(2)]
        G1c = [kb.sb("G1c%d" % r, [128, KC]) for r in range(2)]
        G2c = [kb.sb("G2c%d" % r, [128, KC]) for r in range(2)]
        g1_bc = kb.sb("g1_bc", [128, 1024])
        g2_bc = kb.sb("g2_bc", [128, 1024])
        ident = cm.t[:, 0, :]
        Umat = cm.t[:, 1, :]
        Uxmat = cm.t[:, 2, :]
        GMIX, GFFN, QN, HGG, CW, CB = 0, 8, 16, 19, 20, 86
        KT = kb.sb("KT", [128, NH, PAST], BF16)
        Vr = kb.sb("Vr", [128, PAST // 128, 512], BF16)
        kpeT = kb.sb("kpeT", [128, PAST], BF16)
        KTs = kb.sb("KTs", [128, NH, S_S], BF16)
        Vs = kb.sb("Vs", [S_S, 512], BF16)
        kpeTs = kb.sb("kpeTs", [128, S_S], BF16)
        Sst = kb.sb("Sst", [128, NH, 128])
        Sbf = kb.sb("Sbf", [128, NH, 128], BF16)
        aprev = kb.sb("aprev", [128, NJ, 2])
        NPOOL = 4
        wpool = [kb.sb("wp%d" % i, [128, 4096], BF16) for i in range(NPOOL)]
        wpi = [0]

        def wnext():
            b = wpool[wpi[0] % NPOOL]
            wpi[0] += 1
            return b

        xbuf = [kb.sb("xb%d" % i, [128, NST, D]) for i in range(2)]
        xparts = []
        for xb_ in xbuf:
            ps_list = []
            for st_ in range(NST):
                pb_ = Buf(xb_.t, xb_.name + "s%d" % st_)
                kb.allbufs.append(pb_)
                ps_list.append(pb_)
            xparts.append(ps_list)
        xsb = kb.sb("xsb", [128, D])
        hTs = [kb.sb("hT%d" % i, [128, KC, TP], BF16) for i in range(2)]
        mixT = kb.sb("mixT", [128, KC, TP], BF16)
        stat = kb.sb("stat", [128, 8])
        tA = [kb.sb("tA%d" % i, [128, 512]) for i in range(2)]
        tB = [kb.sb("tB%d" % i, [128, 512]) for i in range(2)]
        tC = [kb.sb("tC%d" % i, [128, 512]) for i in range(2)]
        tD = kb.sb("tD", [128, 512])
        khat = kb.sb("khat", [128, NST, 512], BF16)
        vv = kb.sb("vv", [128, NST, 512], BF16)
        eb = kb.sb("eb", [128, NH, TP])
        ktT = kb.sb("ktT", [128, NH, TP], BF16)
        qtT = kb.sb("qtT", [128, NH, TP], BF16)
        scm = kb.sb("scm", [128, NH, 128], BF16)
        sqb = kb.sb("sqb", [128, 2, TP], BF16)
        sg = tA[0]
        rs = tA[1]
        cqnT = kb.sb("cqnT", [128, 3, TP], BF16)
        latT = kb.sb("latT", [128, 2, TP], BF16)
        qnT = kb.sb("qnT", [128, NH, TP], BF16)
        qpeT = kb.sb("qpeT", [128, NH, TP], BF16)
        ropeTs = kb.sb("ropeTs", [128, NST, 128])
        ropeFs = kb.sb("ropeFs", [64, 2, TP])
        lato1 = kb.sb("lato0", [128, NST, 256])
        kpeo1 = kb.sb("kpeo0", [128, NST, 64])
        lato = [lato1, lato1]
        kpeo = [kpeo1, kpeo1]
        pT = [kb.sb("pT%d" % i, [128, TP], BF16) for i in range(4)]
        aS = [kb.sb("aS%d" % i, [128, TP + 2]) for i in range(2)]
        cc = [kb.sb("cc%d" % i, [128, TP]) for i in range(2)]
        uT = [kb.sb("uT%d" % i, [128, TP], BF16) for i in range(3)]
        PSB = [kb.ps("psb%d" % i, [128, 512]) for i in range(8)]
        rot = [0]

        def pnext():
            b = PSB[rot[0] % 4]
            rot[0] += 1
            return b

        LB = PSB[4:8]
        wcvA = kb.virt("wcvA")
        wcvC = kb.virt("wcvC")
        wcv0 = kb.virt("wcv0")
        wcvB = kb.virt("wcvB")
        cst = kb.virt("cst")
        modv = kb.virt("modv")

        const_bufs = []

        def cload(buf, out_ap, in_ap):
            kb.dma('sp', out_ap, in_ap, reads=(), writes=(), semof=cst, nowait_w=True)
            const_bufs.append(buf)

        cload(cm, cm.t[:], cmat)
        cload(colsb, colsb.t[:], cols)
        cload(fg_bc, fg_bc.t[:], bcs[0, :].partition_broadcast(128))
        cload(kvn_bc, kvn_bc.t[:], bcs[1, 0:256].partition_broadcast(128))
        cload(tD, tD.t[:], bcs[2, 0:512].partition_broadcast(128))
        cload(lb_bc, lb_bc.t[:], bcs[3, 0:512].partition_broadcast(128))
        cload(c2s, c2s.t[:], c2)
        cev = ('ls_cst', cst.lsem, cst.lcnt, 'dma')
        for b in const_bufs:
            b.w = cev

        def wcast(dst2, src2, virt):
            R = dst2.shape[0]
            for r0 in range(0, R, 512):
                r1 = min(R, r0 + 512)
                kb.dma('pool', dst2[r0:r1, :], src2[r0:r1, :], reads=(), writes=(), semof=virt, nowait_w=True)

        wcast(wukv_b.rearrange("p (a b) -> (p a) b", b=1024), w_ukv_l.rearrange("p k c -> (p k) c"), wcv0)
        wcv0.w = ('ls_wcv0', wcv0.lsem, wcv0.lcnt, 'dma')
        wcast(win_b.rearrange("p (a b) -> (p a) b", b=1376), w_in_l.rearrange("p k c -> p (k c)").rearrange("p (a b) -> (p a) b", b=1376), wcvA)
        wcvA.w = ('ls_wcvA', wcvA.lsem, wcvA.lcnt, 'dma')
        op('pool', lambda e: e.memset(ones_bf.t[:], 1.0), (), (ones_bf,))
        op('pool', lambda e: e.memset(mean_bf.t[:], 1.0 / 128.0), (), (mean_bf,))
        op('pool', lambda e: e.memset(mhalf.t[:], -0.5), (), (mhalf,))
        op('pool', lambda e: e.memset(kpeT.t[64:128, :], 0.0), (), (kpeT,))
        op('pool', lambda e: e.memset(kpeTs.t[64:128, :], 0.0), (), (kpeTs,))
        op('pool', lambda e: e.memset(qpeT.t[64:128, :, :], 0.0), (), (qpeT,))
        for h in range(NH):
            op('pool', lambda e, h=h: e.tensor_copy(maskU4.t[:, h, :], Umat), (cm,), (maskU4,))
        op('dve', lambda e: e.tensor_tensor(lb_bc.t[:], lb_bc.t[:], tD.t[:], ALU.subtract), (lb_bc, tD), (lb_bc,))
        op('act', lambda e: e.activation(lb_bc.t[:], lb_bc.t[:], AF.Exp), (lb_bc,), (lb_bc,))
        op('dve', lambda e: e.tensor_scalar(lb_bc.t[:], lb_bc.t[:], 1.0, None, ALU.add), (lb_bc,), (lb_bc,))
        op('dve', lambda e: e.reciprocal(lb_bc.t[:], lb_bc.t[:]), (lb_bc,), (lb_bc,))
        op('act', lambda e: e.activation(c2e.t[:], c2s.t[:], AF.Exp, scale=-1.0), (c2s,), (c2e,))
        op('dve', lambda e: e.tensor_scalar(c2e.t[:], c2e.t[:], 1.0, None, ALU.add), (c2e,), (c2e,))
        op('dve', lambda e: e.reciprocal(c2e.t[:], c2e.t[:]), (c2e,), (c2e,))
        op('dve', lambda e: e.tensor_tensor(c2e.t[:], c2e.t[:], c2s.t[:], ALU.mult), (c2e, c2s), (c2e,))
        mod_steps = []

        def mk_mod_step(cb, k):
            def step():
                if k == 0:
                    kb.dma('sp', tA[0].t[0:2, :], b_ada2[:, cb * 1024:cb * 1024 + 512], (), (tA[0],), semof=tA[0])
                    kb.dma('sp', tA[1].t[0:2, :], b_ada2[:, cb * 1024 + 512:(cb + 1) * 1024], (), (tA[1],), semof=tA[1])
                wb = wnext()
                wv = wb.t[:, 0:2048].bitcast(F32)
                kb.dma('sp', wv, w_ada[k * 128:(k + 1) * 128, cb * 1024:(cb + 1) * 1024], (), (wb,), semof=wb)
                for i in range(2):
                    op('pe', lambda e, i=i: e.matmul(LB[i].t[0:2, :], c2e.t[:, k, :], wv[:, i * 512:(i + 1) * 512],
                                                     start=(k == 0), stop=(k == KC - 1)),
                       (c2e, wb), (LB[i],))
                if k == KC - 1:
                    for i in range(2):
                        op('dve', lambda e, i=i: e.tensor_tensor(xsb.t[0:2, i * 512:(i + 1) * 512], LB[i].t[0:2, :], tA[i].t[0:2, :], ALU.add),
                           (LB[i], tA[i]), (xsb,))
                    kb.dma('sp', modscr[:, cb * 1024:(cb + 1) * 1024], xsb.t[0:2, :], (xsb,), (modv,), semof=modv, nowait_w=True)
            return step

        for cb in range(6):
            for k in range(KC):
                mod_steps.append(mk_mod_step(cb, k))

        def mod_finish():
            mt = tA[0]
            kb.dma('sp', mt.t[0:96, 0:128], modscr.rearrange("r (m k p) -> (r m k) p", m=6, k=KC), (modv,), (mt,), semof=mt)
            pb = LB[0]
            op('pe', lambda e: e.transpose(pb.t[:, 0:96], mt.t[0:96, 0:128], ident[0:96, 0:96]), (mt, cm), (pb,))
            for r in range(2):
                pv4 = pb.t[:, r * 48:(r + 1) * 48].rearrange("p (m k) -> p m k", m=6)
                op('act', lambda e, r=r, pv4=pv4: e.activation(modc[r].t[:, 0:2, :], pv4[:, 0:2, :], AF.Copy), (pb,), (modc[r],))
                op('act', lambda e, r=r, pv4=pv4: e.activation(modc[r].t[:, 2:4, :], pv4[:, 3:5, :], AF.Copy), (pb,), (modc[r],))
                op('dve', lambda e, r=r: e.scalar_tensor_tensor(G1c[r].t[:], modc[r].t[:, 1, :], 1.0, colsb.t[:, GMIX:GMIX + 8], ALU.add, ALU.mult),
                   (modc[r], colsb), (G1c[r],))
                op('dve', lambda e, r=r: e.scalar_tensor_tensor(G2c[r].t[:], modc[r].t[:, 3, :], 1.0, colsb.t[:, GFFN:GFFN + 8], ALU.add, ALU.mult),
                   (modc[r], colsb), (G2c[r],))

        def load_gbc(r, P):
            kb.dma('sp', g1_bc.t[0:P, :], modscr[r, 2048:3072].partition_broadcast(P), (modv,), (g1_bc,), semof=g1_bc)
            kb.dma('sp', g2_bc.t[0:P, :], modscr[r, 5120:6144].partition_broadcast(P), (modv,), (g2_bc,), semof=g2_bc)

        def wload(src_ap, ncols, virt):
            wb = wnext()
            kb.dma('sp', wb.t[:, 0:ncols], src_ap, (virt,), (wb,), semof=wb)
            return wb

        win3 = win_b.rearrange("p (k c) -> p k c", k=KC)

        def load_win(c0, c1):
            wb = wnext()
            n = c1 - c0
            kb.dma('sp', wb.t[:, 0:KC * n].rearrange("p (k c) -> p k c", k=KC), win3[:, :, c0:c1], (wcvA,), (wb,), semof=wb)
            return wb, wb.t[:, 0:KC * n].rearrange("p (k c) -> p k c", k=KC)

        def rstd_from(col, P, N):
            op('pool', lambda e: e.tensor_scalar(stat.t[0:P, col:col + 1], stat.t[0:P, col:col + 1], 1.0 / N, EPS, ALU.mult, ALU.add),
               (stat,), (stat,))
            op('pool', lambda e: e.tensor_tensor(stat.t[0:P, col:col + 1], stat.t[0:P, col:col + 1], mhalf.t[0:P, 0:1], ALU.pow),
               (stat, mhalf), (stat,))

        def norm_T(xb, st, P, Gc, shc_buf, shc_idx, dst, T0, bank=None):
            norm_T_a(xb, st, P)
            norm_T_b(st, P, Gc, shc_buf, shc_idx, dst, T0, bank)

        def xs_view(xs_):
            if xs_ is xsb:
                return xsb.t[:, :]
            return xs_.t[:, :, :].rearrange("p a b -> p (a b)").bitcast(F32)

        def norm_T_a(xb, st, P, use_pool=False, scol=0, xs_=None):
            xs_ = xsb if xs_ is None else xs_
            xsv = xs_view(xs_)
            xv = xb.t[0:P, st, :]
            op('act', lambda e: e.activation(xsv[0:P, :], xv, AF.Square, accum_out=stat.t[0:P, scol:scol + 1]), (xb,), (xs_, stat))
            rstd_from(scol, P, D)
            op('act', lambda e: e.activation(xsv[0:P, :], xv, AF.Copy, scale=stat.t[0:P, scol:scol + 1]), (xb, stat), (xs_,))

        def norm_T_b(st, P, Gc, shc_buf, shc_idx, dst, T0, bank=None, halves=(0, 1), part=None, xs_=None):
            xs_ = xsb if xs_ is None else xs_
            xsv = xs_view(xs_)
            for half in halves:
                pb = pnext() if bank is None else bank
                if part in (None, 'pe'):
                    for kk in range(4):
                        k = half * 4 + kk
                        op('pe', lambda e, k=k, kk=kk, pb=pb: e.transpose(pb.t[:, kk * P:(kk + 1) * P], xsv[0:P, k * 128:(k + 1) * 128], ident[0:P, 0:P]),
                           (xs_, cm), (pb,))
                if part == 'pe':
                    continue
                for kk in range(4):
                    k = half * 4 + kk
                    eng = 'act' if kk % 2 == 0 else 'dve'
                    if eng == 'act':
                        op('act', lambda e, k=k, kk=kk, pb=pb: e.activation(dst.t[:, k, T0:T0 + P], pb.t[:, kk * P:(kk + 1) * P], AF.Identity,
                                                                         scale=Gc.t[:, k:k + 1], bias=shc_buf.t[:, shc_idx, k:k + 1]),
                           (pb, Gc, shc_buf), (dst,))
                    else:
                        op('dve', lambda e, k=k, kk=kk, pb=pb: e.tensor_scalar(dst.t[:, k, T0:T0 + P], pb.t[:, kk * P:(kk + 1) * P],
                                                                            Gc.t[:, k:k + 1], shc_buf.t[:, shc_idx, k:k + 1], ALU.mult, ALU.add),
                           (pb, Gc, shc_buf), (dst,))

        def kv_project(T, wkv3, wkvb, KT_dst, V_dst_fn, nst, P):
            for hp in range(2):
                pb = pnext()
                for hh in range(2):
                    h = hp * 2 + hh
                    for kc in range(2):
                        op('pe', lambda e, h=h, hh=hh, kc=kc, pb=pb: e.matmul(pb.t[:, hh * T:(hh + 1) * T], wkv3[:, kc, h * 128:(h + 1) * 128], latT.t[:, kc, 0:T],
                                                                          start=(kc == 0), stop=(kc == 1)),
                           (wkvb, latT), (pb,))
                for hh in range(2):
                    h = hp * 2 + hh
                    dst_ap, dst_buf = KT_dst(h)
                    op('act', lambda e, hh=hh, pb=pb, dst_ap=dst_ap: e.activation(dst_ap, pb.t[:, hh * T:(hh + 1) * T], AF.Copy), (pb,), (dst_buf,))
            for st in range(nst):
                pb = pnext()
                for kc in range(2):
                    op('pe', lambda e, kc=kc, pb=pb, st=st: e.matmul(pb.t[0:P, :], latT.t[:, kc, st * P:(st + 1) * P], wkv3[:, kc, 512:1024],
                                                                 start=(kc == 0), stop=(kc == 1)),
                       (wkvb, latT), (pb,))
                dst_ap, dst_buf = V_dst_fn(st)
                op('dve', lambda e, pb=pb, dst_ap=dst_ap: e.tensor_copy(dst_ap, pb.t[0:P, :]), (pb,), (dst_buf,))

        def attend(NQ, keyblocks):
            nb = len(keyblocks)
            items = [(h, bi) for h in range(NH) for bi in range(nb)]
            sTs = {}
            ps_ = {}

            def scores(i):
                h, bi = items[i]
                KT_fn, kpe_ap, kpe_buf, V_fn, nk, c0, maskq = keyblocks[bi]
                sT = pnext()
                kt_ap, kt_buf = KT_fn(h)
                n = NQ - c0
                op('pe', lambda e: e.matmul(sT.t[0:nk, 0:n], kt_ap, qnT.t[:, h, c0:NQ], start=True, stop=False), (kt_buf, qnT), (sT,))
                op('pe', lambda e: e.matmul(sT.t[0:nk, 0:n], kpe_ap, qpeT.t[:, h, c0:NQ], start=False, stop=True), (kpe_buf, qpeT), (sT,))
                sTs[i] = sT

            def expo(i):
                h, bi = items[i]
                KT_fn, kpe_ap, kpe_buf, V_fn, nk, c0, maskq = keyblocks[bi]
                n = NQ - c0
                sT = sTs.pop(i)
                p = pT[i % len(pT)]
                op('act', lambda e: e.activation(p.t[0:nk, 0:n], sT.t[0:nk, 0:n], AF.Exp, scale=MLA_SCALE), (sT,), (p,))
                if maskq:
                    op('dve', lambda e: e.memset(p.t[64:128, 0:64], 0.0), (), (p,))
                ps_[i] = p

            def pv(i):
                h, bi = items[i]
                KT_fn, kpe_ap, kpe_buf, V_fn, nk, c0, maskq = keyblocks[bi]
                n = NQ - c0
                p = ps_.pop(i)
                oT = LB[0 + 2 * (h % 2)]
                sm = LB[1 + 2 * (h % 2)]
                v_ap, v_buf = V_fn(h)
                op('pe', lambda e: e.matmul(oT.t[:, c0:NQ], v_ap, p.t[0:nk, 0:n], start=(bi == 0), stop=(bi == nb - 1)), (v_buf, p), (oT,))
                op('pe', lambda e: e.matmul(sm.t[:, c0:NQ], ones_bf.t[0:nk, :], p.t[0:nk, 0:n], start=(bi == 0), stop=(bi == nb - 1)), (ones_bf, p), (sm,))
                if bi == nb - 1:
                    rc = tD
                    op('dve', lambda e: e.reciprocal(rc.t[:, 0:NQ], sm.t[:, 0:NQ]), (sm,), (rc,))
                    op('dve', lambda e: e.tensor_tensor(mixT.t[:, 4 + h, 0:NQ], oT.t[:, 0:NQ], rc.t[:, 0:NQ], ALU.mult), (oT, rc), (mixT,))

            NI = len(items)
            AHEAD = 3
            for i in range(min(AHEAD, NI)):
                scores(i)
            for i in range(NI):
                expo(i)
                if i + AHEAD < NI:
                    scores(i + AHEAD)
                pv(i)

        prefetched = set()

        def prefetch(seq, ti, g, bank=None, step=None):
            P = seq['P']
            nst = seq['nst']
            T = P * nst
            r = seq['r']
            xb = xbuf[g % 2]
            hT = hTs[g % 2]
            tok0 = ti * T
            pos0 = seq['pos0'] + tok0
            if step is None or step == 'dma':
                kb.dma('sp', xb.t[0:P, 0:nst, :], seq['x'][tok0:tok0 + T, :].rearrange("(n p) d -> p n d", p=P), (), tuple(xparts[g % 2]), semof=xb)
                kb.dma('sp', ropeTs.t[0:P, 0:nst, :], ropeT[pos0:pos0 + T, :].rearrange("(n p) c -> p n c", p=P), (), (ropeTs,), semof=ropeTs)
                kb.dma('sp', ropeFs.t[:, :, 0:T], ropeF[:, :, pos0:pos0 + T], (), (ropeFs,), semof=ropeFs)
            for st in range(nst):
                if step is None or step == ('a', st):
                    norm_T_a(xparts[g % 2][st], st, P, use_pool=False)
                if step is None or step == ('b', st):
                    norm_T_b(st, P, G1c[r], modc[r], 0, hT, st * P, bank)
                for hf_ in range(2):
                    for part in ('pe', 'ev'):
                        if step == ('bh', st, hf_, part):
                            norm_T_b(st, P, G1c[r], modc[r], 0, hT, st * P, bank, halves=(hf_,), part=part)
            if step is None or step == ('b', nst - 1) or step == ('bh', nst - 1, 1, 'ev'):
                prefetched.add(g)

        preloaded_w = {}
        pending_tail = []

        def run_tile(seq, ti, g, nxt=None):
            P = seq['P']
            nst = seq['nst']
            T = P * nst
            C = seq['C']
            nch = P // C
            r = seq['r']
            xb = xbuf[g % 2]
            xps = xparts[g % 2]
            hT = hTs[g % 2]
            tok0 = ti * T
            pos0 = seq['pos0'] + tok0
            last = (ti == seq['ntiles'] - 1)
            if g not in prefetched:
                prefetch(seq, ti, g)
            if g in preloaded_w:
                (wb_f, wf3), (wb_i, wi3), (wb_c, wc3), (wb_c2, wc3b) = preloaded_w.pop(g)
            else:
                wb_f, wf3 = load_win(512, 1024)
                wb_i, wi3 = load_win(1024, 1536)
                wb_c, wc3 = load_win(2048, 2432)
                wb_c2, wc3b = load_win(2432, 2752)
            lo = lato[0]
            ko = kpeo[0]

            def rotator(banks):
                cnt_ = [0]

                def nx():
                    b = banks[cnt_[0] % len(banks)]
                    cnt_[0] += 1
                    return b
                return nx

            def run_chains(chains):
                n = max(len(c) for c in chains)
                for s_ in range(n):
                    for c in chains:
                        if s_ < len(c):
                            c[s_]()

            def H_chain(st, banks):
                a, b_, c_ = tA[st], tB[st], tC[st]
                pn = rotator(banks)
                S = {}

                def s0():
                    pf = pn()
                    for k in range(KC):
                        op('pe', lambda e, k=k: e.matmul(pf.t[0:P, :], hT.t[:, k, st * P:(st + 1) * P], wf3[:, k, :], start=(k == 0), stop=(k == KC - 1)),
                           (hT, wb_f), (pf,))
                    op('act', lambda e: e.activation(a.t[0:P, :], pf.t[0:P, :], AF.Exp, scale=-1.0), (pf,), (a,))
                    pi = S['pi'] = pn()
                    for k in range(KC):
                        op('pe', lambda e, k=k: e.matmul(pi.t[0:P, :], hT.t[:, k, st * P:(st + 1) * P], wi3[:, k, :], start=(k == 0), stop=(k == KC - 1)),
                           (hT, wb_i), (pi,))

                def s1():
                    pi = S['pi']
                    op('dve', lambda e: e.tensor_tensor(b_.t[0:P, :], a.t[0:P, :], lb_bc.t[0:P, :], ALU.mult), (a, lb_bc), (b_,))
                    op('act', lambda e: e.activation(vv.t[0:P, st, :], pi.t[0:P, :], AF.Copy), (pi,), (vv,))

                def s2():
                    op('act', lambda e: e.activation(b_.t[0:P, :], b_.t[0:P, :], AF.Ln, bias=1.0), (b_,), (b_,))
                    op('act', lambda e: e.activation(a.t[0:P, :], a.t[0:P, :], AF.Ln, bias=1.0), (a,), (a,))

                def s3():
                    op('dve', lambda e: e.tensor_tensor(c_.t[0:P, :], b_.t[0:P, :], a.t[0:P, :], ALU.subtract), (a, b_), (c_,))

                def s4():
                    op('act', lambda e: e.activation(a.t[0:P, :], c_.t[0:P, :], AF.Exp), (c_,), (a,))
                    pbb = S['pbb'] = pn()
                    op('pe', lambda e: e.matmul(pbb.t[0:P, :], Umat[0:P, 0:P], c_.t[0:P, :], start=True, stop=True), (cm, c_), (pbb,))
                    pdd = S['pdd'] = pn()
                    op('pe', lambda e: e.matmul(pdd.t[0:P, :], Uxmat[0:P, 0:P], c_.t[0:P, :], start=True, stop=True), (cm, c_), (pdd,))

                def s5():
                    pbb, pdd = S['pbb'], S['pdd']
                    op('pool', lambda e: e.tensor_scalar(a.t[0:P, :], a.t[0:P, :], -1.0, 1.0, ALU.mult, ALU.add), (a,), (a,))
                    op('act', lambda e: e.activation(b_.t[0:P, :], pbb.t[0:P, :], AF.Exp, scale=-1.0), (pbb,), (b_,))
                    op('act', lambda e: e.activation(pdd.t[0:P, :], pdd.t[0:P, :], AF.Exp), (pdd,), (pdd,))

                def s6():
                    pdd = S['pdd']
                    op('dve', lambda e: e.tensor_tensor(b_.t[0:P, :], b_.t[0:P, :], a.t[0:P, :], ALU.mult), (a, b_), (b_,))
                    op('dve', lambda e: e.tensor_tensor(khat.t[0:P, st, :], pdd.t[0:P, :], a.t[0:P, :], ALU.mult), (a, pdd), (khat,))
                    pbt = S['pbt'] = pn()
                    for h in range(NH):
                        op('pe', lambda e, h=h: e.matmul(pbt.t[:, h * P:(h + 1) * P], c_.t[0:P, h * 128:(h + 1) * 128], Umat[0:P, 0:P], start=True, stop=True),
                           (c_, cm), (pbt,))

                def s7():
                    pbt = S['pbt']
                    op('act', lambda e: e.activation(eb.t[:, :, st * P:(st + 1) * P], pbt.t[:, 0:NH * P].rearrange("p (h t) -> p h t", h=NH), AF.Exp),
                       (pbt,), (eb,))
                    pkt = S['pkt'] = pn()
                    for h in range(NH):
                        op('pe', lambda e, h=h: e.transpose(pkt.t[:, h * P:(h + 1) * P], b_.t[0:P, h * 128:(h + 1) * 128], ident[0:P, 0:P]),
                           (b_, cm), (pkt,))

                def s8():
                    pkt = S['pkt']
                    op('dve', lambda e: e.tensor_copy(ktT.t[:, :, st * P:(st + 1) * P], pkt.t[:, 0:NH * P].rearrange("p (h t) -> p h t", h=NH)),
                       (pkt,), (ktT,))

                return [s0, s1, s2, s3, s4, s5, s6, s7, s8]

            def M_chain(st, banks, xs_buf, c1, c2):
                pn = rotator(banks)
                S = {}
                tcs = tC[st]
                tc_, ts_ = cc[0], cc[1]

                def m0():
                    pc1 = S['pc1'] = pn()
                    for k in range(KC):
                        op('pe', lambda e, k=k: e.matmul(pc1.t[0:P, 0:384], hT.t[:, k, st * P:(st + 1) * P], wc3[:, k, 0:384], start=(k == 0), stop=(k == KC - 1)),
                           (hT, wb_c), (pc1,))
                    op('act', lambda e: e.activation(sqb.t[0:P, :, :].rearrange("p h t -> p (h t)")[:, 0:384], pc1.t[0:P, 0:384], AF.Square, accum_out=stat.t[0:P, c1:c1 + 1]),
                       (pc1,), (sqb, stat))
                    pc2 = S['pc2'] = pn()
                    for k in range(KC):
                        op('pe', lambda e, k=k: e.matmul(pc2.t[0:P, 0:320], hT.t[:, k, st * P:(st + 1) * P], wc3b[:, k, 0:320], start=(k == 0), stop=(k == KC - 1)),
                           (hT, wb_c2), (pc2,))

                def m1():
                    rstd_from(c1, P, 384)

                def m2():
                    pc1 = S['pc1']
                    op('act', lambda e: e.activation(xs_buf.t[0:P, 0:384], pc1.t[0:P, 0:384], AF.Copy, scale=stat.t[0:P, c1:c1 + 1]), (pc1, stat), (xs_buf,))
                    pc2 = S['pc2']
                    op('act', lambda e: e.activation(sqb.t[0:P, :, :].rearrange("p h t -> p (h t)")[:, 0:256], pc2.t[0:P, 0:256], AF.Square, accum_out=stat.t[0:P, c2:c2 + 1]),
                       (pc2,), (sqb, stat))

                def m3():
                    rstd_from(c2, P, 256)
                    pt = S['pt'] = pn()
                    for kc in range(3):
                        op('pe', lambda e, kc=kc: e.transpose(pt.t[:, kc * P:(kc + 1) * P], xs_buf.t[0:P, kc * 128:(kc + 1) * 128], ident[0:P, 0:P]), (xs_buf, cm), (pt,))

                def m4():
                    pt, pc2 = S['pt'], S['pc2']
                    for kc in range(3):
                        op('act', lambda e, kc=kc: e.activation(cqnT.t[:, kc, st * P:(st + 1) * P], pt.t[:, kc * P:(kc + 1) * P], AF.Copy, scale=colsb.t[:, QN + kc:QN + kc + 1]),
                           (pt, colsb), (cqnT,))
                    op('dve', lambda e: e.scalar_tensor_tensor(lo.t[0:P, st, :], pc2.t[0:P, 0:256], stat.t[0:P, c2:c2 + 1], kvn_bc.t[0:P, :], ALU.mult, ALU.mult),
                       (pc2, stat, kvn_bc), (lo,))
                    o0 = st * 128
                    op('dve', lambda e: e.tensor_tensor(tc_.t[0:P, o0:o0 + 64], pc2.t[0:P, 256:320], ropeTs.t[0:P, st, 0:64], ALU.mult), (pc2, ropeTs), (tc_,))
                    op('dve', lambda e: e.tensor_tensor(ts_.t[0:P, o0:o0 + 64], pc2.t[0:P, 256:320], ropeTs.t[0:P, st, 64:128], ALU.mult), (pc2, ropeTs), (ts_,))

                def m5():
                    o0 = st * 128
                    pt2 = S['pt2'] = pn()
                    for kc in range(2):
                        op('pe', lambda e, kc=kc: e.transpose(pt2.t[:, kc * P:(kc + 1) * P], lo.t[0:P, st, kc * 128:(kc + 1) * 128], ident[0:P, 0:P]), (lo, cm), (pt2,))
                    op('pool', lambda e: e.tensor_tensor(ko.t[0:P, st, 0:32], tc_.t[0:P, o0:o0 + 32], ts_.t[0:P, o0 + 32:o0 + 64], ALU.subtract), (tc_, ts_), (ko,))
                    op('pool', lambda e: e.tensor_tensor(ko.t[0:P, st, 32:64], tc_.t[0:P, o0 + 32:o0 + 64], ts_.t[0:P, o0:o0 + 32], ALU.add), (tc_, ts_), (ko,))

                def m6():
                    pt2 = S['pt2']
                    op('act', lambda e: e.activation(latT.t[:, :, st * P:(st + 1) * P], pt2.t[:, 0:2 * P].rearrange("p (k t) -> p k t", k=2), AF.Copy), (pt2,), (latT,))
                    pt3 = S['pt3'] = pn()
                    op('pe', lambda e: e.transpose(pt3.t[0:64, 0:P], ko.t[0:P, st, :], ident[0:P, 0:P]), (ko, cm), (pt3,))

                def m7():
                    pt3 = S['pt3']
                    kdst_ap, kdst_buf = seq['kpe_dst'](tok0 + st * P, P)
                    op('act', lambda e: e.activation(kdst_ap, pt3.t[0:64, 0:P], AF.Copy), (pt3,), (kdst_buf,))

                return [m0, m1, m2, m3, m4, m5, m6, m7]

            chains = []
            ptail = pending_tail.pop() if pending_tail else None
            if ptail is not None:
                chains.append([lambda: None, ptail])
            for st in range(nst):
                chains.append(H_chain(st, [PSB[2 * st], PSB[2 * st + 1]]))
                mch = M_chain(st, [PSB[4 + 2 * st], PSB[5 + 2 * st]], xsb if st == 0 else tD, 1 + 3 * st, 2 + 3 * st)
                if ptail is not None:
                    mch = [lambda: None] + mch
                chains.append(mch)
            run_chains(chains)
            kb.dma('sp', seq['lat_out'][tok0:tok0 + T, :].rearrange("(n p) c -> p n c", p=P), lo.t[0:P, 0:nst, :], (lo,), (), semof=lo, store=True)
            kb.dma('sp', seq['kpe_out'][tok0:tok0 + T, :].rearrange("(n p) c -> p n c", p=P), ko.t[0:P, 0:nst, :], (ko,), (), semof=ko, store=True)

            wb_q, wq3 = load_win(0, 512)
            RB = rotator([PSB[2], PSB[3]])
            for hp in range(2):
                pq = RB()
                for hh in range(2):
                    h = hp * 2 + hh
                    for k in range(KC):
                        op('pe', lambda e, k=k, h=h, hh=hh: e.matmul(pq.t[:, hh * T:(hh + 1) * T], wq3[:, k, h * 128:(h + 1) * 128], hT.t[:, k, 0:T],
                                                                  start=(k == 0), stop=(k == KC - 1)),
                           (hT, wb_q), (pq,))
                op('dve', lambda e: e.scalar_tensor_tensor(qtT.t[:, hp * 2:hp * 2 + 2, 0:T], pq.t[:, 0:2 * T].rearrange("p (h t) -> p h t", h=2),
                                                           float(128 ** -0.5), eb.t[:, hp * 2:hp * 2 + 2, 0:T], ALU.mult, ALU.mult),
                   (pq, eb), (qtT,))

            wb_g, wg3 = load_win(1536, 2048)
            wb_u = wload(wuq_b, 3072, wcvC)
            wu3 = wb_u.t[:, 0:3072].rearrange("p (k c) -> p k c", k=3)
            wb_kv = wload(wukv_b, 2048, wcvA)
            wkv3 = wb_kv.t[:, 0:2048].rearrange("p (k c) -> p k c", k=2)
            oTb = [PSB[0], PSB[1]]
            sgX = [tA[0], tA[1]]
            rsX = [tC[0], tC[1]]

            def SEQ_chain():
                stages = []
                for st in range(nst):
                    def q0(st=st):
                        psc = PSB[2]
                        for h in range(NH):
                            op('pe', lambda e, h=h: e.matmul(psc.t[0:P, h * P:(h + 1) * P], ktT.t[:, h, st * P:(st + 1) * P], qtT.t[:, h, st * P:(st + 1) * P],
                                                             start=True, stop=True), (ktT, qtT), (psc,))
                        op('dve', lambda e: e.tensor_tensor(scm.t[0:P, :, 0:P], psc.t[0:P, 0:NH * P].rearrange("p (h t) -> p h t", h=NH), maskU4.t[0:P, :, 0:P], ALU.mult),
                           (psc, maskU4), (scm,))
                    stages.append(q0)
                    for ch in range(nch):
                        def cA(st=st, ch=ch):
                            r0 = ch * C
                            t0 = st * P + ch * C
                            pkv = PSB[3]
                            for h in range(NH):
                                op('pe', lambda e, h=h: e.matmul(pkv.t[:, h * 128:(h + 1) * 128], khat.t[r0:r0 + C, st, h * 128:(h + 1) * 128], vv.t[r0:r0 + C, st, h * 128:(h + 1) * 128],
                                                                 start=True, stop=True), (khat, vv), (pkv,))
                            for h in range(NH):
                                ob = oTb[h // 2]
                                oc = (h % 2) * T + t0
                                op('pe', lambda e, h=h, ob=ob, oc=oc: e.matmul(ob.t[:, oc:oc + C], Sbf.t[:, h, :], qtT.t[:, h, t0:t0 + C], start=True, stop=False),
                                   (Sbf, qtT), (ob,))
                                op('pe', lambda e, h=h, ob=ob, oc=oc: e.matmul(ob.t[:, oc:oc + C], vv.t[r0:r0 + C, st, h * 128:(h + 1) * 128], scm.t[r0:r0 + C, h, r0:r0 + C],
                                                                            start=False, stop=True), (vv, scm), (ob,))
                            dec = eb.t[:, :, t0 + C - 1:t0 + C].to_broadcast([128, NH, 128])
                            op('dve', lambda e: e.tensor_tensor(Sst.t[:], Sst.t[:], dec, ALU.mult), (Sst, eb), (Sst,))

                        def cB(st=st, ch=ch):
                            pkv = PSB[3]
                            pk3 = pkv.t[:].rearrange("p (h v) -> p h v", h=NH)
                            op('dve', lambda e: e.tensor_tensor(Sbf.t[:], Sst.t[:], pk3, ALU.add), (Sst, pkv), (Sbf,))
                            op('dve', lambda e: e.tensor_tensor(Sst.t[:], Sst.t[:], pk3, ALU.add), (Sst, pkv), (Sst,))
                        stages += [cA, cB]
                return stages

            def G_chain():
                stages = []
                for hp in range(2):
                    sgb = sgX[hp]
                    sg3 = sgb.t[:, 0:2 * T].rearrange("p (h t) -> p h t", h=2)

                    def g0(hp=hp, sgb=sgb, sg3=sg3):
                        pg = PSB[4]
                        for hh in range(2):
                            h = hp * 2 + hh
                            for k in range(KC):
                                op('pe', lambda e, k=k, h=h, hh=hh: e.matmul(pg.t[:, hh * T:(hh + 1) * T], wg3[:, k, h * 128:(h + 1) * 128], hT.t[:, k, 0:T],
                                                                          start=(k == 0), stop=(k == KC - 1)),
                                   (hT, wb_g), (pg,))
                        pg3 = pg.t[:, 0:2 * T].rearrange("p (h t) -> p h t", h=2)
                        op('act', lambda e: e.activation(sg3, pg3, AF.Tanh, scale=0.5), (pg,), (sgb,))

                    def g1(hp=hp, sgb=sgb, sg3=sg3):
                        pg = PSB[4]
                        pg3 = pg.t[:, 0:2 * T].rearrange("p (h t) -> p h t", h=2)
                        op('dve', lambda e: e.scalar_tensor_tensor(sg3, sg3, 1.0, pg3, ALU.add, ALU.mult), (sgb, pg), (sgb,))
                    stages += [g0, g1]
                return stages

            def Q_chain():
                stages = []
                S = {}
                QB = rotator([PSB[5], PSB[6]])
                for hp in range(2):
                    def q0(hp=hp):
                        pq = QB()
                        for hh in range(2):
                            h = hp * 2 + hh
                            for kc in range(3):
                                op('pe', lambda e, kc=kc, h=h, hh=hh: e.matmul(pq.t[:, hh * T:(hh + 1) * T], wu3[:, kc, h * 192:h * 192 + 128], cqnT.t[:, kc, 0:T],
                                                                            start=(kc == 0), stop=(kc == 2)), (wb_u, cqnT), (pq,))
                        op('act', lambda e: e.activation(qnT.t[:, hp * 2:hp * 2 + 2, 0:T], pq.t[:, 0:2 * T].rearrange("p (h t) -> p h t", h=2), AF.Copy), (pq,), (qnT,))

                    def q1(hp=hp):
                        pp = S['pp'] = QB()
                        for hh in range(2):
                            h = hp * 2 + hh
                            for kc in range(3):
                                op('pe', lambda e, kc=kc, h=h, hh=hh: e.matmul(pp.t[0:64, hh * T:(hh + 1) * T], wu3[:, kc, h * 192 + 128:h * 192 + 192], cqnT.t[:, kc, 0:T],
                                                                            start=(kc == 0), stop=(kc == 2)), (wb_u, cqnT), (pp,))

                    def q2(hp=hp):
                        ps_ = S['ps'] = QB()
                        for hh in range(2):
                            h = hp * 2 + hh
                            for kc in range(3):
                                op('pe', lambda e, kc=kc, h=h, hh=hh: e.matmul(ps_.t[0:64, hh * T:(hh + 1) * T], wu3[:, kc, 768 + h * 64:768 + (h + 1) * 64], cqnT.t[:, kc, 0:T],
                                                                            start=(kc == 0), stop=(kc == 2)), (wb_u, cqnT), (ps_,))

                    def q3(hp=hp):
                        pp, ps_ = S['pp'], S['ps']
                        op('act', lambda e: e.activation(tB[0].t[0:64, 0:2 * T], pp.t[0:64, 0:2 * T], AF.Copy), (pp,), (tB[0],))
                        op('act', lambda e: e.activation(tB[1].t[0:64, 0:2 * T], ps_.t[0:64, 0:2 * T], AF.Copy), (ps_,), (tB[1],))

                    def q4(hp=hp):
                        cosb = ropeFs.t[:, 0:1, 0:T].to_broadcast([64, 2, T])
                        sinb = ropeFs.t[:, 1:2, 0:T].to_broadcast([64, 2, T])
                        t0v = tB[0].t[0:64, 0:2 * T].rearrange("p (h t) -> p h t", h=2)
                        t1v = tB[1].t[0:64, 0:2 * T].rearrange("p (h t) -> p h t", h=2)
                        op('pool', lambda e: e.tensor_tensor(t0v, t0v, cosb, ALU.mult), (tB[0], ropeFs), (tB[0],))
                        op('pool', lambda e: e.tensor_tensor(t1v, t1v, sinb, ALU.mult), (tB[1], ropeFs), (tB[1],))
                        op('pool', lambda e: e.tensor_tensor(qpeT.t[0:64, hp * 2:hp * 2 + 2, 0:T], t0v, t1v, ALU.add), (tB[0], tB[1]), (qpeT,))
                    stages += [q0, q1, q2, q3, q4]
                return stages

            def KV_chain():
                stages = []
                pb = PSB[7]
                for hp in range(2):
                    def k0(hp=hp):
                        for hh in range(2):
                            h = hp * 2 + hh
                            for kc in range(2):
                                op('pe', lambda e, h=h, hh=hh, kc=kc: e.matmul(pb.t[:, hh * T:(hh + 1) * T], wkv3[:, kc, h * 128:(h + 1) * 128], latT.t[:, kc, 0:T],
                                                                            start=(kc == 0), stop=(kc == 1)),
                                   (wb_kv, latT), (pb,))
                        for hh in range(2):
                            h = hp * 2 + hh
                            dst_ap, dst_buf = seq['KT_dst'](h, tok0, T)
                            op('act', lambda e, hh=hh, dst_ap=dst_ap: e.activation(dst_ap, pb.t[:, hh * T:(hh + 1) * T], AF.Copy), (pb,), (dst_buf,))
                    stages.append(k0)
                for st in range(nst):
                    def v0(st=st):
                        for kc in range(2):
                            op('pe', lambda e, kc=kc: e.matmul(pb.t[0:P, :], latT.t[:, kc, st * P:(st + 1) * P], wkv3[:, kc, 512:1024],
                                                             start=(kc == 0), stop=(kc == 1)),
                               (wb_kv, latT), (pb,))
                        dst_ap, dst_buf = seq['V_dst'](tok0, st, P)
                        op('dve', lambda e: e.tensor_copy(dst_ap, pb.t[0:P, :]), (pb,), (dst_buf,))
                    stages.append(v0)
                return stages

            run_chains([SEQ_chain(), G_chain(), Q_chain(), KV_chain()])
            if last:
                kb.dma('sp', seq['hg_out'].rearrange("h k v -> k h v"), Sst.t[:], (Sst,), (), semof=Sst, store=True)

            def R_chain(hp):
                ob = oTb[hp]
                ob3 = ob.t[:, 0:2 * T].rearrange("p (h t) -> p h t", h=2)
                sgb = sgX[hp]
                sg3 = sgb.t[:, 0:2 * T].rearrange("p (h t) -> p h t", h=2)
                rsb = rsX[hp]
                rs3 = rsb.t[:, 0:2 * T].rearrange("p (h t) -> p h t", h=2)
                sq3 = sqb.t[:, :, 0:T] if hp == 0 else khat.t[:, 0, 0:2 * T].rearrange("p (h t) -> p h t", h=2)
                sqbuf = sqb if hp == 0 else khat
                sq2 = sq3
                S = {}

                def r0():
                    op('act', lambda e: e.activation(sq3, ob3, AF.Square), (ob,), (sqbuf,))
                    pm = S['pm'] = RB()
                    op('pe', lambda e: e.matmul(pm.t[:, 0:2 * T], mean_bf.t[:], sq2, start=True, stop=True), (mean_bf, sqbuf), (pm,))

                def r1():
                    pm = S['pm']
                    pm3 = pm.t[:, 0:2 * T].rearrange("p (h t) -> p h t", h=2)
                    op('act', lambda e: e.activation(rs3, pm3, AF.Ln, bias=EPS), (pm,), (rsb,))

                def r2():
                    op('act', lambda e: e.activation(rs3, rs3, AF.Exp, scale=-0.5), (rsb,), (rsb,))

                def r3():
                    op('dve', lambda e: e.scalar_tensor_tensor(rs3, rs3, 0.5, ob3, ALU.mult, ALU.mult), (rsb, ob), (rsb,))

                def r4():
                    op('dve', lambda e: e.scalar_tensor_tensor(mixT.t[:, hp * 2:hp * 2 + 2, 0:T], rs3, colsb.t[:, HGG:HGG + 1], sg3, ALU.mult, ALU.mult),
                       (rsb, sgb, colsb), (mixT,))
                return [r0, r1, r2, r3, r4]

            run_chains([R_chain(0), R_chain(1)])
            attend(T, seq['keyblocks'](ti))
            wbs = {}

            wffn3 = wffn_b.rearrange("(j p) c -> p j c", p=128)
            dnbufs = [tA[0], tA[1], tC[0], tC[1], tD]
            wdns = {}

            def ffn_load_up(q):
                wb = wnext()
                kb.dma('sp', wb.t[:, 0:4096].rearrange("p (j c) -> p j c", j=2), wffn3[:, 2 * q:2 * q + 2, 0:2048], (wcvB,), (wb,), semof=wb)
                wbs[2 * q] = (wb, wb.t[:, 0:2048])
                wbs[2 * q + 1] = (wb, wb.t[:, 2048:4096])

            def ffn_load_dn(j):
                db = dnbufs[j % 5]
                dv = db.t[:, :].bitcast(BF16)
                kb.dma('sp', dv, wffn_b[j * 128:(j + 1) * 128, 2048:3072], (wcvB,), (db,), semof=db)
                wdns[j] = (db, dv)

            wos = []
            for half in range(2):
                wb_o = wload(wout_b[:, half * 4096:(half + 1) * 4096], 4096, wcvC)
                wos.append((wb_o, wb_o.t[:, 0:4096].rearrange("p (k c) -> p k c", k=KC)))
            ffn_load_up(0)
            ffn_load_up(1)
            for j in range(5):
                ffn_load_dn(j)
            for st in range(nst):
                for half in range(2):
                    wb_o, wo3 = wos[half]
                    po = pnext()
                    for c in range(KC):
                        op('pe', lambda e, c=c: e.matmul(po.t[0:P, :], mixT.t[:, c, st * P:(st + 1) * P], wo3[:, c, :], start=(c == 0), stop=(c == KC - 1)),
                           (mixT, wb_o), (po,))
                    tt = tB[st % 2]
                    op('dve', lambda e: e.tensor_tensor(tt.t[0:P, :], po.t[0:P, :], g1_bc.t[0:P, half * 512:(half + 1) * 512], ALU.mult), (po, g1_bc), (tt,))
                    op('pool', lambda e: e.tensor_tensor(xb.t[0:P, st, half * 512:(half + 1) * 512], xb.t[0:P, st, half * 512:(half + 1) * 512], tt.t[0:P, :], ALU.add),
                       (xps[st], tt), (xps[st],))
            for st in range(nst):
                norm_T_a(xps[st], st, P, scol=4 + st, xs_=(xsb if st == 0 else mixT))
            for st in range(nst):
                norm_T_b(st, P, G2c[r], modc[r], 2, hT, st * P, xs_=(xsb if st == 0 else mixT))
            acc = [[LB[st * 2 + half] for half in range(2)] for st in range(nst)]
            pavs = {}

            def ffn_up(j):
                wb, wv_ = wbs[j]
                wup = wv_.rearrange("p (k a c) -> p k a c", k=KC, a=2)
                pav = bank_of[('c', j)]
                for a_ in range(2):
                    for k in range(KC):
                        op('pe', lambda e, k=k, a_=a_: e.matmul(pav.t[:, a_ * T:(a_ + 1) * T], wup[:, k, a_, :], hT.t[:, k, 0:T], start=(k == 0), stop=(k == KC - 1)),
                           (wb, hT), (pav,))
                pavs[j] = pav

            def ffn_elem(j):
                pav = pavs[j]
                as_ = aS[j % 2]
                c1 = cc[j % 2]
                u = uT[j % 3]
                cwb = CW + j * 3
                op('pool', lambda e: e.tensor_copy(as_.t[:, 0:2], aprev.t[:, j, :]), (aprev,), (as_,))
                op('act', lambda e: e.activation(as_.t[:, 2:T + 2], pav.t[:, 0:T], AF.Copy), (pav,), (as_,))
                op('act', lambda e: e.activation(c1.t[:, 0:T], pav.t[:, 0:T], AF.Identity, scale=colsb.t[:, cwb + 2:cwb + 3], bias=colsb.t[:, CB + j:CB + j + 1]),
                   (pav, colsb), (c1,))
                op('pool', lambda e: e.tensor_copy(aprev.t[:, j, :], as_.t[:, T:T + 2]), (as_,), (aprev,))
                op('dve', lambda e: e.scalar_tensor_tensor(c1.t[:, 0:T], as_.t[:, 1:T + 1], colsb.t[:, cwb + 1:cwb + 2], c1.t[:, 0:T], ALU.mult, ALU.add),
                   (as_, colsb, c1), (c1,))
                op('dve', lambda e: e.scalar_tensor_tensor(c1.t[:, 0:T], as_.t[:, 0:T], colsb.t[:, cwb:cwb + 1], c1.t[:, 0:T], ALU.mult, ALU.add),
                   (as_, colsb, c1), (c1,))
                op('act', lambda e: e.activation(c1.t[:, 0:T], c1.t[:, 0:T], AF.Gelu), (c1,), (c1,))
                op('dve', lambda e: e.tensor_tensor(u.t[:, 0:T], c1.t[:, 0:T], pav.t[:, T:2 * T], ALU.mult), (c1, pav), (u,))

            def ffn_down(j):
                wbs.pop(j)
                wb, wdn = wdns.pop(j)
                u = uT[j % 3]
                for st in range(nst):
                    for half in range(2):
                        ab = acc[st][half]
                        op('pe', lambda e, st=st, half=half, ab=ab: e.matmul(ab.t[0:P, :], u.t[:, st * P:(st + 1) * P], wdn[:, half * 512:(half + 1) * 512],
                                                                          start=(j == 0), stop=(j == NJ - 1)), (u, wb), (ab,))

            ffn_load_up(2)
            ffn_load_up(3)
            items = [('c', j) for j in range(NJ)]
            post = {}
            if nxt is not None:
                nn = nxt[0]['nst']
                ins_at = [(8, 0)] + ([(16, 1)] if nn > 1 else [])
                for (cj, st_) in reversed(ins_at):
                    pos = items.index(('c', cj))
                    items[pos:pos] = [('p', st_, 0), ('p', st_, 1)]
                post[('c', 1)] = ('a', 0)
                if nn > 1:
                    post[('c', 8)] = ('a', 1)
                prefetch(nxt[0], nxt[1], g + 1, None, 'dma')
            bank_of = {it: PSB[i % 4] for i, it in enumerate(items)}
            AH = 3

            def stage1(it):
                if it[0] == 'c':
                    ffn_up(it[1])
                else:
                    prefetch(nxt[0], nxt[1], g + 1, bank_of[it], ('bh', it[1], it[2], 'pe'))

            def stage2(it):
                if it[0] == 'c':
                    ffn_elem(it[1])
                else:
                    prefetch(nxt[0], nxt[1], g + 1, bank_of[it], ('bh', it[1], it[2], 'ev'))

            for i in range(min(AH, len(items))):
                stage1(items[i])
            for i, it in enumerate(items):
                if it[0] == 'c':
                    j = it[1]
                    if j % 2 == 0 and j // 2 + 4 < NJ // 2:
                        ffn_load_up(j // 2 + 4)
                stage2(it)
                if it in post:
                    prefetch(nxt[0], nxt[1], g + 1, None, post[it])
                if i + AH < len(items):
                    stage1(items[i + AH])
                if it[0] == 'c':
                    ffn_down(it[1])
                    if it[1] + 5 < NJ:
                        ffn_load_dn(it[1] + 5)
            if nxt is not None:
                preloaded_w[g + 1] = (load_win(512, 1024), load_win(1024, 1536), load_win(2048, 2432), load_win(2432, 2752))
            def tail():
                if last:
                    kb.dma('sp', seq['cv_out'], aprev.t[:], (aprev,), (), semof=aprev, store=True)
                for st in range(nst):
                    for half in range(2):
                        ab = acc[st][half]
                        tt = tB[(st * 2 + half) % 2]
                        op('dve', lambda e: e.tensor_tensor(tt.t[0:P, :], ab.t[0:P, :], g2_bc.t[0:P, half * 512:(half + 1) * 512], ALU.mult), (ab, g2_bc), (tt,))
                        op('pool', lambda e: e.tensor_tensor(xb.t[0:P, st, half * 512:(half + 1) * 512], xb.t[0:P, st, half * 512:(half + 1) * 512], tt.t[0:P, :], ALU.add),
                           (xps[st], tt), (xps[st],))
                for st in range(nst):
                    xv = xb.t[0:P, st, :]
                    op('act', lambda e: e.activation(xsb.t[0:P, :], xv, AF.Square, accum_out=stat.t[0:P, 3:4]), (xps[st],), (xsb, stat))
                    rstd_from(3, P, D)
                    op('dve', lambda e: e.scalar_tensor_tensor(xv, xv, stat.t[0:P, 3:4], fg_bc.t[0:P, :], ALU.mult, ALU.mult), (xps[st], stat, fg_bc), (xps[st],))
                kb.dma('sp', seq['y_out'][tok0:tok0 + T, :].rearrange("(n p) d -> p n d", p=P), xb.t[0:P, 0:nst, :], tuple(xps[0:nst]), (), semof=xb, store=True)

            if (not last) and P == 128:
                pending_tail.append(tail)
            else:
                tail()

        kb.dma('sp', mixT.t[:, :, :].rearrange("p a b -> p (a b)"), wukv_b, (wcv0,), (mixT,), semof=mixT)
        wkv3c = mixT.t[:, :, :].rearrange("p a b -> p (a b)").rearrange("p (k c) -> p k c", k=2)
        msi = 0
        for g in range(PAST // TP):
            for _ in range(3):
                if msi < len(mod_steps):
                    mod_steps[msi]()
                    msi += 1
            cl = xparts[g % 2][0]
            clv = cl.t[:, 0, 0:640].rearrange("p (n c) -> p n c", n=2)
            kb.dma('sp', clv[:, :, 0:256], clat[g * TP:(g + 1) * TP, :].rearrange("(n p) c -> p n c", p=128), (), (cl,), semof=xbuf[g % 2])
            kb.dma('sp', clv[:, :, 256:320], ckpe[g * TP:(g + 1) * TP, :].rearrange("(n p) c -> p n c", p=128), (), (cl,), semof=xbuf[g % 2])
            for st in range(NST):
                pt2 = pnext()
                for kc in range(2):
                    op('pe', lambda e, kc=kc: e.transpose(pt2.t[:, kc * 128:(kc + 1) * 128], clv[:, st, kc * 128:(kc + 1) * 128], ident), (cl, cm), (pt2,))
                op('act', lambda e: e.activation(latT.t[:, :, st * 128:(st + 1) * 128], pt2.t[:, 0:256].rearrange("p (k t) -> p k t", k=2), AF.Copy), (pt2,), (latT,))
                pt3 = pnext()
                op('pe', lambda e: e.transpose(pt3.t[0:64, 0:128], clv[:, st, 256:320], ident), (cl, cm), (pt3,))
                c0 = g * TP + st * 128
                op('dve', lambda e: e.tensor_copy(kpeT.t[0:64, c0:c0 + 128], pt3.t[0:64, 0:128]), (pt3,), (kpeT,))
            kv_project(TP, wkv3c, mixT, lambda h, g=g: (KT.t[:, h, g * TP:(g + 1) * TP], KT),
                       lambda st, g=g: (Vr.t[:, g * NST + st, :], Vr), NST, 128)
        while msi < len(mod_steps):
            mod_steps[msi]()
            msi += 1
        mod_finish()
        kb.dma('pool', wuq_b.rearrange("p (a b) -> (p a) b", b=1024), w_uq_l.rearrange("p k c -> (p k) c"), reads=(modv,), semof=wcvC, nowait_w=True)
        kb.dma('pool', wout_b.rearrange("p (a b) -> (p a) b", b=2048), w_out_l.rearrange("p h k c -> p (h k c)").rearrange("p (a b) -> (p a) b", b=2048),
               reads=(modv,), semof=wcvC, nowait_w=True)
        wcvC.w = ('ls_wcvC', wcvC.lsem, wcvC.lcnt, 'dma')
        wsrc = w_ffn_l.rearrange("j p c -> (j p) c")
        RH = NJ * 128 // 2
        for r0 in range(0, NJ * 128, RH):
            kb.dma('pool', wffn_b[r0:r0 + RH, 0:2048], wsrc[r0:r0 + RH, 0:2048], reads=(modv,), semof=wcvB, nowait_w=True)
            kb.dma('pool', wffn_b[r0:r0 + RH, 2048:3072], wsrc[r0:r0 + RH, 2048:3072], reads=(modv,), semof=wcvB, nowait_w=True)
        wcvB.w = ('ls_wcvB', wcvB.lsem, wcvB.lcnt, 'dma')


        def full_block(kt):
            return (lambda h: (KT.t[:, h, kt * 128:(kt + 1) * 128], KT), kpeT.t[:, kt * 128:(kt + 1) * 128], kpeT,
                    lambda h: (Vr.t[:, kt, h * 128:(h + 1) * 128], Vr), 128)

        def sample_blocks(ti):
            bl = []
            for kt in range(PAST // 128):
                bl.append(full_block(kt) + (0, False))
            bl.append((lambda h: (KTs.t[:, h, :], KTs), kpeTs.t[:, :], kpeTs, lambda h: (Vs.t[:, h * 128:(h + 1) * 128], Vs), S_S, 0, False))
            return bl

        seq_s = dict(P=S_S, nst=1, C=S_S, r=1, ntiles=1, pos0=PAST, x=xs_in, y_out=y_s, lat_out=lat_s, kpe_out=kpe_s, hg_out=hg_s, cv_out=cv_s,
                     kpe_dst=lambda t0, P: (kpeTs.t[0:64, t0:t0 + P], kpeTs),
                     KT_dst=lambda h, t0, T: (KTs.t[:, h, t0:t0 + T], KTs),
                     V_dst=lambda t0, st, P: (Vs.t[0:P, :], Vs),
                     keyblocks=sample_blocks)
        def prompt_blocks(ti):
            bl = []
            for kt in range(ti * NST):
                bl.append(full_block(kt) + (0, False))
            for j in range(NST):
                bl.append(full_block(ti * NST + j) + (j * 128, True))
            return bl

        seq_p = dict(P=128, nst=NST, C=64, r=0, ntiles=S_P // TP, pos0=0, x=xp, y_out=y_p, lat_out=lat_p, kpe_out=kpe_p, hg_out=hg_p, cv_out=cv_p,
                     kpe_dst=lambda t0, P: (kpeT.t[0:64, t0:t0 + P], kpeT),
                     KT_dst=lambda h, t0, T: (KT.t[:, h, t0:t0 + T], KT),
                     V_dst=lambda t0, st, P: (Vr.t[:, t0 // 128 + st, :], Vr),
                     keyblocks=prompt_blocks)
        kb.dma('sp', Sst.t[:], shg.rearrange("h k v -> k h v"), (), (Sst,), semof=Sst)
        op('act', lambda e: e.activation(Sbf.t[:], Sst.t[:], AF.Copy), (Sst,), (Sbf,))
        kb.dma('sp', aprev.t[:], sconv, (), (aprev,), semof=aprev)
        load_gbc(1, S_S)
        run_tile(seq_s, 0, 0, (seq_p, 0))

        op('pool', lambda e: e.memset(Sst.t[:], 0.0), (), (Sst,))
        op('pool', lambda e: e.memset(Sbf.t[:], 0.0), (), (Sbf,))
        op('pool', lambda e: e.memset(aprev.t[:], 0.0), (), (aprev,))
        load_gbc(0, 128)

        for ti in range(seq_p['ntiles']):
            run_tile(seq_p, ti, 1 + ti, (seq_p, ti + 1) if ti + 1 < seq_p['ntiles'] else None)
        kb.finish()
        build_nc.stats = (kb.ninst, kb.nsem, dict(kb.cnt))
        build_nc.used = kb.used
        build_nc.sbuf_left = nc.sbuf_bytes_remaining
    return nc


def _rope_tables():
    half = 32
    inv_freq = (np.float32(10000.0) ** (-np.arange(half, dtype=np.float32) / np.float32(half))).astype(np.float32)
    pos = np.arange(S_P + S_S, dtype=np.float32)
    ang = (pos[:, None] * inv_freq[None, :]).astype(np.float32)
    cos = np.cos(ang).astype(np.float32)
    sin = np.sin(ang).astype(np.float32)
    ropeT = np.concatenate([cos, cos, sin, sin], axis=1).astype(np.float32)
    ropeF = np.stack([np.concatenate([cos.T, cos.T], 0), np.concatenate([-sin.T, sin.T], 0)], axis=1)
    return np.ascontiguousarray(ropeT), np.ascontiguousarray(ropeF.astype(np.float32))


def _const_mats():
    idx = np.arange(128)
    same = (idx[:, None] // 64) == (idx[None, :] // 64)
    U = (same & (idx[:, None] <= idx[None, :])).astype(np.float32)
    Ux = (same & (idx[:, None] > idx[None, :])).astype(np.float32)
    return np.ascontiguousarray(np.stack([np.eye(128, dtype=np.float32), U, Ux], axis=1))


def kernel(x_prompt, x_sample, c_prompt, c_sample, cache_kv_latent, cache_k_rope, state_hgrn,
           state_ffn_conv, w_ada, b_ada, norm_mix_gain, w_in, hg_lb_logits, hg_norm_gain,
           mla_q_norm_gain, mla_kv_norm_gain, w_uq, w_uk, w_uv, w_out, norm_ffn_gain, w_up,
           conv_w, conv_b, w_down, final_norm_gain):
    f = lambda a: np.ascontiguousarray(np.asarray(a, dtype=np.float32))
    x_prompt, x_sample, c_prompt, c_sample = f(x_prompt), f(x_sample), f(c_prompt), f(c_sample)
    n = 8
    ropeT, ropeF = _rope_tables()
    cmat = _const_mats()
    cols = np.zeros((128, 128), np.float32)
    cols[:, 0:8] = f(norm_mix_gain)[0].reshape(8, 128).T
    cols[:, 8:16] = f(norm_ffn_gain)[0].reshape(8, 128).T
    cols[:, 16:19] = f(mla_q_norm_gain)[0].reshape(3, 128).T
    cols[:, 19] = f(hg_norm_gain)[0]
    cw = f(conv_w)[0].reshape(3, NJ, 128)
    cols[:, 20:20 + 66] = cw.transpose(2, 1, 0).reshape(128, 66)
    cols[:, 86:86 + NJ] = f(conv_b)[0].reshape(NJ, 128).T
    bcs = np.zeros((4, 1024), np.float32)
    bcs[0] = f(final_norm_gain)
    bcs[1, 0:256] = f(mla_kv_norm_gain)[0]
    bcs[2, 0:512] = f(hg_lb_logits)[0]
    bcs[3, 0:512] = f(hg_lb_logits)[1]
    w_in_l = np.ascontiguousarray(f(w_in)[0].reshape(KC, 128, 2752).transpose(1, 0, 2))
    wq = f(w_uq)[0]
    sw = []
    for h in range(NH):
        sw.append(wq[:, h * 192 + 160:h * 192 + 192])
        sw.append(wq[:, h * 192 + 128:h * 192 + 160])
    wq_ext = np.concatenate([wq] + sw, axis=1)
    w_uq_l = np.ascontiguousarray(wq_ext.reshape(3, 128, 1024).transpose(1, 0, 2))
    wkv = np.concatenate([f(w_uk)[0].reshape(256, 512), f(w_uv)[0].reshape(256, 512)], axis=1)
    w_ukv_l = np.ascontiguousarray(wkv.reshape(2, 128, 1024).transpose(1, 0, 2))
    wo = f(w_out)[0]
    w_out_l = np.ascontiguousarray(wo.reshape(KC, 128, 2, 512).transpose(1, 2, 0, 3))
    wu = f(w_up)[0]
    wa = wu[:, :DFF].reshape(KC, 128, NJ, 128)
    wv = wu[:, DFF:].reshape(KC, 128, NJ, 128)
    wup_l = np.stack([wa, wv], axis=3)
    wup_l = wup_l.transpose(2, 1, 0, 3, 4).reshape(NJ, 128, 2048)
    wd_l = f(w_down)[0].reshape(NJ, 128, 1024)
    w_ffn_l = np.ascontiguousarray(np.concatenate([wup_l, wd_l], axis=2))
    b_ada2 = np.ascontiguousarray(np.broadcast_to(f(b_ada)[0][None, :], (2, 6 * D)))
    shared = dict(w_ada=f(w_ada)[0], b_ada2=b_ada2, cols=cols, bcs=bcs, cmat=cmat, ropeT=ropeT, ropeF=ropeF,
                  w_in_l=w_in_l, w_uq_l=w_uq_l, w_ukv_l=w_ukv_l, w_out_l=w_out_l, w_ffn_l=w_ffn_l)
    in_maps = []
    for b in range(n):
        c2 = np.stack([c_prompt[b], c_sample[b]], axis=0)
        c2l = np.ascontiguousarray(c2.reshape(2, KC, 128).transpose(2, 1, 0))
        sc = f(state_ffn_conv)[0, b]
        scl = np.ascontiguousarray(sc.reshape(2, NJ, 128).transpose(2, 1, 0))
        m = dict(shared)
        m.update(xp=x_prompt[b], xs=x_sample[b], c2=c2l, clat=f(cache_kv_latent)[0, b], ckpe=f(cache_k_rope)[0, b],
                 shg=f(state_hgrn)[0, b], sconv=scl)
        in_maps.append(m)
    build_nc()
    needed = {e: [] for e in ['pe', 'act', 'dve', 'pool']}
    for (e, i) in build_nc.used:
        needed[e].append(i)
    for e in needed:
        needed[e].sort()
    nc = build_nc(needed)
    res = run_bass_kernel_spmd(nc, in_maps, core_ids=list(range(n)))
    R = res.results

    def st(name):
        return np.stack([np.asarray(R[b][name], dtype=np.float32) for b in range(n)], axis=0)

    def cvfix(a):
        return np.ascontiguousarray(a.transpose(0, 3, 2, 1).reshape(n, 2, DFF))

    y_prompt = st("y_p")
    y_sample = st("y_s")
    return (y_prompt, y_sample, st("lat_p")[None], st("kpe_p")[None], st("hg_p")[None], cvfix(st("cv_p"))[None],
            st("lat_s")[None], st("kpe_s")[None], st("hg_s")[None], cvfix(st("cv_s"))[None])
```

```python
import bisect
import numpy as np
from contextlib import ExitStack
import concourse.bass as bass
import concourse.mybir as mybir
from concourse.bass_utils import run_bass_kernel_spmd

F32 = mybir.dt.float32
BF16 = mybir.dt.bfloat16
AF = mybir.ActivationFunctionType
ALU = mybir.AluOpType

D = 1024
KC = 8
S_P = 4096
S_S = 16
PAST = 4096
NH = 4
DFF = 2816
NJ = 22
EPS = 1e-6
MLA_SCALE = float((128 + 64) ** -0.5)
NST = 2
TP = NST * 128
SAME_ENGINE_SYNC = True


class Buf:
    def __init__(self, t, name):
        self.t = t
        self.name = name
        self.w = None
        self.r = {}
        self.lsem = None
        self.lcnt = 0
        self.ssem = None
        self.scnt = 0

    def __getitem__(self, k):
        return self.t[k]


class KB:
    def __init__(self, nc, es, needed=None):
        self.nc = nc
        self.es = es
        self.needed = needed
        self.used = set()
        self.eng = {'pe': nc.tensor, 'act': nc.scalar, 'dve': nc.vector, 'pool': nc.gpsimd, 'sp': nc.sync}
        self.sem = {}
        self.cnt = {}
        for e in ['pe', 'act', 'dve', 'pool']:
            self.sem[e] = es.enter_context(nc.semaphore('s_' + e))
            self.cnt[e] = 0
        self.seen = {e: {} for e in self.eng}
        self.needed_set = {e: set(v) for e, v in needed.items()} if needed is not None else None
        self.nsem = 4
        self.allbufs = []
        self.ninst = 0

    def newsem(self, name):
        self.nsem += 1
        return self.es.enter_context(self.nc.semaphore(name))

    def sb(self, name, shape, dt=F32):
        t = self.es.enter_context(self.nc.sbuf_tensor(name, list(shape), dt))
        b = Buf(t, name)
        self.allbufs.append(b)
        return b

    def ps(self, name, shape, dt=F32):
        t = self.es.enter_context(self.nc.psum_tensor(name, list(shape), dt))
        b = Buf(t, name)
        self.allbufs.append(b)
        return b

    def virt(self, name):
        b = Buf(None, name)
        self.allbufs.append(b)
        return b

    def _waits(self, eng, reads, writes):
        need = {}

        def add(ev):
            if ev is None:
                return
            sn, sem, val, src = ev
            if src == eng and (eng == 'pe' or not SAME_ENGINE_SYNC):
                return
            if self.seen[eng].get(sn, 0) >= val:
                return
            if sn not in need or need[sn][1] < val:
                need[sn] = (sem, val)

        for b in reads:
            add(b.w)
        for b in writes:
            add(b.w)
            for ev in b.r.values():
                add(ev)
        E = self.eng[eng]
        for sn, (sem, val) in need.items():
            v = val
            if sn.startswith('s_'):
                src = sn[2:]
                self.used.add((src, val))
                if self.needed is not None:
                    v = bisect.bisect_right(self.needed[src], val)
            E.wait_ge(sem, v)
            self.seen[eng][sn] = val
            self.ninst += 1

    def _record(self, ev, reads, writes):
        for b in writes:
            b.w = ev
            b.r = {}
        for b in reads:
            if b not in writes:
                b.r[ev[0]] = ev

    def op(self, eng, fn, reads=(), writes=()):
        self._waits(eng, reads, writes)
        ins = fn(self.eng[eng])
        self.cnt[eng] += 1
        if self.needed is None or self.cnt[eng] in self.needed_set[eng]:
            ins.then_inc(self.sem[eng], 1)
        self.ninst += 1
        ev = ('s_' + eng, self.sem[eng], self.cnt[eng], eng)
        self._record(ev, reads, writes)

    def dma(self, q, out, in_, reads=(), writes=(), semof=None, store=False, nowait_w=False, **kw):
        if nowait_w:
            self._waits(q, reads, ())
        else:
            self._waits(q, reads, writes)
        ins = self.eng[q].dma_start(out=out, in_=in_, **kw)
        self.ninst += 1
        if store:
            if semof.ssem is None:
                semof.ssem = self.newsem('ss_' + semof.name)
            semof.scnt += 16
            ins.then_inc(semof.ssem, 16)
            ev = ('ss_' + semof.name, semof.ssem, semof.scnt, 'dma')
        else:
            if semof.lsem is None:
                semof.lsem = self.newsem('ls_' + semof.name)
            semof.lcnt += 16
            ins.then_inc(semof.lsem, 16)
            ev = ('ls_' + semof.name, semof.lsem, semof.lcnt, 'dma')
        self._record(ev, reads, writes)
        return ev

    def finish(self):
        for b in self.allbufs:
            if b.ssem is not None and b.scnt > 0:
                self.nc.sync.wait_ge(b.ssem, b.scnt)
            if b.lsem is not None and b.lcnt > 0:
                self.nc.sync.wait_ge(b.lsem, b.lcnt)


def build_nc(needed=None):
    nc = bass.Bass("TRN2", target_bir_lowering=False)

    def din(name, shape, dt=F32):
        return nc.dram_tensor(name, list(shape), dt, kind="ExternalInput").ap()

    def dout(name, shape, dt=F32):
        return nc.dram_tensor(name, list(shape), dt, kind="ExternalOutput").ap()

    def dscr(name, shape, dt):
        return nc.dram_tensor(name, list(shape), dt, kind="Internal").ap()

    xp = din("xp", [S_P, D])
    xs_in = din("xs", [S_S, D])
    c2 = din("c2", [128, KC, 2])
    clat = din("clat", [PAST, 256])
    ckpe = din("ckpe", [PAST, 64])
    shg = din("shg", [NH, 128, 128])
    sconv = din("sconv", [128, NJ, 2])
    w_ada = din("w_ada", [D, 6 * D])
    b_ada2 = din("b_ada2", [2, 6 * D])
    cols = din("cols", [128, 128])
    bcs = din("bcs", [4, 1024])
    cmat = din("cmat", [128, 3, 128])
    ropeT = din("ropeT", [S_P + S_S, 128])
    ropeF = din("ropeF", [64, 2, S_P + S_S])
    w_in_l = din("w_in_l", [128, KC, 2752])
    w_uq_l = din("w_uq_l", [128, 3, 1024])
    w_ukv_l = din("w_ukv_l", [128, 2, 1024])
    w_out_l = din("w_out_l", [128, 2, KC, 512])
    w_ffn_l = din("w_ffn_l", [NJ, 128, 3072])

    y_p = dout("y_p", [S_P, D])
    y_s = dout("y_s", [S_S, D])
    lat_p = dout("lat_p", [S_P, 256])
    kpe_p = dout("kpe_p", [S_P, 64])
    hg_p = dout("hg_p", [NH, 128, 128])
    cv_p = dout("cv_p", [128, NJ, 2])
    lat_s = dout("lat_s", [S_S, 256])
    kpe_s = dout("kpe_s", [S_S, 64])
    hg_s = dout("hg_s", [NH, 128, 128])
    cv_s = dout("cv_s", [128, NJ, 2])

    modscr = dscr("modscr", [2, 6 * D], F32)
    win_b = dscr("win_b", [128, KC * 2752], BF16)
    wuq_b = dscr("wuq_b", [128, 3 * 1024], BF16)
    wukv_b = dscr("wukv_b", [128, 2 * 1024], BF16)
    wout_b = dscr("wout_b", [128, 2 * KC * 512], BF16)
    wffn_b = dscr("wffn_b", [NJ * 128, 3072], BF16)

    es = ExitStack()
    with es:
        nc_ctx = es.enter_context(nc.allow_non_contiguous_dma(reason="small layout loads"))
        es.enter_context(nc.allow_low_precision(reason="bf16 matmul operands"))
        kb = KB(nc, es, needed)
        op = kb.op

        cm = kb.sb("cm", [128, 3, 128])
        colsb = kb.sb("colsb", [128, 128])
        fg_bc = kb.sb("fg_bc", [128, 1024])
        kvn_bc = kb.sb("kvn_bc", [128, 256])
        lb_bc = kb.sb("lb_bc", [128, 512])
        maskU4 = kb.sb("maskU4", [128, 4, 128], BF16)
        ones_bf = kb.sb("ones_bf", [128, 128], BF16)
        mean_bf = kb.sb("mean_bf", [128, 128], BF16)
        mhalf = kb.sb("mhalf", [128, 1])
        c2s = kb.sb("c2s", [128, KC, 2])
        c2e = kb.sb("c2e", [128, KC, 2])
        modc = [kb.sb("modc%d" % r, [128, 4, KC]) for r in range(2)]
        G1c = [kb.sb("G1c%d" % r, [128, KC]) for r in range(2)]
        G2c = [kb.sb("G2c%d" % r, [128, KC]) for r in range(2)]
        g1_bc = kb.sb("g1_bc", [128, 1024])
        g2_bc = kb.sb("g2_bc", [128, 1024])
        ident = cm.t[:, 0, :]
        Umat = cm.t[:, 1, :]
        Uxmat = cm.t[:, 2, :]
        GMIX, GFFN, QN, HGG, CW, CB = 0, 8, 16, 19, 20, 86
        KT = kb.sb("KT", [128, NH, PAST], BF16)
        Vr = kb.sb("Vr", [128, PAST // 128, 512], BF16)
        kpeT = kb.sb("kpeT", [128, PAST], BF16)
        KTs = kb.sb("KTs", [128, NH, S_S], BF16)
        Vs = kb.sb("Vs", [S_S, 512], BF16)
        kpeTs = kb.sb("kpeTs", [128, S_S], BF16)
        Sst = kb.sb("Sst", [128, NH, 128])
        Sbf = kb.sb("Sbf", [128, NH, 128], BF16)
        aprev = kb.sb("aprev", [128, NJ, 2])
        NPOOL = 4
        wpool = [kb.sb("wp%d" % i, [128, 4096], BF16) for i in range(NPOOL)]
        wpi = [0]

        def wnext():
            b = wpool[wpi[0] % NPOOL]
            wpi[0] += 1
            return b

        xbuf = [kb.sb("xb%d" % i, [128, NST, D]) for i in range(2)]
        xparts = []
        for xb_ in xbuf:
            ps_list = []
            for st_ in range(NST):
                pb_ = Buf(xb_.t, xb_.name + "s%d" % st_)
                kb.allbufs.append(pb_)
                ps_list.append(pb_)
            xparts.append(ps_list)
        xsb = kb.sb("xsb", [128, D])
        hTs = [kb.sb("hT%d" % i, [128, KC, TP], BF16) for i in range(2)]
        mixT = kb.sb("mixT", [128, KC, TP], BF16)
        stat = kb.sb("stat", [128, 8])
        tA = [kb.sb("tA%d" % i, [128, 512]) for i in range(2)]
        tB = [kb.sb("tB%d" % i, [128, 512]) for i in range(2)]
        tC = [kb.sb("tC%d" % i, [128, 512]) for i in range(2)]
        tD = kb.sb("tD", [128, 512])
        khat = kb.sb("khat", [128, NST, 512], BF16)
        vv = kb.sb("vv", [128, NST, 512], BF16)
        eb = kb.sb("eb", [128, NH, TP])
        ktT = kb.sb("ktT", [128, NH, TP], BF16)
        qtT = kb.sb("qtT", [128, NH, TP], BF16)
        scm = kb.sb("scm", [128, NH, 128], BF16)
        sqb = kb.sb("sqb", [128, 2, TP], BF16)
        sg = tA[0]
        rs = tA[1]
        cqnT = kb.sb("cqnT", [128, 3, TP], BF16)
        latT = kb.sb("latT", [128, 2, TP], BF16)
        qnT = kb.sb("qnT", [128, NH, TP], BF16)
        qpeT = kb.sb("qpeT", [128, NH, TP], BF16)
        ropeTs = kb.sb("ropeTs", [128, NST, 128])
        ropeFs = kb.sb("ropeFs", [64, 2, TP])
        lato1 = kb.sb("lato0", [128, NST, 256])
        kpeo1 = kb.sb("kpeo0", [128, NST, 64])
        lato = [lato1, lato1]
        kpeo = [kpeo1, kpeo1]
        pT = [kb.sb("pT%d" % i, [128, TP], BF16) for i in range(4)]
        aS = [kb.sb("aS%d" % i, [128, TP + 2]) for i in range(2)]
        cc = [kb.sb("cc%d" % i, [128, TP]) for i in range(2)]
        uT = [kb.sb("uT%d" % i, [128, TP], BF16) for i in range(3)]
        PSB = [kb.ps("psb%d" % i, [128, 512]) for i in range(8)]
        rot = [0]

        def pnext():
            b = PSB[rot[0] % 4]
            rot[0] += 1
            return b

        LB = PSB[4:8]
        wcvA = kb.virt("wcvA")
        wcvC = kb.virt("wcvC")
        wcv0 = kb.virt("wcv0")
        wcvB = kb.virt("wcvB")
        cst = kb.virt("cst")
        modv = kb.virt("modv")

        const_bufs = []

        def cload(buf, out_ap, in_ap):
            kb.dma('sp', out_ap, in_ap, reads=(), writes=(), semof=cst, nowait_w=True)
            const_bufs.append(buf)

        cload(cm, cm.t[:], cmat)
        cload(colsb, colsb.t[:], cols)
        cload(fg_bc, fg_bc.t[:], bcs[0, :].partition_broadcast(128))
        cload(kvn_bc, kvn_bc.t[:], bcs[1, 0:256].partition_broadcast(128))
        cload(tD, tD.t[:], bcs[2, 0:512].partition_broadcast(128))
        cload(lb_bc, lb_bc.t[:], bcs[3, 0:512].partition_broadcast(128))
        cload(c2s, c2s.t[:], c2)
        cev = ('ls_cst', cst.lsem, cst.lcnt, 'dma')
        for b in const_bufs:
            b.w = cev

        def wcast(dst2, src2, virt):
            R = dst2.shape[0]
            for r0 in range(0, R, 512):
                r1 = min(R, r0 + 512)
                kb.dma('pool', dst2[r0:r1, :], src2[r0:r1, :], reads=(), writes=(), semof=virt, nowait_w=True)

        wcast(wukv_b.rearrange("p (a b) -> (p a) b", b=1024), w_ukv_l.rearrange("p k c -> (p k) c"), wcv0)
        wcv0.w = ('ls_wcv0', wcv0.lsem, wcv0.lcnt, 'dma')
        wcast(win_b.rearrange("p (a b) -> (p a) b", b=1376), w_in_l.rearrange("p k c -> p (k c)").rearrange("p (a b) -> (p a) b", b=1376), wcvA)
        wcvA.w = ('ls_wcvA', wcvA.lsem, wcvA.lcnt, 'dma')
        op('pool', lambda e: e.memset(ones_bf.t[:], 1.0), (), (ones_bf,))
        op('pool', lambda e: e.memset(mean_bf.t[:], 1.0 / 128.0), (), (mean_bf,))
        op('pool', lambda e: e.memset(mhalf.t[:], -0.5), (), (mhalf,))
        op('pool', lambda e: e.memset(kpeT.t[64:128, :], 0.0), (), (kpeT,))
        op('pool', lambda e: e.memset(kpeTs.t[64:128, :], 0.0), (), (kpeTs,))
        op('pool', lambda e: e.memset(qpeT.t[64:128, :, :], 0.0), (), (qpeT,))
        for h in range(NH):
            op('pool', lambda e, h=h: e.tensor_copy(maskU4.t[:, h, :], Umat), (cm,), (maskU4,))
        op('dve', lambda e: e.tensor_tensor(lb_bc.t[:], lb_bc.t[:], tD.t[:], ALU.subtract), (lb_bc, tD), (lb_bc,))
        op('act', lambda e: e.activation(lb_bc.t[:], lb_bc.t[:], AF.Exp), (lb_bc,), (lb_bc,))
        op('dve', lambda e: e.tensor_scalar(lb_bc.t[:], lb_bc.t[:], 1.0, None, ALU.add), (lb_bc,), (lb_bc,))
        op('dve', lambda e: e.reciprocal(lb_bc.t[:], lb_bc.t[:]), (lb_bc,), (lb_bc,))
        op('act', lambda e: e.activation(c2e.t[:], c2s.t[:], AF.Exp, scale=-1.0), (c2s,), (c2e,))
        op('dve', lambda e: e.tensor_scalar(c2e.t[:], c2e.t[:], 1.0, None, ALU.add), (c2e,), (c2e,))
        op('dve', lambda e: e.reciprocal(c2e.t[:], c2e.t[:]), (c2e,), (c2e,))
        op('dve', lambda e: e.tensor_tensor(c2e.t[:], c2e.t[:], c2s.t[:], ALU.mult), (c2e, c2s), (c2e,))
        mod_steps = []

        def mk_mod_step(cb, k):
            def step():
                if k == 0:
                    kb.dma('sp', tA[0].t[0:2, :], b_ada2[:, cb * 1024:cb * 1024 + 512], (), (tA[0],), semof=tA[0])
                    kb.dma('sp', tA[1].t[0:2, :], b_ada2[:, cb * 1024 + 512:(cb + 1) * 1024], (), (tA[1],), semof=tA[1])
                wb = wnext()
                wv = wb.t[:, 0:2048].bitcast(F32)
                kb.dma('sp', wv, w_ada[k * 128:(k + 1) * 128, cb * 1024:(cb + 1) * 1024], (), (wb,), semof=wb)
                for i in range(2):
                    op('pe', lambda e, i=i: e.matmul(LB[i].t[0:2, :], c2e.t[:, k, :], wv[:, i * 512:(i + 1) * 512],
                                                     start=(k == 0), stop=(k == KC - 1)),
                       (c2e, wb), (LB[i],))
                if k == KC - 1:
                    for i in range(2):
                        op('dve', lambda e, i=i: e.tensor_tensor(xsb.t[0:2, i * 512:(i + 1) * 512], LB[i].t[0:2, :], tA[i].t[0:2, :], ALU.add),
                           (LB[i], tA[i]), (xsb,))
                    kb.dma('sp', modscr[:, cb * 1024:(cb + 1) * 1024], xsb.t[0:2, :], (xsb,), (modv,), semof=modv, nowait_w=True)
            return step

        for cb in range(6):
            for k in range(KC):
                mod_steps.append(mk_mod_step(cb, k))

        def mod_finish():
            mt = tA[0]
            kb.dma('sp', mt.t[0:96, 0:128], modscr.rearrange("r (m k p) -> (r m k) p", m=6, k=KC), (modv,), (mt,), semof=mt)
            pb = LB[0]
            op('pe', lambda e: e.transpose(pb.t[:, 0:96], mt.t[0:96, 0:128], ident[0:96, 0:96]), (mt, cm), (pb,))
            for r in range(2):
                pv4 = pb.t[:, r * 48:(r + 1) * 48].rearrange("p (m k) -> p m k", m=6)
                op('act', lambda e, r=r, pv4=pv4: e.activation(modc[r].t[:, 0:2, :], pv4[:, 0:2, :], AF.Copy), (pb,), (modc[r],))
                op('act', lambda e, r=r, pv4=pv4: e.activation(modc[r].t[:, 2:4, :], pv4[:, 3:5, :], AF.Copy), (pb,), (modc[r],))
                op('dve', lambda e, r=r: e.scalar_tensor_tensor(G1c[r].t[:], modc[r].t[:, 1, :], 1.0, colsb.t[:, GMIX:GMIX + 8], ALU.add, ALU.mult),
                   (modc[r], colsb), (G1c[r],))
                op('dve', lambda e, r=r: e.scalar_tensor_tensor(G2c[r].t[:], modc[r].t[:, 3, :], 1.0, colsb.t[:, GFFN:GFFN + 8], ALU.add, ALU.mult),
                   (modc[r], colsb), (G2c[r],))

        def load_gbc(r, P):
            kb.dma('sp', g1_bc.t[0:P, :], modscr[r, 2048:3072].partition_broadcast(P), (modv,), (g1_bc,), semof=g1_bc)
            kb.dma('sp', g2_bc.t[0:P, :], modscr[r, 5120:6144].partition_broadcast(P), (modv,), (g2_bc,), semof=g2_bc)

        def wload(src_ap, ncols, virt):
            wb = wnext()
            kb.dma('sp', wb.t[:, 0:ncols], src_ap, (virt,), (wb,), semof=wb)
            return wb

        win3 = win_b.rearrange("p (k c) -> p k c", k=KC)

        def load_win(c0, c1):
            wb = wnext()
            n = c1 - c0
            kb.dma('sp', wb.t[:, 0:KC * n].rearrange("p (k c) -> p k c", k=KC), win3[:, :, c0:c1], (wcvA,), (wb,), semof=wb)
            return wb, wb.t[:, 0:KC * n].rearrange("p (k c) -> p k c", k=KC)

        def rstd_from(col, P, N):
            op('pool', lambda e: e.tensor_scalar(stat.t[0:P, col:col + 1], stat.t[0:P, col:col + 1], 1.0 / N, EPS, ALU.mult, ALU.add),
               (stat,), (stat,))
            op('pool', lambda e: e.tensor_tensor(stat.t[0:P, col:col + 1], stat.t[0:P, col:col + 1], mhalf.t[0:P, 0:1], ALU.pow),
               (stat, mhalf), (stat,))

        def norm_T(xb, st, P, Gc, shc_buf, shc_idx, dst, T0, bank=None):
            norm_T_a(xb, st, P)
            norm_T_b(st, P, Gc, shc_buf, shc_idx, dst, T0, bank)

        def xs_view(xs_):
            if xs_ is xsb:
                return xsb.t[:, :]
            return xs_.t[:, :, :].rearrange("p a b -> p (a b)").bitcast(F32)

        def norm_T_a(xb, st, P, use_pool=False, scol=0, xs_=None):
            xs_ = xsb if xs_ is None else xs_
            xsv = xs_view(xs_)
            xv = xb.t[0:P, st, :]
            op('act', lambda e: e.activation(xsv[0:P, :], xv, AF.Square, accum_out=stat.t[0:P, scol:scol + 1]), (xb,), (xs_, stat))
            rstd_from(scol, P, D)
            op('act', lambda e: e.activation(xsv[0:P, :], xv, AF.Copy, scale=stat.t[0:P, scol:scol + 1]), (xb, stat), (xs_,))

        def norm_T_b(st, P, Gc, shc_buf, shc_idx, dst, T0, bank=None, halves=(0, 1), part=None, xs_=None):
            xs_ = xsb if xs_ is None else xs_
            xsv = xs_view(xs_)
            for half in halves:
                pb = pnext() if bank is None else bank
                if part in (None, 'pe'):
                    for kk in range(4):
                        k = half * 4 + kk
                        op('pe', lambda e, k=k, kk=kk, pb=pb: e.transpose(pb.t[:, kk * P:(kk + 1) * P], xsv[0:P, k * 128:(k + 1) * 128], ident[0:P, 0:P]),
                           (xs_, cm), (pb,))
                if part == 'pe':
                    continue
                for kk in range(4):
                    k = half * 4 + kk
                    eng = 'act' if kk % 2 == 0 else 'dve'
                    if eng == 'act':
                        op('act', lambda e, k=k, kk=kk, pb=pb: e.activation(dst.t[:, k, T0:T0 + P], pb.t[:, kk * P:(kk + 1) * P], AF.Identity,
                                                                         scale=Gc.t[:, k:k + 1], bias=shc_buf.t[:, shc_idx, k:k + 1]),
                           (pb, Gc, shc_buf), (dst,))
                    else:
                        op('dve', lambda e, k=k, kk=kk, pb=pb: e.tensor_scalar(dst.t[:, k, T0:T0 + P], pb.t[:, kk * P:(kk + 1) * P],
                                                                            Gc.t[:, k:k + 1], shc_buf.t[:, shc_idx, k:k + 1], ALU.mult, ALU.add),
                           (pb, Gc, shc_buf), (dst,))

        def kv_project(T, wkv3, wkvb, KT_dst, V_dst_fn, nst, P):
            for hp in range(2):
                pb = pnext()
                for hh in range(2):
                    h = hp * 2 + hh
                    for kc in range(2):
                        op('pe', lambda e, h=h, hh=hh, kc=kc, pb=pb: e.matmul(pb.t[:, hh * T:(hh + 1) * T], wkv3[:, kc, h * 128:(h + 1) * 128], latT.t[:, kc, 0:T],
                                                                          start=(kc == 0), stop=(kc == 1)),
                           (wkvb, latT), (pb,))
                for hh in range(2):
                    h = hp * 2 + hh
                    dst_ap, dst_buf = KT_dst(h)
                    op('act', lambda e, hh=hh, pb=pb, dst_ap=dst_ap: e.activation(dst_ap, pb.t[:, hh * T:(hh + 1) * T], AF.Copy), (pb,), (dst_buf,))
            for st in range(nst):
                pb = pnext()
                for kc in range(2):
                    op('pe', lambda e, kc=kc, pb=pb, st=st: e.matmul(pb.t[0:P, :], latT.t[:, kc, st * P:(st + 1) * P], wkv3[:, kc, 512:1024],
                                                                 start=(kc == 0), stop=(kc == 1)),
                       (wkvb, latT), (pb,))
                dst_ap, dst_buf = V_dst_fn(st)
                op('dve', lambda e, pb=pb, dst_ap=dst_ap: e.tensor_copy(dst_ap, pb.t[0:P, :]), (pb,), (dst_buf,))

        def attend(NQ, keyblocks):
            nb = len(keyblocks)
            items = [(h, bi) for h in range(NH) for bi in range(nb)]
            sTs = {}
            ps_ = {}

            def scores(i):
                h, bi = items[i]
                KT_fn, kpe_ap, kpe_buf, V_fn, nk, c0, maskq = keyblocks[bi]
                sT = pnext()
                kt_ap, kt_buf = KT_fn(h)
                n = NQ - c0
                op('pe', lambda e: e.matmul(sT.t[0:nk, 0:n], kt_ap, qnT.t[:, h, c0:NQ], start=True, stop=False), (kt_buf, qnT), (sT,))
                op('pe', lambda e: e.matmul(sT.t[0:nk, 0:n], kpe_ap, qpeT.t[:, h, c0:NQ], start=False, stop=True), (kpe_buf, qpeT), (sT,))
                sTs[i] = sT

            def expo(i):
                h, bi = items[i]
                KT_fn, kpe_ap, kpe_buf, V_fn, nk, c0, maskq = keyblocks[bi]
                n = NQ - c0
                sT = sTs.pop(i)
                p = pT[i % len(pT)]
                op('act', lambda e: e.activation(p.t[0:nk, 0:n], sT.t[0:nk, 0:n], AF.Exp, scale=MLA_SCALE), (sT,), (p,))
                if maskq:
                    op('pool', lambda e: e.memset(p.t[64:128, 0:64], 0.0), (), (p,))
                ps_[i] = p

            def pv(i):
                h, bi = items[i]
                KT_fn, kpe_ap, kpe_buf, V_fn, nk, c0, maskq = keyblocks[bi]
                n = NQ - c0
                p = ps_.pop(i)
                oT = LB[0 + 2 * (h % 2)]
                sm = LB[1 + 2 * (h % 2)]
                v_ap, v_buf = V_fn(h)
                op('pe', lambda e: e.matmul(oT.t[:, c0:NQ], v_ap, p.t[0:nk, 0:n], start=(bi == 0), stop=(bi == nb - 1)), (v_buf, p), (oT,))
                op('pe', lambda e: e.matmul(sm.t[:, c0:NQ], ones_bf.t[0:nk, :], p.t[0:nk, 0:n], start=(bi == 0), stop=(bi == nb - 1)), (ones_bf, p), (sm,))
                if bi == nb - 1:
                    rc = tD
                    op('dve', lambda e: e.reciprocal(rc.t[:, 0:NQ], sm.t[:, 0:NQ]), (sm,), (rc,))
                    op('dve', lambda e: e.tensor_tensor(mixT.t[:, 4 + h, 0:NQ], oT.t[:, 0:NQ], rc.t[:, 0:NQ], ALU.mult), (oT, rc), (mixT,))

            NI = len(items)
            AHEAD = 3
            for i in range(min(AHEAD, NI)):
                scores(i)
            for i in range(NI):
                expo(i)
                if i + AHEAD < NI:
                    scores(i + AHEAD)
                pv(i)

        prefetched = set()

        def prefetch(seq, ti, g, bank=None, step=None):
            P = seq['P']
            nst = seq['nst']
            T = P * nst
            r = seq['r']
            xb = xbuf[g % 2]
            hT = hTs[g % 2]
            tok0 = ti * T
            pos0 = seq['pos0'] + tok0
            if step is None or step == 'dma':
                kb.dma('sp', xb.t[0:P, 0:nst, :], seq['x'][tok0:tok0 + T, :].rearrange("(n p) d -> p n d", p=P), (), tuple(xparts[g % 2]), semof=xb)
                kb.dma('sp', ropeTs.t[0:P, 0:nst, :], ropeT[pos0:pos0 + T, :].rearrange("(n p) c -> p n c", p=P), (), (ropeTs,), semof=ropeTs)
                kb.dma('sp', ropeFs.t[:, :, 0:T], ropeF[:, :, pos0:pos0 + T], (), (ropeFs,), semof=ropeFs)
            for st in range(nst):
                if step is None or step == ('a', st):
                    norm_T_a(xparts[g % 2][st], st, P, use_pool=False)
                if step is None or step == ('b', st):
                    norm_T_b(st, P, G1c[r], modc[r], 0, hT, st * P, bank)
                for hf_ in range(2):
                    for part in ('pe', 'ev'):
                        if step == ('bh', st, hf_, part):
                            norm_T_b(st, P, G1c[r], modc[r], 0, hT, st * P, bank, halves=(hf_,), part=part)
            if step is None or step == ('b', nst - 1) or step == ('bh', nst - 1, 1, 'ev'):
                prefetched.add(g)

        preloaded_w = {}
        pending_tail = []

        def run_tile(seq, ti, g, nxt=None):
            P = seq['P']
            nst = seq['nst']
            T = P * nst
            C = seq['C']
            nch = P // C
            r = seq['r']
            xb = xbuf[g % 2]
            xps = xparts[g % 2]
            hT = hTs[g % 2]
            tok0 = ti * T
            pos0 = seq['pos0'] + tok0
            last = (ti == seq['ntiles'] - 1)
            if g not in prefetched:
                prefetch(seq, ti, g)
            if g in preloaded_w:
                (wb_f, wf3), (wb_i, wi3), (wb_c, wc3), (wb_c2, wc3b) = preloaded_w.pop(g)
            else:
                wb_f, wf3 = load_win(512, 1024)
                wb_i, wi3 = load_win(1024, 1536)
                wb_c, wc3 = load_win(2048, 2432)
                wb_c2, wc3b = load_win(2432, 2752)
            lo = lato[0]
            ko = kpeo[0]

            def rotator(banks):
                cnt_ = [0]

                def nx():
                    b = banks[cnt_[0] % len(banks)]
                    cnt_[0] += 1
                    return b
                return nx

            def run_chains(chains):
                n = max(len(c) for c in chains)
                for s_ in range(n):
                    for c in chains:
                        if s_ < len(c):
                            c[s_]()

            def H_chain(st, banks):
                a, b_, c_ = tA[st], tB[st], tC[st]
                pn = rotator(banks)
                S = {}

                def s0():
                    pf = pn()
                    for k in range(KC):
                        op('pe', lambda e, k=k: e.matmul(pf.t[0:P, :], hT.t[:, k, st * P:(st + 1) * P], wf3[:, k, :], start=(k == 0), stop=(k == KC - 1)),
                           (hT, wb_f), (pf,))
                    op('act', lambda e: e.activation(a.t[0:P, :], pf.t[0:P, :], AF.Exp, scale=-1.0), (pf,), (a,))
                    pi = S['pi'] = pn()
                    for k in range(KC):
                        op('pe', lambda e, k=k: e.matmul(pi.t[0:P, :], hT.t[:, k, st * P:(st + 1) * P], wi3[:, k, :], start=(k == 0), stop=(k == KC - 1)),
                           (hT, wb_i), (pi,))

                def s1():
                    pi = S['pi']
                    op('pool', lambda e: e.tensor_tensor(b_.t[0:P, :], a.t[0:P, :], lb_bc.t[0:P, :], ALU.mult), (a, lb_bc), (b_,))
                    op('act', lambda e: e.activation(vv.t[0:P, st, :], pi.t[0:P, :], AF.Copy), (pi,), (vv,))

                def s2():
                    op('act', lambda e: e.activation(b_.t[0:P, :], b_.t[0:P, :], AF.Ln, bias=1.0), (b_,), (b_,))
                    op('act', lambda e: e.activation(a.t[0:P, :], a.t[0:P, :], AF.Ln, bias=1.0), (a,), (a,))

                def s3():
                    op('dve', lambda e: e.tensor_tensor(c_.t[0:P, :], b_.t[0:P, :], a.t[0:P, :], ALU.subtract), (a, b_), (c_,))

                def s4():
                    op('act', lambda e: e.activation(a.t[0:P, :], c_.t[0:P, :], AF.Exp), (c_,), (a,))
                    pbb = S['pbb'] = pn()
                    op('pe', lambda e: e.matmul(pbb.t[0:P, :], Umat[0:P, 0:P], c_.t[0:P, :], start=True, stop=True), (cm, c_), (pbb,))
                    pdd = S['pdd'] = pn()
                    op('pe', lambda e: e.matmul(pdd.t[0:P, :], Uxmat[0:P, 0:P], c_.t[0:P, :], start=True, stop=True), (cm, c_), (pdd,))

                def s5():
                    pbb, pdd = S['pbb'], S['pdd']
                    op('pool', lambda e: e.tensor_scalar(a.t[0:P, :], a.t[0:P, :], -1.0, 1.0, ALU.mult, ALU.add), (a,), (a,))
                    op('act', lambda e: e.activation(b_.t[0:P, :], pbb.t[0:P, :], AF.Exp, scale=-1.0), (pbb,), (b_,))
                    op('act', lambda e: e.activation(pdd.t[0:P, :], pdd.t[0:P, :], AF.Exp), (pdd,), (pdd,))

                def s6():
                    pdd = S['pdd']
                    op('dve', lambda e: e.tensor_tensor(b_.t[0:P, :], b_.t[0:P, :], a.t[0:P, :], ALU.mult), (a, b_), (b_,))
                    op('dve', lambda e: e.tensor_tensor(khat.t[0:P, st, :], pdd.t[0:P, :], a.t[0:P, :], ALU.mult), (a, pdd), (khat,))
                    pbt = S['pbt'] = pn()
                    for h in range(NH):
                        op('pe', lambda e, h=h: e.matmul(pbt.t[:, h * P:(h + 1) * P], c_.t[0:P, h * 128:(h + 1) * 128], Umat[0:P, 0:P], start=True, stop=True),
                           (c_, cm), (pbt,))

                def s7():
                    pbt = S['pbt']
                    op('act', lambda e: e.activation(eb.t[:, :, st * P:(st + 1) * P], pbt.t[:, 0:NH * P].rearrange("p (h t) -> p h t", h=NH), AF.Exp),
                       (pbt,), (eb,))
                    pkt = S['pkt'] = pn()
                    for h in range(NH):
                        op('pe', lambda e, h=h: e.transpose(pkt.t[:, h * P:(h + 1) * P], b_.t[0:P, h * 128:(h + 1) * 128], ident[0:P, 0:P]),
                           (b_, cm), (pkt,))

                def s8():
                    pkt = S['pkt']
                    op('dve', lambda e: e.tensor_copy(ktT.t[:, :, st * P:(st + 1) * P], pkt.t[:, 0:NH * P].rearrange("p (h t) -> p h t", h=NH)),
                       (pkt,), (ktT,))

                return [s0, s1, s2, s3, s4, s5, s6, s7, s8]

            def M_chain(st, banks, xs_buf, c1, c2):
                pn = rotator(banks)
                S = {}
                tcs = tC[st]
                tc_, ts_ = cc[0], cc[1]

                def m0():
                    pc1 = S['pc1'] = pn()
                    for k in range(KC):
                        op('pe', lambda e, k=k: e.matmul(pc1.t[0:P, 0:384], hT.t[:, k, st * P:(st + 1) * P], wc3[:, k, 0:384], start=(k == 0), stop=(k == KC - 1)),
                           (hT, wb_c), (pc1,))
                    op('act', lambda e: e.activation(sqb.t[0:P, :, :].rearrange("p h t -> p (h t)")[:, 0:384], pc1.t[0:P, 0:384], AF.Square, accum_out=stat.t[0:P, c1:c1 + 1]),
                       (pc1,), (sqb, stat))
                    pc2 = S['pc2'] = pn()
                    for k in range(KC):
                        op('pe', lambda e, k=k: e.matmul(pc2.t[0:P, 0:320], hT.t[:, k, st * P:(st + 1) * P], wc3b[:, k, 0:320], start=(k == 0), stop=(k == KC - 1)),
                           (hT, wb_c2), (pc2,))

                def m1():
                    rstd_from(c1, P, 384)

                def m2():
                    pc1 = S['pc1']
                    op('act', lambda e: e.activation(xs_buf.t[0:P, 0:384], pc1.t[0:P, 0:384], AF.Copy, scale=stat.t[0:P, c1:c1 + 1]), (pc1, stat), (xs_buf,))
                    pc2 = S['pc2']
                    op('act', lambda e: e.activation(sqb.t[0:P, :, :].rearrange("p h t -> p (h t)")[:, 0:256], pc2.t[0:P, 0:256], AF.Square, accum_out=stat.t[0:P, c2:c2 + 1]),
                       (pc2,), (sqb, stat))

                def m3():
                    rstd_from(c2, P, 256)
                    pt = S['pt'] = pn()
                    for kc in range(3):
                        op('pe', lambda e, kc=kc: e.transpose(pt.t[:, kc * P:(kc + 1) * P], xs_buf.t[0:P, kc * 128:(kc + 1) * 128], ident[0:P, 0:P]), (xs_buf, cm), (pt,))

                def m4():
                    pt, pc2 = S['pt'], S['pc2']
                    for kc in range(3):
                        op('act', lambda e, kc=kc: e.activation(cqnT.t[:, kc, st * P:(st + 1) * P], pt.t[:, kc * P:(kc + 1) * P], AF.Copy, scale=colsb.t[:, QN + kc:QN + kc + 1]),
                           (pt, colsb), (cqnT,))
                    op('dve', lambda e: e.scalar_tensor_tensor(lo.t[0:P, st, :], pc2.t[0:P, 0:256], stat.t[0:P, c2:c2 + 1], kvn_bc.t[0:P, :], ALU.mult, ALU.mult),
                       (pc2, stat, kvn_bc), (lo,))
                    o0 = st * 128
                    op('dve', lambda e: e.tensor_tensor(tc_.t[0:P, o0:o0 + 64], pc2.t[0:P, 256:320], ropeTs.t[0:P, st, 0:64], ALU.mult), (pc2, ropeTs), (tc_,))
                    op('dve', lambda e: e.tensor_tensor(ts_.t[0:P, o0:o0 + 64], pc2.t[0:P, 256:320], ropeTs.t[0:P, st, 64:128], ALU.mult), (pc2, ropeTs), (ts_,))

                def m5():
                    o0 = st * 128
                    pt2 = S['pt2'] = pn()
                    for kc in range(2):
                        op('pe', lambda e, kc=kc: e.transpose(pt2.t[:, kc * P:(kc + 1) * P], lo.t[0:P, st, kc * 128:(kc + 1) * 128], ident[0:P, 0:P]), (lo, cm), (pt2,))
                    op('pool', lambda e: e.tensor_tensor(ko.t[0:P, st, 0:32], tc_.t[0:P, o0:o0 + 32], ts_.t[0:P, o0 + 32:o0 + 64], ALU.subtract), (tc_, ts_), (ko,))
                    op('pool', lambda e: e.tensor_tensor(ko.t[0:P, st, 32:64], tc_.t[0:P, o0 + 32:o0 + 64], ts_.t[0:P, o0:o0 + 32], ALU.add), (tc_, ts_), (ko,))

                def m6():
                    pt2 = S['pt2']
                    op('act', lambda e: e.activation(latT.t[:, :, st * P:(st + 1) * P], pt2.t[:, 0:2 * P].rearrange("p (k t) -> p k t", k=2), AF.Copy), (pt2,), (latT,))
                    pt3 = S['pt3'] = pn()
                    op('pe', lambda e: e.transpose(pt3.t[0:64, 0:P], ko.t[0:P, st, :], ident[0:P, 0:P]), (ko, cm), (pt3,))

                def m7():
                    pt3 = S['pt3']
                    kdst_ap, kdst_buf = seq['kpe_dst'](tok0 + st * P, P)
                    op('act', lambda e: e.activation(kdst_ap, pt3.t[0:64, 0:P], AF.Copy), (pt3,), (kdst_buf,))

                return [m0, m1, m2, m3, m4, m5, m6, m7]

            chains = []
            ptail = pending_tail.pop() if pending_tail else None
            if ptail is not None:
                chains.append([lambda: None, ptail])
            for st in range(nst):
                chains.append(H_chain(st, [PSB[2 * st], PSB[2 * st + 1]]))
                mch = M_chain(st, [PSB[4 + 2 * st], PSB[5 + 2 * st]], xsb if st == 0 else tD, 1 + 3 * st, 2 + 3 * st)
                if ptail is not None:
                    mch = [lambda: None] + mch
                chains.append(mch)
            run_chains(chains)
            wb_q, wq3 = load_win(0, 512)
            kb.dma('sp', seq['lat_out'][tok0:tok0 + T, :].rearrange("(n p) c -> p n c", p=P), lo.t[0:P, 0:nst, :], (lo,), (), semof=lo, store=True)
            kb.dma('sp', seq['kpe_out'][tok0:tok0 + T, :].rearrange("(n p) c -> p n c", p=P), ko.t[0:P, 0:nst, :], (ko,), (), semof=ko, store=True)
            RB = rotator([PSB[2], PSB[3]])
            for hp in range(2):
                pq = RB()
                for hh in range(2):
                    h = hp * 2 + hh
                    for k in range(KC):
                        op('pe', lambda e, k=k, h=h, hh=hh: e.matmul(pq.t[:, hh * T:(hh + 1) * T], wq3[:, k, h * 128:(h + 1) * 128], hT.t[:, k, 0:T],
                                                                  start=(k == 0), stop=(k == KC - 1)),
                           (hT, wb_q), (pq,))
                op('dve', lambda e: e.scalar_tensor_tensor(qtT.t[:, hp * 2:hp * 2 + 2, 0:T], pq.t[:, 0:2 * T].rearrange("p (h t) -> p h t", h=2),
                                                           float(128 ** -0.5), eb.t[:, hp * 2:hp * 2 + 2, 0:T], ALU.mult, ALU.mult),
                   (pq, eb), (qtT,))

            wb_g, wg3 = load_win(1536, 2048)
            wb_u = wload(wuq_b, 3072, wcvC)
            wu3 = wb_u.t[:, 0:3072].rearrange("p (k c) -> p k c", k=3)
            wb_kv = wload(wukv_b, 2048, wcvA)
            wkv3 = wb_kv.t[:, 0:2048].rearrange("p (k c) -> p k c", k=2)
            oTb = [PSB[0], PSB[1]]
            sgX = [tA[0], tA[1]]
            rsX = [tC[0], tC[1]]

            def SEQ_chain():
                stages = []
                for st in range(nst):
                    def q0(st=st):
                        psc = PSB[2]
                        for h in range(NH):
                            op('pe', lambda e, h=h: e.matmul(psc.t[0:P, h * P:(h + 1) * P], ktT.t[:, h, st * P:(st + 1) * P], qtT.t[:, h, st * P:(st + 1) * P],
                                                             start=True, stop=True), (ktT, qtT), (psc,))
                        op('dve', lambda e: e.tensor_tensor(scm.t[0:P, :, 0:P], psc.t[0:P, 0:NH * P].rearrange("p (h t) -> p h t", h=NH), maskU4.t[0:P, :, 0:P], ALU.mult),
                           (psc, maskU4), (scm,))
                    stages.append(q0)
                    for ch in range(nch):
                        def cA(st=st, ch=ch):
                            r0 = ch * C
                            t0 = st * P + ch * C
                            pkv = PSB[3]
                            for h in range(NH):
                                op('pe', lambda e, h=h: e.matmul(pkv.t[:, h * 128:(h + 1) * 128], khat.t[r0:r0 + C, st, h * 128:(h + 1) * 128], vv.t[r0:r0 + C, st, h * 128:(h + 1) * 128],
                                                                 start=True, stop=True), (khat, vv), (pkv,))
                            for h in range(NH):
                                ob = oTb[h // 2]
                                oc = (h % 2) * T + t0
                                op('pe', lambda e, h=h, ob=ob, oc=oc: e.matmul(ob.t[:, oc:oc + C], Sbf.t[:, h, :], qtT.t[:, h, t0:t0 + C], start=True, stop=False),
                                   (Sbf, qtT), (ob,))
                                op('pe', lambda e, h=h, ob=ob, oc=oc: e.matmul(ob.t[:, oc:oc + C], vv.t[r0:r0 + C, st, h * 128:(h + 1) * 128], scm.t[r0:r0 + C, h, r0:r0 + C],
                                                                            start=False, stop=True), (vv, scm), (ob,))
                            dec = eb.t[:, :, t0 + C - 1:t0 + C].to_broadcast([128, NH, 128])
                            op('dve', lambda e: e.tensor_tensor(Sst.t[:], Sst.t[:], dec, ALU.mult), (Sst, eb), (Sst,))

                        def cB(st=st, ch=ch):
                            pkv = PSB[3]
                            pk3 = pkv.t[:].rearrange("p (h v) -> p h v", h=NH)
                            op('dve', lambda e: e.tensor_tensor(Sbf.t[:], Sst.t[:], pk3, ALU.add), (Sst, pkv), (Sbf,))
                            op('dve', lambda e: e.tensor_tensor(Sst.t[:], Sst.t[:], pk3, ALU.add), (Sst, pkv), (Sst,))
                        stages += [cA, cB]
                return stages

            def G_chain():
                stages = []
                for hp in range(2):
                    sgb = sgX[hp]
                    sg3 = sgb.t[:, 0:2 * T].rearrange("p (h t) -> p h t", h=2)

                    def g0(hp=hp, sgb=sgb, sg3=sg3):
                        pg = PSB[4]
                        for hh in range(2):
                            h = hp * 2 + hh
                            for k in range(KC):
                                op('pe', lambda e, k=k, h=h, hh=hh: e.matmul(pg.t[:, hh * T:(hh + 1) * T], wg3[:, k, h * 128:(h + 1) * 128], hT.t[:, k, 0:T],
                                                                          start=(k == 0), stop=(k == KC - 1)),
                                   (hT, wb_g), (pg,))
                        pg3 = pg.t[:, 0:2 * T].rearrange("p (h t) -> p h t", h=2)
                        op('act', lambda e: e.activation(sg3, pg3, AF.Tanh, scale=0.5), (pg,), (sgb,))

                    def g1(hp=hp, sgb=sgb, sg3=sg3):
                        pg = PSB[4]
                        pg3 = pg.t[:, 0:2 * T].rearrange("p (h t) -> p h t", h=2)
                        op('dve', lambda e: e.scalar_tensor_tensor(sg3, sg3, 1.0, pg3, ALU.add, ALU.mult), (sgb, pg), (sgb,))
                    stages += [g0, g1]
                return stages

            def Q_chain():
                stages = []
                S = {}
                QB = rotator([PSB[5], PSB[6]])
                for hp in range(2):
                    def q0(hp=hp):
                        pq = QB()
                        for hh in range(2):
                            h = hp * 2 + hh
                            for kc in range(3):
                                op('pe', lambda e, kc=kc, h=h, hh=hh: e.matmul(pq.t[:, hh * T:(hh + 1) * T], wu3[:, kc, h * 192:h * 192 + 128], cqnT.t[:, kc, 0:T],
                                                                            start=(kc == 0), stop=(kc == 2)), (wb_u, cqnT), (pq,))
                        op('act', lambda e: e.activation(qnT.t[:, hp * 2:hp * 2 + 2, 0:T], pq.t[:, 0:2 * T].rearrange("p (h t) -> p h t", h=2), AF.Copy), (pq,), (qnT,))

                    def q1(hp=hp):
                        pp = S['pp'] = QB()
                        for hh in range(2):
                            h = hp * 2 + hh
                            for kc in range(3):
                                op('pe', lambda e, kc=kc, h=h, hh=hh: e.matmul(pp.t[0:64, hh * T:(hh + 1) * T], wu3[:, kc, h * 192 + 128:h * 192 + 192], cqnT.t[:, kc, 0:T],
                                                                            start=(kc == 0), stop=(kc == 2)), (wb_u, cqnT), (pp,))

                    def q2(hp=hp):
                        ps_ = S['ps'] = QB()
                        for hh in range(2):
                            h = hp * 2 + hh
                            for kc in range(3):
                                op('pe', lambda e, kc=kc, h=h, hh=hh: e.matmul(ps_.t[0:64, hh * T:(hh + 1) * T], wu3[:, kc, 768 + h * 64:768 + (h + 1) * 64], cqnT.t[:, kc, 0:T],
                                                                            start=(kc == 0), stop=(kc == 2)), (wb_u, cqnT), (ps_,))

                    def q3(hp=hp):
                        pp, ps_ = S['pp'], S['ps']
                        op('act', lambda e: e.activation(tB[0].t[0:64, 0:2 * T], pp.t[0:64, 0:2 * T], AF.Copy), (pp,), (tB[0],))
                        op('act', lambda e: e.activation(tB[1].t[0:64, 0:2 * T], ps_.t[0:64, 0:2 * T], AF.Copy), (ps_,), (tB[1],))

                    def q4(hp=hp):
                        cosb = ropeFs.t[:, 0:1, 0:T].to_broadcast([64, 2, T])
                        sinb = ropeFs.t[:, 1:2, 0:T].to_broadcast([64, 2, T])
                        t0v = tB[0].t[0:64, 0:2 * T].rearrange("p (h t) -> p h t", h=2)
                        t1v = tB[1].t[0:64, 0:2 * T].rearrange("p (h t) -> p h t", h=2)
                        op('pool', lambda e: e.tensor_tensor(t0v, t0v, cosb, ALU.mult), (tB[0], ropeFs), (tB[0],))
                        op('pool', lambda e: e.tensor_tensor(t1v, t1v, sinb, ALU.mult), (tB[1], ropeFs), (tB[1],))
                        op('pool', lambda e: e.tensor_tensor(qpeT.t[0:64, hp * 2:hp * 2 + 2, 0:T], t0v, t1v, ALU.add), (tB[0], tB[1]), (qpeT,))
                    stages += [q0, q1, q2, q3, q4]
                return stages

            def KV_chain():
                stages = []
                pb = PSB[7]
                for hp in range(2):
                    def k0(hp=hp):
                        for hh in range(2):
                            h = hp * 2 + hh
                            for kc in range(2):
                                op('pe', lambda e, h=h, hh=hh, kc=kc: e.matmul(pb.t[:, hh * T:(hh + 1) * T], wkv3[:, kc, h * 128:(h + 1) * 128], latT.t[:, kc, 0:T],
                                                                            start=(kc == 0), stop=(kc == 1)),
                                   (wb_kv, latT), (pb,))
                        for hh in range(2):
                            h = hp * 2 + hh
                            dst_ap, dst_buf = seq['KT_dst'](h, tok0, T)
                            op('act', lambda e, hh=hh, dst_ap=dst_ap: e.activation(dst_ap, pb.t[:, hh * T:(hh + 1) * T], AF.Copy), (pb,), (dst_buf,))
                    stages.append(k0)
                for st in range(nst):
                    def v0(st=st):
                        for kc in range(2):
                            op('pe', lambda e, kc=kc: e.matmul(pb.t[0:P, :], latT.t[:, kc, st * P:(st + 1) * P], wkv3[:, kc, 512:1024],
                                                             start=(kc == 0), stop=(kc == 1)),
                               (wb_kv, latT), (pb,))
                        dst_ap, dst_buf = seq['V_dst'](tok0, st, P)
                        op('dve', lambda e: e.tensor_copy(dst_ap, pb.t[0:P, :]), (pb,), (dst_buf,))
                    stages.append(v0)
                return stages

            run_chains([SEQ_chain(), G_chain(), Q_chain(), KV_chain()])
            if last:
                kb.dma('sp', seq['hg_out'].rearrange("h k v -> k h v"), Sst.t[:], (Sst,), (), semof=Sst, store=True)

            def R_chain(hp):
                ob = oTb[hp]
                ob3 = ob.t[:, 0:2 * T].rearrange("p (h t) -> p h t", h=2)
                sgb = sgX[hp]
                sg3 = sgb.t[:, 0:2 * T].rearrange("p (h t) -> p h t", h=2)
                rsb = rsX[hp]
                rs3 = rsb.t[:, 0:2 * T].rearrange("p (h t) -> p h t", h=2)
                sq3 = sqb.t[:, :, 0:T] if hp == 0 else khat.t[:, 0, 0:2 * T].rearrange("p (h t) -> p h t", h=2)
                sqbuf = sqb if hp == 0 else khat
                sq2 = sq3
                S = {}

                def r0():
                    op('act', lambda e: e.activation(sq3, ob3, AF.Square), (ob,), (sqbuf,))
                    pm = S['pm'] = RB()
                    op('pe', lambda e: e.matmul(pm.t[:, 0:2 * T], mean_bf.t[:], sq2, start=True, stop=True), (mean_bf, sqbuf), (pm,))

                def r1():
                    pm = S['pm']
                    pm3 = pm.t[:, 0:2 * T].rearrange("p (h t) -> p h t", h=2)
                    op('act', lambda e: e.activation(rs3, pm3, AF.Ln, bias=EPS), (pm,), (rsb,))

                def r2():
                    op('act', lambda e: e.activation(rs3, rs3, AF.Exp, scale=-0.5), (rsb,), (rsb,))

                def r3():
                    op('dve', lambda e: e.scalar_tensor_tensor(rs3, rs3, 0.5, ob3, ALU.mult, ALU.mult), (rsb, ob), (rsb,))

                def r4():
                    op('dve', lambda e: e.scalar_tensor_tensor(mixT.t[:, hp * 2:hp * 2 + 2, 0:T], rs3, colsb.t[:, HGG:HGG + 1], sg3, ALU.mult, ALU.mult),
                       (rsb, sgb, colsb), (mixT,))
                return [r0, r1, r2, r3, r4]

            run_chains([R_chain(0), R_chain(1)])
            attend(T, seq['keyblocks'](ti))
            wbs = {}

            wffn3 = wffn_b.rearrange("(j p) c -> p j c", p=128)
            dnbufs = [tA[0], tA[1], tC[0], tC[1], tD]
            wdns = {}

            def ffn_load_up(q):
                wb = wnext()
                kb.dma('sp', wb.t[:, 0:4096].rearrange("p (j c) -> p j c", j=2), wffn3[:, 2 * q:2 * q + 2, 0:2048], (wcvB,), (wb,), semof=wb)
                wbs[2 * q] = (wb, wb.t[:, 0:2048])
                wbs[2 * q + 1] = (wb, wb.t[:, 2048:4096])

            def ffn_load_dn(j):
                db = dnbufs[j % 5]
                dv = db.t[:, :].bitcast(BF16)
                kb.dma('sp', dv, wffn_b[j * 128:(j + 1) * 128, 2048:3072], (wcvB,), (db,), semof=db)
                wdns[j] = (db, dv)

            wos = []
            for half in range(2):
                wb_o = wload(wout_b[:, half * 4096:(half + 1) * 4096], 4096, wcvC)
                wos.append((wb_o, wb_o.t[:, 0:4096].rearrange("p (k c) -> p k c", k=KC)))
            ffn_load_up(0)
            ffn_load_up(1)
            for j in range(5):
                ffn_load_dn(j)
            for st in range(nst):
                for half in range(2):
                    wb_o, wo3 = wos[half]
                    po = pnext()
                    for c in range(KC):
                        op('pe', lambda e, c=c: e.matmul(po.t[0:P, :], mixT.t[:, c, st * P:(st + 1) * P], wo3[:, c, :], start=(c == 0), stop=(c == KC - 1)),
                           (mixT, wb_o), (po,))
                    tt = tB[st % 2]
                    op('dve', lambda e: e.tensor_tensor(tt.t[0:P, :], po.t[0:P, :], g1_bc.t[0:P, half * 512:(half + 1) * 512], ALU.mult), (po, g1_bc), (tt,))
                    op('pool', lambda e: e.tensor_tensor(xb.t[0:P, st, half * 512:(half + 1) * 512], xb.t[0:P, st, half * 512:(half + 1) * 512], tt.t[0:P, :], ALU.add),
                       (xps[st], tt), (xps[st],))
            for st in range(nst):
                norm_T_a(xps[st], st, P, scol=4 + st, xs_=(xsb if st == 0 else mixT))
            for st in range(nst):
                norm_T_b(st, P, G2c[r], modc[r], 2, hT, st * P, xs_=(xsb if st == 0 else mixT))
            acc = [[LB[st * 2 + half] for half in range(2)] for st in range(nst)]
            pavs = {}

            def ffn_up(j):
                wb, wv_ = wbs[j]
                wup = wv_.rearrange("p (k a c) -> p k a c", k=KC, a=2)
                pav = bank_of[('c', j)]
                for a_ in range(2):
                    for k in range(KC):
                        op('pe', lambda e, k=k, a_=a_: e.matmul(pav.t[:, a_ * T:(a_ + 1) * T], wup[:, k, a_, :], hT.t[:, k, 0:T], start=(k == 0), stop=(k == KC - 1)),
                           (wb, hT), (pav,))
                pavs[j] = pav

            def ffn_elem(j):
                pav = pavs[j]
                as_ = aS[j % 2]
                c1 = cc[j % 2]
                u = uT[j % 3]
                cwb = CW + j * 3
                op('pool', lambda e: e.tensor_copy(as_.t[:, 0:2], aprev.t[:, j, :]), (aprev,), (as_,))
                op('act', lambda e: e.activation(as_.t[:, 2:T + 2], pav.t[:, 0:T], AF.Copy), (pav,), (as_,))
                op('act', lambda e: e.activation(c1.t[:, 0:T], pav.t[:, 0:T], AF.Identity, scale=colsb.t[:, cwb + 2:cwb + 3], bias=colsb.t[:, CB + j:CB + j + 1]),
                   (pav, colsb), (c1,))
                op('pool', lambda e: e.tensor_copy(aprev.t[:, j, :], as_.t[:, T:T + 2]), (as_,), (aprev,))
                op('dve', lambda e: e.scalar_tensor_tensor(c1.t[:, 0:T], as_.t[:, 1:T + 1], colsb.t[:, cwb + 1:cwb + 2], c1.t[:, 0:T], ALU.mult, ALU.add),
                   (as_, colsb, c1), (c1,))
                op('dve', lambda e: e.scalar_tensor_tensor(c1.t[:, 0:T], as_.t[:, 0:T], colsb.t[:, cwb:cwb + 1], c1.t[:, 0:T], ALU.mult, ALU.add),
                   (as_, colsb, c1), (c1,))
                op('act', lambda e: e.activation(c1.t[:, 0:T], c1.t[:, 0:T], AF.Gelu), (c1,), (c1,))
                op('dve', lambda e: e.tensor_tensor(u.t[:, 0:T], c1.t[:, 0:T], pav.t[:, T:2 * T], ALU.mult), (c1, pav), (u,))

            def ffn_down(j):
                wbs.pop(j)
                wb, wdn = wdns.pop(j)
                u = uT[j % 3]
                for st in range(nst):
                    for half in range(2):
                        ab = acc[st][half]
                        op('pe', lambda e, st=st, half=half, ab=ab: e.matmul(ab.t[0:P, :], u.t[:, st * P:(st + 1) * P], wdn[:, half * 512:(half + 1) * 512],
                                                                          start=(j == 0), stop=(j == NJ - 1)), (u, wb), (ab,))

            ffn_load_up(2)
            ffn_load_up(3)
            items = [('c', j) for j in range(NJ)]
            post = {}
            if nxt is not None:
                nn = nxt[0]['nst']
                ins_at = [(8, 0)] + ([(16, 1)] if nn > 1 else [])
                for (cj, st_) in reversed(ins_at):
                    pos = items.index(('c', cj))
                    items[pos:pos] = [('p', st_, 0), ('p', st_, 1)]
                post[('c', 1)] = ('a', 0)
                if nn > 1:
                    post[('c', 8)] = ('a', 1)
                prefetch(nxt[0], nxt[1], g + 1, None, 'dma')
            bank_of = {it: PSB[i % 4] for i, it in enumerate(items)}
            AH = 3

            def stage1(it):
                if it[0] == 'c':
                    ffn_up(it[1])
                else:
                    prefetch(nxt[0], nxt[1], g + 1, bank_of[it], ('bh', it[1], it[2], 'pe'))

            def stage2(it):
                if it[0] == 'c':
                    ffn_elem(it[1])
                else:
                    prefetch(nxt[0], nxt[1], g + 1, bank_of[it], ('bh', it[1], it[2], 'ev'))

            for i in range(min(AH, len(items))):
                stage1(items[i])
            for i, it in enumerate(items):
                if it[0] == 'c':
                    j = it[1]
                    if j % 2 == 0 and j // 2 + 4 < NJ // 2:
                        ffn_load_up(j // 2 + 4)
                stage2(it)
                if it in post:
                    prefetch(nxt[0], nxt[1], g + 1, None, post[it])
                if i + AH < len(items):
                    stage1(items[i + AH])
                if it[0] == 'c':
                    ffn_down(it[1])
                    if it[1] + 5 < NJ:
                        ffn_load_dn(it[1] + 5)
            if nxt is not None:
                preloaded_w[g + 1] = (load_win(512, 1024), load_win(1024, 1536), load_win(2048, 2432), load_win(2432, 2752))
            def tail():
                if last:
                    kb.dma('sp', seq['cv_out'], aprev.t[:], (aprev,), (), semof=aprev, store=True)
                for st in range(nst):
                    for half in range(2):
                        ab = acc[st][half]
                        tt = tB[(st * 2 + half) % 2]
                        op('dve', lambda e: e.tensor_tensor(tt.t[0:P, :], ab.t[0:P, :], g2_bc.t[0:P, half * 512:(half + 1) * 512], ALU.mult), (ab, g2_bc), (tt,))
                        op('pool', lambda e: e.tensor_tensor(xb.t[0:P, st, half * 512:(half + 1) * 512], xb.t[0:P, st, half * 512:(half + 1) * 512], tt.t[0:P, :], ALU.add),
                           (xps[st], tt), (xps[st],))
                for st in range(nst):
                    xv = xb.t[0:P, st, :]
                    op('act', lambda e: e.activation(xsb.t[0:P, :], xv, AF.Square, accum_out=stat.t[0:P, 3:4]), (xps[st],), (xsb, stat))
                    rstd_from(3, P, D)
                    op('dve', lambda e: e.scalar_tensor_tensor(xv, xv, stat.t[0:P, 3:4], fg_bc.t[0:P, :], ALU.mult, ALU.mult), (xps[st], stat, fg_bc), (xps[st],))
                kb.dma('sp', seq['y_out'][tok0:tok0 + T, :].rearrange("(n p) d -> p n d", p=P), xb.t[0:P, 0:nst, :], tuple(xps[0:nst]), (), semof=xb, store=True)

            if (not last) and P == 128:
                pending_tail.append(tail)
            else:
                tail()

        kb.dma('sp', mixT.t[:, :, :].rearrange("p a b -> p (a b)"), wukv_b, (wcv0,), (mixT,), semof=mixT)
        wkv3c = mixT.t[:, :, :].rearrange("p a b -> p (a b)").rearrange("p (k c) -> p k c", k=2)
        msi = 0
        for g in range(PAST // TP):
            for _ in range(3):
                if msi < len(mod_steps):
                    mod_steps[msi]()
                    msi += 1
            cl = xparts[g % 2][0]
            clv = cl.t[:, 0, 0:640].rearrange("p (n c) -> p n c", n=2)
            kb.dma('sp', clv[:, :, 0:256], clat[g * TP:(g + 1) * TP, :].rearrange("(n p) c -> p n c", p=128), (), (cl,), semof=xbuf[g % 2])
            kb.dma('sp', clv[:, :, 256:320], ckpe[g * TP:(g + 1) * TP, :].rearrange("(n p) c -> p n c", p=128), (), (cl,), semof=xbuf[g % 2])
            for st in range(NST):
                pt2 = pnext()
                for kc in range(2):
                    op('pe', lambda e, kc=kc: e.transpose(pt2.t[:, kc * 128:(kc + 1) * 128], clv[:, st, kc * 128:(kc + 1) * 128], ident), (cl, cm), (pt2,))
                op('act', lambda e: e.activation(latT.t[:, :, st * 128:(st + 1) * 128], pt2.t[:, 0:256].rearrange("p (k t) -> p k t", k=2), AF.Copy), (pt2,), (latT,))
                pt3 = pnext()
                op('pe', lambda e: e.transpose(pt3.t[0:64, 0:128], clv[:, st, 256:320], ident), (cl, cm), (pt3,))
                c0 = g * TP + st * 128
                op('dve', lambda e: e.tensor_copy(kpeT.t[0:64, c0:c0 + 128], pt3.t[0:64, 0:128]), (pt3,), (kpeT,))
            kv_project(TP, wkv3c, mixT, lambda h, g=g: (KT.t[:, h, g * TP:(g + 1) * TP], KT),
                       lambda st, g=g: (Vr.t[:, g * NST + st, :], Vr), NST, 128)
        while msi < len(mod_steps):
            mod_steps[msi]()
            msi += 1
        mod_finish()
        kb.dma('pool', wuq_b.rearrange("p (a b) -> (p a) b", b=1024), w_uq_l.rearrange("p k c -> (p k) c"), reads=(modv,), semof=wcvC, nowait_w=True)
        kb.dma('pool', wout_b.rearrange("p (a b) -> (p a) b", b=2048), w_out_l.rearrange("p h k c -> p (h k c)").rearrange("p (a b) -> (p a) b", b=2048),
               reads=(modv,), semof=wcvC, nowait_w=True)
        wcvC.w = ('ls_wcvC', wcvC.lsem, wcvC.lcnt, 'dma')
        wsrc = w_ffn_l.rearrange("j p c -> (j p) c")
        RH = NJ * 128 // 2
        for r0 in range(0, NJ * 128, RH):
            kb.dma('pool', wffn_b[r0:r0 + RH, 0:2048], wsrc[r0:r0 + RH, 0:2048], reads=(modv,), semof=wcvB, nowait_w=True)
            kb.dma('pool', wffn_b[r0:r0 + RH, 2048:3072], wsrc[r0:r0 + RH, 2048:3072], reads=(modv,), semof=wcvB, nowait_w=True)
        wcvB.w = ('ls_wcvB', wcvB.lsem, wcvB.lcnt, 'dma')


        def full_block(kt):
            return (lambda h: (KT.t[:, h, kt * 128:(kt + 1) * 128], KT), kpeT.t[:, kt * 128:(kt + 1) * 128], kpeT,
                    lambda h: (Vr.t[:, kt, h * 128:(h + 1) * 128], Vr), 128)

        def sample_blocks(ti):
            bl = []
            for kt in range(PAST // 128):
                bl.append(full_block(kt) + (0, False))
            bl.append((lambda h: (KTs.t[:, h, :], KTs), kpeTs.t[:, :], kpeTs, lambda h: (Vs.t[:, h * 128:(h + 1) * 128], Vs), S_S, 0, False))
            return bl

        seq_s = dict(P=S_S, nst=1, C=S_S, r=1, ntiles=1, pos0=PAST, x=xs_in, y_out=y_s, lat_out=lat_s, kpe_out=kpe_s, hg_out=hg_s, cv_out=cv_s,
                     kpe_dst=lambda t0, P: (kpeTs.t[0:64, t0:t0 + P], kpeTs),
                     KT_dst=lambda h, t0, T: (KTs.t[:, h, t0:t0 + T], KTs),
                     V_dst=lambda t0, st, P: (Vs.t[0:P, :], Vs),
                     keyblocks=sample_blocks)
        def prompt_blocks(ti):
            bl = []
            for kt in range(ti * NST):
                bl.append(full_block(kt) + (0, False))
            for j in range(NST):
                bl.append(full_block(ti * NST + j) + (j * 128, True))
            return bl

        seq_p = dict(P=128, nst=NST, C=64, r=0, ntiles=S_P // TP, pos0=0, x=xp, y_out=y_p, lat_out=lat_p, kpe_out=kpe_p, hg_out=hg_p, cv_out=cv_p,
                     kpe_dst=lambda t0, P: (kpeT.t[0:64, t0:t0 + P], kpeT),
                     KT_dst=lambda h, t0, T: (KT.t[:, h, t0:t0 + T], KT),
                     V_dst=lambda t0, st, P: (Vr.t[:, t0 // 128 + st, :], Vr),
                     keyblocks=prompt_blocks)
        kb.dma('sp', Sst.t[:], shg.rearrange("h k v -> k h v"), (), (Sst,), semof=Sst)
        op('act', lambda e: e.activation(Sbf.t[:], Sst.t[:], AF.Copy), (Sst,), (Sbf,))
        kb.dma('sp', aprev.t[:], sconv, (), (aprev,), semof=aprev)
        load_gbc(1, S_S)
        run_tile(seq_s, 0, 0, (seq_p, 0))

        op('pool', lambda e: e.memset(Sst.t[:], 0.0), (), (Sst,))
        op('pool', lambda e: e.memset(Sbf.t[:], 0.0), (), (Sbf,))
        op('pool', lambda e: e.memset(aprev.t[:], 0.0), (), (aprev,))
        load_gbc(0, 128)

        for ti in range(seq_p['ntiles']):
            run_tile(seq_p, ti, 1 + ti, (seq_p, ti + 1) if ti + 1 < seq_p['ntiles'] else None)
        kb.finish()
        build_nc.stats = (kb.ninst, kb.nsem, dict(kb.cnt))
        build_nc.used = kb.used
        build_nc.sbuf_left = nc.sbuf_bytes_remaining
    return nc


def _rope_tables():
    half = 32
    inv_freq = (np.float32(10000.0) ** (-np.arange(half, dtype=np.float32) / np.float32(half))).astype(np.float32)
    pos = np.arange(S_P + S_S, dtype=np.float32)
    ang = (pos[:, None] * inv_freq[None, :]).astype(np.float32)
    cos = np.cos(ang).astype(np.float32)
    sin = np.sin(ang).astype(np.float32)
    ropeT = np.concatenate([cos, cos, sin, sin], axis=1).astype(np.float32)
    ropeF = np.stack([np.concatenate([cos.T, cos.T], 0), np.concatenate([-sin.T, sin.T], 0)], axis=1)
    return np.ascontiguousarray(ropeT), np.ascontiguousarray(ropeF.astype(np.float32))


def _const_mats():
    idx = np.arange(128)
    same = (idx[:, None] // 64) == (idx[None, :] // 64)
    U = (same & (idx[:, None] <= idx[None, :])).astype(np.float32)
    Ux = (same & (idx[:, None] > idx[None, :])).astype(np.float32)
    return np.ascontiguousarray(np.stack([np.eye(128, dtype=np.float32), U, Ux], axis=1))


def kernel(x_prompt, x_sample, c_prompt, c_sample, cache_kv_latent, cache_k_rope, state_hgrn,
           state_ffn_conv, w_ada, b_ada, norm_mix_gain, w_in, hg_lb_logits, hg_norm_gain,
           mla_q_norm_gain, mla_kv_norm_gain, w_uq, w_uk, w_uv, w_out, norm_ffn_gain, w_up,
           conv_w, conv_b, w_down, final_norm_gain):
    f = lambda a: np.ascontiguousarray(np.asarray(a, dtype=np.float32))
    x_prompt, x_sample, c_prompt, c_sample = f(x_prompt), f(x_sample), f(c_prompt), f(c_sample)
    n = 8
    ropeT, ropeF = _rope_tables()
    cmat = _const_mats()
    cols = np.zeros((128, 128), np.float32)
    cols[:, 0:8] = f(norm_mix_gain)[0].reshape(8, 128).T
    cols[:, 8:16] = f(norm_ffn_gain)[0].reshape(8, 128).T
    cols[:, 16:19] = f(mla_q_norm_gain)[0].reshape(3, 128).T
    cols[:, 19] = f(hg_norm_gain)[0]
    cw = f(conv_w)[0].reshape(3, NJ, 128)
    cols[:, 20:20 + 66] = cw.transpose(2, 1, 0).reshape(128, 66)
    cols[:, 86:86 + NJ] = f(conv_b)[0].reshape(NJ, 128).T
    bcs = np.zeros((4, 1024), np.float32)
    bcs[0] = f(final_norm_gain)
    bcs[1, 0:256] = f(mla_kv_norm_gain)[0]
    bcs[2, 0:512] = f(hg_lb_logits)[0]
    bcs[3, 0:512] = f(hg_lb_logits)[1]
    w_in_l = np.ascontiguousarray(f(w_in)[0].reshape(KC, 128, 2752).transpose(1, 0, 2))
    wq = f(w_uq)[0]
    sw = []
    for h in range(NH):
        sw.append(wq[:, h * 192 + 160:h * 192 + 192])
        sw.append(wq[:, h * 192 + 128:h * 192 + 160])
    wq_ext = np.concatenate([wq] + sw, axis=1)
    w_uq_l = np.ascontiguousarray(wq_ext.reshape(3, 128, 1024).transpose(1, 0, 2))
    wkv = np.concatenate([f(w_uk)[0].reshape(256, 512), f(w_uv)[0].reshape(256, 512)], axis=1)
    w_ukv_l = np.ascontiguousarray(wkv.reshape(2, 128, 1024).transpose(1, 0, 2))
    wo = f(w_out)[0]
    w_out_l = np.ascontiguousarray(wo.reshape(KC, 128, 2, 512).transpose(1, 2, 0, 3))
    wu = f(w_up)[0]
    wa = wu[:, :DFF].reshape(KC, 128, NJ, 128)
    wv = wu[:, DFF:].reshape(KC, 128, NJ, 128)
    wup_l = np.stack([wa, wv], axis=3)
    wup_l = wup_l.transpose(2, 1, 0, 3, 4).reshape(NJ, 128, 2048)
    wd_l = f(w_down)[0].reshape(NJ, 128, 1024)
    w_ffn_l = np.ascontiguousarray(np.concatenate([wup_l, wd_l], axis=2))
    b_ada2 = np.ascontiguousarray(np.broadcast_to(f(b_ada)[0][None, :], (2, 6 * D)))
    shared = dict(w_ada=f(w_ada)[0], b_ada2=b_ada2, cols=cols, bcs=bcs, cmat=cmat, ropeT=ropeT, ropeF=ropeF,
                  w_in_l=w_in_l, w_uq_l=w_uq_l, w_ukv_l=w_ukv_l, w_out_l=w_out_l, w_ffn_l=w_ffn_l)
    in_maps = []
    for b in range(n):
        c2 = np.stack([c_prompt[b], c_sample[b]], axis=0)
        c2l = np.ascontiguousarray(c2.reshape(2, KC, 128).transpose(2, 1, 0))
        sc = f(state_ffn_conv)[0, b]
        scl = np.ascontiguousarray(sc.reshape(2, NJ, 128).transpose(2, 1, 0))
        m = dict(shared)
        m.update(xp=x_prompt[b], xs=x_sample[b], c2=c2l, clat=f(cache_kv_latent)[0, b], ckpe=f(cache_k_rope)[0, b],
                 shg=f(state_hgrn)[0, b], sconv=scl)
        in_maps.append(m)
    build_nc()
    needed = {e: [] for e in ['pe', 'act', 'dve', 'pool']}
    for (e, i) in build_nc.used:
        needed[e].append(i)
    for e in needed:
        needed[e].sort()
    nc = build_nc(needed)
    res = run_bass_kernel_spmd(nc, in_maps, core_ids=list(range(n)))
    R = res.results

    def st(name):
        return np.stack([np.asarray(R[b][name], dtype=np.float32) for b in range(n)], axis=0)

    def cvfix(a):
        return np.ascontiguousarray(a.transpose(0, 3, 2, 1).reshape(n, 2, DFF))

    y_prompt = st("y_p")
    y_sample = st("y_s")
    return (y_prompt, y_sample, st("lat_p")[None], st("kpe_p")[None], st("hg_p")[None], cvfix(st("cv_p"))[None],
            st("lat_s")[None], st("kpe_s")[None], st("hg_s")[None], cvfix(st("cv_s"))[None])
```

```python
import bisect
import numpy as np
from contextlib import ExitStack
import concourse.bass as bass
import concourse.mybir as mybir
from concourse.bass_utils import run_bass_kernel_spmd

F32 = mybir.dt.float32
BF16 = mybir.dt.bfloat16
AF = mybir.ActivationFunctionType
ALU = mybir.AluOpType

D = 1024
KC = 8
S_P = 4096
S_S = 16
PAST = 4096
NH = 4
DFF = 2816
NJ = 22
EPS = 1e-6
MLA_SCALE = float((128 + 64) ** -0.5)
NST = 2
TP = NST * 128
SAME_ENGINE_SYNC = True


class Buf:
    def __init__(self, t, name):
        self.t = t
        self.name = name
        self.w = None
        self.r = {}
        self.lsem = None
        self.lcnt = 0
        self.ssem = None
        self.scnt = 0

    def __getitem__(self, k):
        return self.t[k]


class KB:
    def __init__(self, nc, es, needed=None):
        self.nc = nc
        self.es = es
        self.needed = needed
        self.used = set()
        self.eng = {'pe': nc.tensor, 'act': nc.scalar, 'dve': nc.vector, 'pool': nc.gpsimd, 'sp': nc.sync}
        self.sem = {}
        self.cnt = {}
        for e in ['pe', 'act', 'dve', 'pool']:
            self.sem[e] = es.enter_context(nc.semaphore('s_' + e))
            self.cnt[e] = 0
        self.seen = {e: {} for e in self.eng}
        self.needed_set = {e: set(v) for e, v in needed.items()} if needed is not None else None
        self.nsem = 4
        self.allbufs = []
        self.ninst = 0

    def newsem(self, name):
        self.nsem += 1
        return self.es.enter_context(self.nc.semaphore(name))

    def sb(self, name, shape, dt=F32):
        t = self.es.enter_context(self.nc.sbuf_tensor(name, list(shape), dt))
        b = Buf(t, name)
        self.allbufs.append(b)
        return b

    def ps(self, name, shape, dt=F32):
        t = self.es.enter_context(self.nc.psum_tensor(name, list(shape), dt))
        b = Buf(t, name)
        self.allbufs.append(b)
        return b

    def virt(self, name):
        b = Buf(None, name)
        self.allbufs.append(b)
        return b

    def _waits(self, eng, reads, writes):
        need = {}

        def add(ev):
            if ev is None:
                return
            sn, sem, val, src = ev
            if src == eng and (eng == 'pe' or not SAME_ENGINE_SYNC):
                return
            if self.seen[eng].get(sn, 0) >= val:
                return
            if sn not in need or need[sn][1] < val:
                need[sn] = (sem, val)

        for b in reads:
            add(b.w)
        for b in writes:
            add(b.w)
            for ev in b.r.values():
                add(ev)
        E = self.eng[eng]
        for sn, (sem, val) in need.items():
            v = val
            if sn.startswith('s_'):
                src = sn[2:]
                self.used.add((src, val))
                if self.needed is not None:
                    v = bisect.bisect_right(self.needed[src], val)
            E.wait_ge(sem, v)
            self.seen[eng][sn] = val
            self.ninst += 1

    def _record(self, ev, reads, writes):
        for b in writes:
            b.w = ev
            b.r = {}
        for b in reads:
            if b not in writes:
                b.r[ev[0]] = ev

    def op(self, eng, fn, reads=(), writes=()):
        self._waits(eng, reads, writes)
        ins = fn(self.eng[eng])
        self.cnt[eng] += 1
        if self.needed is None or self.cnt[eng] in self.needed_set[eng]:
            ins.then_inc(self.sem[eng], 1)
        self.ninst += 1
        ev = ('s_' + eng, self.sem[eng], self.cnt[eng], eng)
        self._record(ev, reads, writes)

    def dma(self, q, out, in_, reads=(), writes=(), semof=None, store=False, nowait_w=False, **kw):
        if nowait_w:
            self._waits(q, reads, ())
        else:
            self._waits(q, reads, writes)
        ins = self.eng[q].dma_start(out=out, in_=in_, **kw)
        self.ninst += 1
        if store:
            if semof.ssem is None:
                semof.ssem = self.newsem('ss_' + semof.name)
            semof.scnt += 16
            ins.then_inc(semof.ssem, 16)
            ev = ('ss_' + semof.name, semof.ssem, semof.scnt, 'dma')
        else:
            if semof.lsem is None:
                semof.lsem = self.newsem('ls_' + semof.name)
            semof.lcnt += 16
            ins.then_inc(semof.lsem, 16)
            ev = ('ls_' + semof.name, semof.lsem, semof.lcnt, 'dma')
        self._record(ev, reads, writes)
        return ev

    def finish(self):
        for b in self.allbufs:
            if b.ssem is not None and b.scnt > 0:
                self.nc.sync.wait_ge(b.ssem, b.scnt)
            if b.lsem is not None and b.lcnt > 0:
                self.nc.sync.wait_ge(b.lsem, b.lcnt)


def build_nc(needed=None):
    nc = bass.Bass("TRN2", target_bir_lowering=False)

    def din(name, shape, dt=F32):
        return nc.dram_tensor(name, list(shape), dt, kind="ExternalInput").ap()

    def dout(name, shape, dt=F32):
        return nc.dram_tensor(name, list(shape), dt, kind="ExternalOutput").ap()

    def dscr(name, shape, dt):
        return nc.dram_tensor(name, list(shape), dt, kind="Internal").ap()

    xp = din("xp", [S_P, D])
    xs_in = din("xs", [S_S, D])
    c2 = din("c2", [128, KC, 2])
    clat = din("clat", [PAST, 256])
    ckpe = din("ckpe", [PAST, 64])
    shg = din("shg", [NH, 128, 128])
    sconv = din("sconv", [128, NJ, 2])
    w_ada = din("w_ada", [D, 6 * D])
    b_ada2 = din("b_ada2", [2, 6 * D])
    cols = din("cols", [128, 128])
    bcs = din("bcs", [4, 1024])
    cmat = din("cmat", [128, 3, 128])
    ropeT = din("ropeT", [S_P + S_S, 128])
    ropeF = din("ropeF", [64, 2, S_P + S_S])
    w_in_l = din("w_in_l", [128, KC, 2752])
    w_uq_l = din("w_uq_l", [128, 3, 1024])
    w_ukv_l = din("w_ukv_l", [128, 2, 1024])
    w_out_l = din("w_out_l", [128, 2, KC, 512])
    w_ffn_l = din("w_ffn_l", [NJ, 128, 3072])

    y_p = dout("y_p", [S_P, D])
    y_s = dout("y_s", [S_S, D])
    lat_p = dout("lat_p", [S_P, 256])
    kpe_p = dout("kpe_p", [S_P, 64])
    hg_p = dout("hg_p", [NH, 128, 128])
    cv_p = dout("cv_p", [128, NJ, 2])
    lat_s = dout("lat_s", [S_S, 256])
    kpe_s = dout("kpe_s", [S_S, 64])
    hg_s = dout("hg_s", [NH, 128, 128])
    cv_s = dout("cv_s", [128, NJ, 2])

    modscr = dscr("modscr", [2, 6 * D], F32)
    win_b = dscr("win_b", [128, KC * 2752], BF16)
    wuq_b = dscr("wuq_b", [128, 3 * 1024], BF16)
    wukv_b = dscr("wukv_b", [128, 2 * 1024], BF16)
    wout_b = dscr("wout_b", [128, 2 * KC * 512], BF16)
    wffn_b = dscr("wffn_b", [NJ * 128, 3072], BF16)

    es = ExitStack()
    with es:
        nc_ctx = es.enter_context(nc.allow_non_contiguous_dma(reason="small layout loads"))
        es.enter_context(nc.allow_low_precision(reason="bf16 matmul operands"))
        kb = KB(nc, es, needed)
        op = kb.op

        cm = kb.sb("cm", [128, 3, 128])
        colsb = kb.sb("colsb", [128, 128])
        fg_bc = kb.sb("fg_bc", [128, 1024])
        kvn_bc = kb.sb("kvn_bc", [128, 256])
        lb_bc = kb.sb("lb_bc", [128, 512])
        maskU4 = kb.sb("maskU4", [128, 4, 128], BF16)
        ones_bf = kb.sb("ones_bf", [128, 128], BF16)
        mean_bf = kb.sb("mean_bf", [128, 128], BF16)
        mhalf = kb.sb("mhalf", [128, 1])
        c2s = kb.sb("c2s", [128, KC, 2])
        c2e = kb.sb("c2e", [128, KC, 2])
        modc = [kb.sb("modc%d" % r, [128, 4, KC]) for r in range(2)]
        G1c = [kb.sb("G1c%d" % r, [128, KC]) for r in range(2)]
        G2c = [kb.sb("G2c%d" % r, [128, KC]) for r in range(2)]
        g1_bc = kb.sb("g1_bc", [128, 1024])
        g2_bc = kb.sb("g2_bc", [128, 1024])
        ident = cm.t[:, 0, :]
        Umat = cm.t[:, 1, :]
        Uxmat = cm.t[:, 2, :]
        GMIX, GFFN, QN, HGG, CW, CB = 0, 8, 16, 19, 20, 86
        KT = kb.sb("KT", [128, NH, PAST], BF16)
        Vr = kb.sb("Vr", [128, PAST // 128, 512], BF16)
        kpeT = kb.sb("kpeT", [128, PAST], BF16)
        KTs = kb.sb("KTs", [128, NH, S_S], BF16)
        Vs = kb.sb("Vs", [S_S, 512], BF16)
        kpeTs = kb.sb("kpeTs", [128, S_S], BF16)
        Sst = kb.sb("Sst", [128, NH, 128])
        Sbf = kb.sb("Sbf", [128, NH, 128], BF16)
        aprev = kb.sb("aprev", [128, NJ, 2])
        NPOOL = 4
        wpool = [kb.sb("wp%d" % i, [128, 4096], BF16) for i in range(NPOOL)]
        wpi = [0]

        def wnext():
            b = wpool[wpi[0] % NPOOL]
            wpi[0] += 1
            return b

        xbuf = [kb.sb("xb%d" % i, [128, NST, D]) for i in range(2)]
        xparts = []
        for xb_ in xbuf:
            ps_list = []
            for st_ in range(NST):
                pb_ = Buf(xb_.t, xb_.name + "s%d" % st_)
                kb.allbufs.append(pb_)
                ps_list.append(pb_)
            xparts.append(ps_list)
        xsb = kb.sb("xsb", [128, D])
        hTs = [kb.sb("hT%d" % i, [128, KC, TP], BF16) for i in range(2)]
        mixT = kb.sb("mixT", [128, KC, TP], BF16)
        stat = kb.sb("stat", [128, 8])
        tA = [kb.sb("tA%d" % i, [128, 512]) for i in range(2)]
        tB = [kb.sb("tB%d" % i, [128, 512]) for i in range(2)]
        tC = [kb.sb("tC%d" % i, [128, 512]) for i in range(2)]
        tD = kb.sb("tD", [128, 512])
        khat = kb.sb("khat", [128, NST, 512], BF16)
        vv = kb.sb("vv", [128, NST, 512], BF16)
        eb = kb.sb("eb", [128, NH, TP])
        ktT = kb.sb("ktT", [128, NH, TP], BF16)
        qtT = kb.sb("qtT", [128, NH, TP], BF16)
        scm = kb.sb("scm", [128, NH, 128], BF16)
        sqb = kb.sb("sqb", [128, 2, TP], BF16)
        sg = tA[0]
        rs = tA[1]
        cqnT = kb.sb("cqnT", [128, 3, TP], BF16)
        latT = kb.sb("latT", [128, 2, TP], BF16)
        qnT = kb.sb("qnT", [128, NH, TP], BF16)
        qpeT = kb.sb("qpeT", [128, NH, TP], BF16)
        ropeTs = kb.sb("ropeTs", [128, NST, 128])
        ropeFs = kb.sb("ropeFs", [64, 2, TP])
        lato1 = kb.sb("lato0", [128, NST, 256])
        kpeo1 = kb.sb("kpeo0", [128, NST, 64])
        lato = [lato1, lato1]
        kpeo = [kpeo1, kpeo1]
        pT = [kb.sb("pT%d" % i, [128, TP], BF16) for i in range(4)]
        aS = [kb.sb("aS%d" % i, [128, TP + 2]) for i in range(2)]
        cc = [kb.sb("cc%d" % i, [128, TP]) for i in range(2)]
        uT = [kb.sb("uT%d" % i, [128, TP], BF16) for i in range(3)]
        PSB = [kb.ps("psb%d" % i, [128, 512]) for i in range(8)]
        rot = [0]

        def pnext():
            b = PSB[rot[0] % 4]
            rot[0] += 1
            return b

        LB = PSB[4:8]
        wcvA = kb.virt("wcvA")
        wcvC = kb.virt("wcvC")
        wcv0 = kb.virt("wcv0")
        wcvB = kb.virt("wcvB")
        cst = kb.virt("cst")
        modv = kb.virt("modv")

        const_bufs = []

        def cload(buf, out_ap, in_ap):
            kb.dma('sp', out_ap, in_ap, reads=(), writes=(), semof=cst, nowait_w=True)
            const_bufs.append(buf)

        cload(cm, cm.t[:], cmat)
        cload(colsb, colsb.t[:], cols)
        cload(fg_bc, fg_bc.t[:], bcs[0, :].partition_broadcast(128))
        cload(kvn_bc, kvn_bc.t[:], bcs[1, 0:256].partition_broadcast(128))
        cload(tD, tD.t[:], bcs[2, 0:512].partition_broadcast(128))
        cload(lb_bc, lb_bc.t[:], bcs[3, 0:512].partition_broadcast(128))
        cload(c2s, c2s.t[:], c2)
        cev = ('ls_cst', cst.lsem, cst.lcnt, 'dma')
        for b in const_bufs:
            b.w = cev

        def wcast(dst2, src2, virt):
            R = dst2.shape[0]
            for r0 in range(0, R, 512):
                r1 = min(R, r0 + 512)
                kb.dma('pool', dst2[r0:r1, :], src2[r0:r1, :], reads=(), writes=(), semof=virt, nowait_w=True)

        wcast(wukv_b.rearrange("p (a b) -> (p a) b", b=1024), w_ukv_l.rearrange("p k c -> (p k) c"), wcv0)
        wcv0.w = ('ls_wcv0', wcv0.lsem, wcv0.lcnt, 'dma')
        wcast(win_b.rearrange("p (a b) -> (p a) b", b=1376), w_in_l.rearrange("p k c -> p (k c)").rearrange("p (a b) -> (p a) b", b=1376), wcvA)
        wcvA.w = ('ls_wcvA', wcvA.lsem, wcvA.lcnt, 'dma')
        op('pool', lambda e: e.memset(ones_bf.t[:], 1.0), (), (ones_bf,))
        op('pool', lambda e: e.memset(mean_bf.t[:], 1.0 / 128.0), (), (mean_bf,))
        op('pool', lambda e: e.memset(mhalf.t[:], -0.5), (), (mhalf,))
        op('pool', lambda e: e.memset(kpeT.t[64:128, :], 0.0), (), (kpeT,))
        op('pool', lambda e: e.memset(kpeTs.t[64:128, :], 0.0), (), (kpeTs,))
        op('pool', lambda e: e.memset(qpeT.t[64:128, :, :], 0.0), (), (qpeT,))
        for h in range(NH):
            op('pool', lambda e, h=h: e.tensor_copy(maskU4.t[:, h, :], Umat), (cm,), (maskU4,))
        op('dve', lambda e: e.tensor_tensor(lb_bc.t[:], lb_bc.t[:], tD.t[:], ALU.subtract), (lb_bc, tD), (lb_bc,))
        op('act', lambda e: e.activation(lb_bc.t[:], lb_bc.t[:], AF.Exp), (lb_bc,), (lb_bc,))
        op('dve', lambda e: e.tensor_scalar(lb_bc.t[:], lb_bc.t[:], 1.0, None, ALU.add), (lb_bc,), (lb_bc,))
        op('dve', lambda e: e.reciprocal(lb_bc.t[:], lb_bc.t[:]), (lb_bc,), (lb_bc,))
        op('act', lambda e: e.activation(c2e.t[:], c2s.t[:], AF.Exp, scale=-1.0), (c2s,), (c2e,))
        op('dve', lambda e: e.tensor_scalar(c2e.t[:], c2e.t[:], 1.0, None, ALU.add), (c2e,), (c2e,))
        op('dve', lambda e: e.reciprocal(c2e.t[:], c2e.t[:]), (c2e,), (c2e,))
        op('dve', lambda e: e.tensor_tensor(c2e.t[:], c2e.t[:], c2s.t[:], ALU.mult), (c2e, c2s), (c2e,))
        mod_steps = []

        def mk_mod_step(cb, k):
            def step():
                if k == 0:
                    kb.dma('sp', tA[0].t[0:2, :], b_ada2[:, cb * 1024:cb * 1024 + 512], (), (tA[0],), semof=tA[0])
                    kb.dma('sp', tA[1].t[0:2, :], b_ada2[:, cb * 1024 + 512:(cb + 1) * 1024], (), (tA[1],), semof=tA[1])
                wb = wnext()
                wv = wb.t[:, 0:2048].bitcast(F32)
                kb.dma('sp', wv, w_ada[k * 128:(k + 1) * 128, cb * 1024:(cb + 1) * 1024], (), (wb,), semof=wb)
                for i in range(2):
                    op('pe', lambda e, i=i: e.matmul(LB[i].t[0:2, :], c2e.t[:, k, :], wv[:, i * 512:(i + 1) * 512],
                                                     start=(k == 0), stop=(k == KC - 1)),
                       (c2e, wb), (LB[i],))
                if k == KC - 1:
                    for i in range(2):
                        op('dve', lambda e, i=i: e.tensor_tensor(xsb.t[0:2, i * 512:(i + 1) * 512], LB[i].t[0:2, :], tA[i].t[0:2, :], ALU.add),
                           (LB[i], tA[i]), (xsb,))
                    kb.dma('sp', modscr[:, cb * 1024:(cb + 1) * 1024], xsb.t[0:2, :], (xsb,), (modv,), semof=modv, nowait_w=True)
            return step

        for cb in range(6):
            for k in range(KC):
                mod_steps.append(mk_mod_step(cb, k))

        def mod_finish():
            mt = tA[0]
            kb.dma('sp', mt.t[0:96, 0:128], modscr.rearrange("r (m k p) -> (r m k) p", m=6, k=KC), (modv,), (mt,), semof=mt)
            pb = LB[0]
            op('pe', lambda e: e.transpose(pb.t[:, 0:96], mt.t[0:96, 0:128], ident[0:96, 0:96]), (mt, cm), (pb,))
            for r in range(2):
                pv4 = pb.t[:, r * 48:(r + 1) * 48].rearrange("p (m k) -> p m k", m=6)
                op('act', lambda e, r=r, pv4=pv4: e.activation(modc[r].t[:, 0:2, :], pv4[:, 0:2, :], AF.Copy), (pb,), (modc[r],))
                op('act', lambda e, r=r, pv4=pv4: e.activation(modc[r].t[:, 2:4, :], pv4[:, 3:5, :], AF.Copy), (pb,), (modc[r],))
                op('dve', lambda e, r=r: e.scalar_tensor_tensor(G1c[r].t[:], modc[r].t[:, 1, :], 1.0, colsb.t[:, GMIX:GMIX + 8], ALU.add, ALU.mult),
                   (modc[r], colsb), (G1c[r],))
                op('dve', lambda e, r=r: e.scalar_tensor_tensor(G2c[r].t[:], modc[r].t[:, 3, :], 1.0, colsb.t[:, GFFN:GFFN + 8], ALU.add, ALU.mult),
                   (modc[r], colsb), (G2c[r],))

        def load_gbc(r, P):
            kb.dma('sp', g1_bc.t[0:P, :], modscr[r, 2048:3072].partition_broadcast(P), (modv,), (g1_bc,), semof=g1_bc)
            kb.dma('sp', g2_bc.t[0:P, :], modscr[r, 5120:6144].partition_broadcast(P), (modv,), (g2_bc,), semof=g2_bc)

        def wload(src_ap, ncols, virt):
            wb = wnext()
            kb.dma('sp', wb.t[:, 0:ncols], src_ap, (virt,), (wb,), semof=wb)
            return wb

        win3 = win_b.rearrange("p (k c) -> p k c", k=KC)

        def load_win(c0, c1):
            wb = wnext()
            n = c1 - c0
            kb.dma('sp', wb.t[:, 0:KC * n].rearrange("p (k c) -> p k c", k=KC), win3[:, :, c0:c1], (wcvA,), (wb,), semof=wb)
            return wb, wb.t[:, 0:KC * n].rearrange("p (k c) -> p k c", k=KC)

        def rstd_from(col, P, N):
            op('pool', lambda e: e.tensor_scalar(stat.t[0:P, col:col + 1], stat.t[0:P, col:col + 1], 1.0 / N, EPS, ALU.mult, ALU.add),
               (stat,), (stat,))
            op('pool', lambda e: e.tensor_tensor(stat.t[0:P, col:col + 1], stat.t[0:P, col:col + 1], mhalf.t[0:P, 0:1], ALU.pow),
               (stat, mhalf), (stat,))

        def norm_T(xb, st, P, Gc, shc_buf, shc_idx, dst, T0, bank=None):
            norm_T_a(xb, st, P)
            norm_T_b(st, P, Gc, shc_buf, shc_idx, dst, T0, bank)

        def xs_view(xs_):
            if xs_ is xsb:
                return xsb.t[:, :]
            return xs_.t[:, :, :].rearrange("p a b -> p (a b)").bitcast(F32)

        def norm_T_a(xb, st, P, use_pool=False, scol=0, xs_=None):
            xs_ = xsb if xs_ is None else xs_
            xsv = xs_view(xs_)
            xv = xb.t[0:P, st, :]
            op('act', lambda e: e.activation(xsv[0:P, :], xv, AF.Square, accum_out=stat.t[0:P, scol:scol + 1]), (xb,), (xs_, stat))
            rstd_from(scol, P, D)
            op('act', lambda e: e.activation(xsv[0:P, :], xv, AF.Copy, scale=stat.t[0:P, scol:scol + 1]), (xb, stat), (xs_,))

        def norm_T_b(st, P, Gc, shc_buf, shc_idx, dst, T0, bank=None, halves=(0, 1), part=None, xs_=None):
            xs_ = xsb if xs_ is None else xs_
            xsv = xs_view(xs_)
            for half in halves:
                pb = pnext() if bank is None else bank
                if part in (None, 'pe'):
                    for kk in range(4):
                        k = half * 4 + kk
                        op('pe', lambda e, k=k, kk=kk, pb=pb: e.transpose(pb.t[:, kk * P:(kk + 1) * P], xsv[0:P, k * 128:(k + 1) * 128], ident[0:P, 0:P]),
                           (xs_, cm), (pb,))
                if part == 'pe':
                    continue
                for kk in range(4):
                    k = half * 4 + kk
                    eng = 'act' if kk % 2 == 0 else 'dve'
                    if eng == 'act':
                        op('act', lambda e, k=k, kk=kk, pb=pb: e.activation(dst.t[:, k, T0:T0 + P], pb.t[:, kk * P:(kk + 1) * P], AF.Identity,
                                                                         scale=Gc.t[:, k:k + 1], bias=shc_buf.t[:, shc_idx, k:k + 1]),
                           (pb, Gc, shc_buf), (dst,))
                    else:
                        op('dve', lambda e, k=k, kk=kk, pb=pb: e.tensor_scalar(dst.t[:, k, T0:T0 + P], pb.t[:, kk * P:(kk + 1) * P],
                                                                            Gc.t[:, k:k + 1], shc_buf.t[:, shc_idx, k:k + 1], ALU.mult, ALU.add),
                           (pb, Gc, shc_buf), (dst,))

        def kv_project(T, wkv3, wkvb, KT_dst, V_dst_fn, nst, P):
            for hp in range(2):
                pb = pnext()
                for hh in range(2):
                    h = hp * 2 + hh
                    for kc in range(2):
                        op('pe', lambda e, h=h, hh=hh, kc=kc, pb=pb: e.matmul(pb.t[:, hh * T:(hh + 1) * T], wkv3[:, kc, h * 128:(h + 1) * 128], latT.t[:, kc, 0:T],
                                                                          start=(kc == 0), stop=(kc == 1)),
                           (wkvb, latT), (pb,))
                for hh in range(2):
                    h = hp * 2 + hh
                    dst_ap, dst_buf = KT_dst(h)
                    op('act', lambda e, hh=hh, pb=pb, dst_ap=dst_ap: e.activation(dst_ap, pb.t[:, hh * T:(hh + 1) * T], AF.Copy), (pb,), (dst_buf,))
            for st in range(nst):
                pb = pnext()
                for kc in range(2):
                    op('pe', lambda e, kc=kc, pb=pb, st=st: e.matmul(pb.t[0:P, :], latT.t[:, kc, st * P:(st + 1) * P], wkv3[:, kc, 512:1024],
                                                                 start=(kc == 0), stop=(kc == 1)),
                       (wkvb, latT), (pb,))
                dst_ap, dst_buf = V_dst_fn(st)
                op('dve', lambda e, pb=pb, dst_ap=dst_ap: e.tensor_copy(dst_ap, pb.t[0:P, :]), (pb,), (dst_buf,))

        def attend(NQ, keyblocks):
            nb = len(keyblocks)
            items = [(h, bi) for h in range(NH) for bi in range(nb)]
            sTs = {}
            ps_ = {}

            def scores(i):
                h, bi = items[i]
                KT_fn, kpe_ap, kpe_buf, V_fn, nk, c0, maskq = keyblocks[bi]
                sT = pnext()
                kt_ap, kt_buf = KT_fn(h)
                n = NQ - c0
                op('pe', lambda e: e.matmul(sT.t[0:nk, 0:n], kt_ap, qnT.t[:, h, c0:NQ], start=True, stop=False), (kt_buf, qnT), (sT,))
                op('pe', lambda e: e.matmul(sT.t[0:nk, 0:n], kpe_ap, qpeT.t[:, h, c0:NQ], start=False, stop=True), (kpe_buf, qpeT), (sT,))
                sTs[i] = sT

            def expo(i):
                h, bi = items[i]
                KT_fn, kpe_ap, kpe_buf, V_fn, nk, c0, maskq = keyblocks[bi]
                n = NQ - c0
                sT = sTs.pop(i)
                p = pT[i % len(pT)]
                op('act', lambda e: e.activation(p.t[0:nk, 0:n], sT.t[0:nk, 0:n], AF.Exp, scale=MLA_SCALE), (sT,), (p,))
                if maskq:
                    op('pool', lambda e: e.memset(p.t[64:128, 0:64], 0.0), (), (p,))
                ps_[i] = p

            def pv(i):
                h, bi = items[i]
                KT_fn, kpe_ap, kpe_buf, V_fn, nk, c0, maskq = keyblocks[bi]
                n = NQ - c0
                p = ps_.pop(i)
                oT = LB[0 + 2 * (h % 2)]
                sm = LB[1 + 2 * (h % 2)]
                v_ap, v_buf = V_fn(h)
                op('pe', lambda e: e.matmul(oT.t[:, c0:NQ], v_ap, p.t[0:nk, 0:n], start=(bi == 0), stop=(bi == nb - 1)), (v_buf, p), (oT,))
                op('pe', lambda e: e.matmul(sm.t[:, c0:NQ], ones_bf.t[0:nk, :], p.t[0:nk, 0:n], start=(bi == 0), stop=(bi == nb - 1)), (ones_bf, p), (sm,))
                if bi == nb - 1:
                    rc = tD
                    op('dve', lambda e: e.reciprocal(rc.t[:, 0:NQ], sm.t[:, 0:NQ]), (sm,), (rc,))
                    op('dve', lambda e: e.tensor_tensor(mixT.t[:, 4 + h, 0:NQ], oT.t[:, 0:NQ], rc.t[:, 0:NQ], ALU.mult), (oT, rc), (mixT,))

            NI = len(items)
            AHEAD = 3
            for i in range(min(AHEAD, NI)):
                scores(i)
            for i in range(NI):
                expo(i)
                if i + AHEAD < NI:
                    scores(i + AHEAD)
                pv(i)

        prefetched = set()

        def prefetch(seq, ti, g, bank=None, step=None):
            P = seq['P']
            nst = seq['nst']
            T = P * nst
            r = seq['r']
            xb = xbuf[g % 2]
            hT = hTs[g % 2]
            tok0 = ti * T
            pos0 = seq['pos0'] + tok0
            if step is None or step == 'dma':
                kb.dma('sp', xb.t[0:P, 0:nst, :], seq['x'][tok0:tok0 + T, :].rearrange("(n p) d -> p n d", p=P), (), tuple(xparts[g % 2]), semof=xb)
                kb.dma('sp', ropeTs.t[0:P, 0:nst, :], ropeT[pos0:pos0 + T, :].rearrange("(n p) c -> p n c", p=P), (), (ropeTs,), semof=ropeTs)
                kb.dma('sp', ropeFs.t[:, :, 0:T], ropeF[:, :, pos0:pos0 + T], (), (ropeFs,), semof=ropeFs)
            for st in range(nst):
                if step is None or step == ('a', st):
                    norm_T_a(xparts[g % 2][st], st, P, use_pool=False)
                if step is None or step == ('b', st):
                    norm_T_b(st, P, G1c[r], modc[r], 0, hT, st * P, bank)
                for hf_ in range(2):
                    for part in ('pe', 'ev'):
                        if step == ('bh', st, hf_, part):
                            norm_T_b(st, P, G1c[r], modc[r], 0, hT, st * P, bank, halves=(hf_,), part=part)
            if step is None or step == ('b', nst - 1) or step == ('bh', nst - 1, 1, 'ev'):
                prefetched.add(g)

        preloaded_w = {}
        pending_tail = []

        def run_tile(seq, ti, g, nxt=None):
            P = seq['P']
            nst = seq['nst']
            T = P * nst
            C = seq['C']
            nch = P // C
            r = seq['r']
            xb = xbuf[g % 2]
            xps = xparts[g % 2]
            hT = hTs[g % 2]
            tok0 = ti * T
            pos0 = seq['pos0'] + tok0
            last = (ti == seq['ntiles'] - 1)
            if g not in prefetched:
                prefetch(seq, ti, g)
            if g in preloaded_w:
                (wb_f, wf3), (wb_i, wi3), (wb_c, wc3), (wb_c2, wc3b) = preloaded_w.pop(g)
            else:
                wb_f, wf3 = load_win(512, 1024)
                wb_i, wi3 = load_win(1024, 1536)
                wb_c, wc3 = load_win(2048, 2432)
                wb_c2, wc3b = load_win(2432, 2752)
            lo = lato[0]
            ko = kpeo[0]

            def rotator(banks):
                cnt_ = [0]

                def nx():
                    b = banks[cnt_[0] % len(banks)]
                    cnt_[0] += 1
                    return b
                return nx

            def run_chains(chains):
                n = max(len(c) for c in chains)
                for s_ in range(n):
                    for c in chains:
                        if s_ < len(c):
                            c[s_]()

            def H_chain(st, banks):
                a, b_, c_ = tA[st], tB[st], tC[st]
                pn = rotator(banks)
                S = {}

                def s0():
                    pf = pn()
                    for k in range(KC):
                        op('pe', lambda e, k=k: e.matmul(pf.t[0:P, :], hT.t[:, k, st * P:(st + 1) * P], wf3[:, k, :], start=(k == 0), stop=(k == KC - 1)),
                           (hT, wb_f), (pf,))
                    op('act', lambda e: e.activation(a.t[0:P, :], pf.t[0:P, :], AF.Exp, scale=-1.0), (pf,), (a,))
                    pi = S['pi'] = pn()
                    for k in range(KC):
                        op('pe', lambda e, k=k: e.matmul(pi.t[0:P, :], hT.t[:, k, st * P:(st + 1) * P], wi3[:, k, :], start=(k == 0), stop=(k == KC - 1)),
                           (hT, wb_i), (pi,))

                def s1():
                    pi = S['pi']
                    op('pool', lambda e: e.tensor_tensor(b_.t[0:P, :], a.t[0:P, :], lb_bc.t[0:P, :], ALU.mult), (a, lb_bc), (b_,))
                    op('act', lambda e: e.activation(vv.t[0:P, st, :], pi.t[0:P, :], AF.Copy), (pi,), (vv,))

                def s2():
                    op('act', lambda e: e.activation(b_.t[0:P, :], b_.t[0:P, :], AF.Ln, bias=1.0), (b_,), (b_,))
                    op('act', lambda e: e.activation(a.t[0:P, :], a.t[0:P, :], AF.Ln, bias=1.0), (a,), (a,))

                def s3():
                    op('dve', lambda e: e.tensor_tensor(c_.t[0:P, :], b_.t[0:P, :], a.t[0:P, :], ALU.subtract), (a, b_), (c_,))

                def s4():
                    op('act', lambda e: e.activation(a.t[0:P, :], c_.t[0:P, :], AF.Exp), (c_,), (a,))
                    pbb = S['pbb'] = pn()
                    op('pe', lambda e: e.matmul(pbb.t[0:P, :], Umat[0:P, 0:P], c_.t[0:P, :], start=True, stop=True), (cm, c_), (pbb,))
                    pdd = S['pdd'] = pn()
                    op('pe', lambda e: e.matmul(pdd.t[0:P, :], Uxmat[0:P, 0:P], c_.t[0:P, :], start=True, stop=True), (cm, c_), (pdd,))

                def s5():
                    pbb, pdd = S['pbb'], S['pdd']
                    op('pool', lambda e: e.tensor_scalar(a.t[0:P, :], a.t[0:P, :], -1.0, 1.0, ALU.mult, ALU.add), (a,), (a,))
                    op('act', lambda e: e.activation(b_.t[0:P, :], pbb.t[0:P, :], AF.Exp, scale=-1.0), (pbb,), (b_,))
                    op('act', lambda e: e.activation(pdd.t[0:P, :], pdd.t[0:P, :], AF.Exp), (pdd,), (pdd,))

                def s6():
                    pdd = S['pdd']
                    op('dve', lambda e: e.tensor_tensor(b_.t[0:P, :], b_.t[0:P, :], a.t[0:P, :], ALU.mult), (a, b_), (b_,))
                    op('dve', lambda e: e.tensor_tensor(khat.t[0:P, st, :], pdd.t[0:P, :], a.t[0:P, :], ALU.mult), (a, pdd), (khat,))
                    pbt = S['pbt'] = pn()
                    for h in range(NH):
                        op('pe', lambda e, h=h: e.matmul(pbt.t[:, h * P:(h + 1) * P], c_.t[0:P, h * 128:(h + 1) * 128], Umat[0:P, 0:P], start=True, stop=True),
                           (c_, cm), (pbt,))

                def s7():
                    pbt = S['pbt']
                    op('act', lambda e: e.activation(eb.t[:, :, st * P:(st + 1) * P], pbt.t[:, 0:NH * P].rearrange("p (h t) -> p h t", h=NH), AF.Exp),
                       (pbt,), (eb,))
                    pkt = S['pkt'] = pn()
                    for h in range(NH):
                        op('pe', lambda e, h=h: e.transpose(pkt.t[:, h * P:(h + 1) * P], b_.t[0:P, h * 128:(h + 1) * 128], ident[0:P, 0:P]),
                           (b_, cm), (pkt,))

                def s8():
                    pkt = S['pkt']
                    op('dve', lambda e: e.tensor_copy(ktT.t[:, :, st * P:(st + 1) * P], pkt.t[:, 0:NH * P].rearrange("p (h t) -> p h t", h=NH)),
                       (pkt,), (ktT,))

                return [s0, s1, s2, s3, s4, s5, s6, s7, s8]

            def M_chain(st, banks, xs_buf, c1, c2):
                pn = rotator(banks)
                S = {}
                tcs = tC[st]
                tc_, ts_ = cc[0], cc[1]

                def m0():
                    pc1 = S['pc1'] = pn()
                    for k in range(KC):
                        op('pe', lambda e, k=k: e.matmul(pc1.t[0:P, 0:384], hT.t[:, k, st * P:(st + 1) * P], wc3[:, k, 0:384], start=(k == 0), stop=(k == KC - 1)),
                           (hT, wb_c), (pc1,))
                    op('act', lambda e: e.activation(sqb.t[0:P, :, :].rearrange("p h t -> p (h t)")[:, 0:384], pc1.t[0:P, 0:384], AF.Square, accum_out=stat.t[0:P, c1:c1 + 1]),
                       (pc1,), (sqb, stat))
                    pc2 = S['pc2'] = pn()
                    for k in range(KC):
                        op('pe', lambda e, k=k: e.matmul(pc2.t[0:P, 0:320], hT.t[:, k, st * P:(st + 1) * P], wc3b[:, k, 0:320], start=(k == 0), stop=(k == KC - 1)),
                           (hT, wb_c2), (pc2,))

                def m1():
                    rstd_from(c1, P, 384)

                def m2():
                    pc1 = S['pc1']
                    op('act', lambda e: e.activation(xs_buf.t[0:P, 0:384], pc1.t[0:P, 0:384], AF.Copy, scale=stat.t[0:P, c1:c1 + 1]), (pc1, stat), (xs_buf,))
                    pc2 = S['pc2']
                    op('act', lambda e: e.activation(sqb.t[0:P, :, :].rearrange("p h t -> p (h t)")[:, 0:256], pc2.t[0:P, 0:256], AF.Square, accum_out=stat.t[0:P, c2:c2 + 1]),
                       (pc2,), (sqb, stat))

                def m3():
                    rstd_from(c2, P, 256)
                    pt = S['pt'] = pn()
                    for kc in range(3):
                        op('pe', lambda e, kc=kc: e.transpose(pt.t[:, kc * P:(kc + 1) * P], xs_buf.t[0:P, kc * 128:(kc + 1) * 128], ident[0:P, 0:P]), (xs_buf, cm), (pt,))

                def m4():
                    pt, pc2 = S['pt'], S['pc2']
                    for kc in range(3):
                        op('act', lambda e, kc=kc: e.activation(cqnT.t[:, kc, st * P:(st + 1) * P], pt.t[:, kc * P:(kc + 1) * P], AF.Copy, scale=colsb.t[:, QN + kc:QN + kc + 1]),
                           (pt, colsb), (cqnT,))
                    op('dve', lambda e: e.scalar_tensor_tensor(lo.t[0:P, st, :], pc2.t[0:P, 0:256], stat.t[0:P, c2:c2 + 1], kvn_bc.t[0:P, :], ALU.mult, ALU.mult),
                       (pc2, stat, kvn_bc), (lo,))
                    o0 = st * 128
                    op('dve', lambda e: e.tensor_tensor(tc_.t[0:P, o0:o0 + 64], pc2.t[0:P, 256:320], ropeTs.t[0:P, st, 0:64], ALU.mult), (pc2, ropeTs), (tc_,))
                    op('dve', lambda e: e.tensor_tensor(ts_.t[0:P, o0:o0 + 64], pc2.t[0:P, 256:320], ropeTs.t[0:P, st, 64:128], ALU.mult), (pc2, ropeTs), (ts_,))

                def m5():
                    o0 = st * 128
                    pt2 = S['pt2'] = pn()
                    for kc in range(2):
                        op('pe', lambda e, kc=kc: e.transpose(pt2.t[:, kc * P:(kc + 1) * P], lo.t[0:P, st, kc * 128:(kc + 1) * 128], ident[0:P, 0:P]), (lo, cm), (pt2,))
                    op('pool', lambda e: e.tensor_tensor(ko.t[0:P, st, 0:32], tc_.t[0:P, o0:o0 + 32], ts_.t[0:P, o0 + 32:o0 + 64], ALU.subtract), (tc_, ts_), (ko,))
                    op('pool', lambda e: e.tensor_tensor(ko.t[0:P, st, 32:64], tc_.t[0:P, o0 + 32:o0 + 64], ts_.t[0:P, o0:o0 + 32], ALU.add), (tc_, ts_), (ko,))

                def m6():
                    pt2 = S['pt2']
                    op('act', lambda e: e.activation(latT.t[:, :, st * P:(st + 1) * P], pt2.t[:, 0:2 * P].rearrange("p (k t) -> p k t", k=2), AF.Copy), (pt2,), (latT,))
                    pt3 = S['pt3'] = pn()
                    op('pe', lambda e: e.transpose(pt3.t[0:64, 0:P], ko.t[0:P, st, :], ident[0:P, 0:P]), (ko, cm), (pt3,))

                def m7():
                    pt3 = S['pt3']
                    kdst_ap, kdst_buf = seq['kpe_dst'](tok0 + st * P, P)
                    op('act', lambda e: e.activation(kdst_ap, pt3.t[0:64, 0:P], AF.Copy), (pt3,), (kdst_buf,))

                return [m0, m1, m2, m3, m4, m5, m6, m7]

            chains = []
            ptail = pending_tail.pop() if pending_tail else None
            if ptail is not None:
                chains.append([lambda: None, ptail])
            for st in range(nst):
                chains.append(H_chain(st, [PSB[2 * st], PSB[2 * st + 1]]))
                mch = M_chain(st, [PSB[4 + 2 * st], PSB[5 + 2 * st]], xsb if st == 0 else tD, 1 + 3 * st, 2 + 3 * st)
                if ptail is not None:
                    mch = [lambda: None] + mch
                chains.append(mch)
            run_chains(chains)
            wb_q, wq3 = load_win(0, 512)
            wb_g, wg3 = load_win(1536, 2048)
            wb_u = wload(wuq_b, 3072, wcvC)
            wu3 = wb_u.t[:, 0:3072].rearrange("p (k c) -> p k c", k=3)
            wb_kv = wload(wukv_b, 2048, wcvA)
            wkv3 = wb_kv.t[:, 0:2048].rearrange("p (k c) -> p k c", k=2)
            kb.dma('sp', seq['lat_out'][tok0:tok0 + T, :].rearrange("(n p) c -> p n c", p=P), lo.t[0:P, 0:nst, :], (lo,), (), semof=lo, store=True)
            kb.dma('sp', seq['kpe_out'][tok0:tok0 + T, :].rearrange("(n p) c -> p n c", p=P), ko.t[0:P, 0:nst, :], (ko,), (), semof=ko, store=True)
            RB = rotator([PSB[2], PSB[3]])
            for hp in range(2):
                pq = RB()
                for hh in range(2):
                    h = hp * 2 + hh
                    for k in range(KC):
                        op('pe', lambda e, k=k, h=h, hh=hh: e.matmul(pq.t[:, hh * T:(hh + 1) * T], wq3[:, k, h * 128:(h + 1) * 128], hT.t[:, k, 0:T],
                                                                  start=(k == 0), stop=(k == KC - 1)),
                           (hT, wb_q), (pq,))
                op('dve', lambda e: e.scalar_tensor_tensor(qtT.t[:, hp * 2:hp * 2 + 2, 0:T], pq.t[:, 0:2 * T].rearrange("p (h t) -> p h t", h=2),
                                                           float(128 ** -0.5), eb.t[:, hp * 2:hp * 2 + 2, 0:T], ALU.mult, ALU.mult),
                   (pq, eb), (qtT,))

            oTb = [PSB[0], PSB[1]]
            sgX = [tA[0], tA[1]]
            rsX = [tC[0], tC[1]]

            def SEQ_chain():
                stages = []
                for st in range(nst):
                    def q0(st=st):
                        psc = PSB[2]
                        for h in range(NH):
                            op('pe', lambda e, h=h: e.matmul(psc.t[0:P, h * P:(h + 1) * P], ktT.t[:, h, st * P:(st + 1) * P], qtT.t[:, h, st * P:(st + 1) * P],
                                                             start=True, stop=True), (ktT, qtT), (psc,))
                        op('dve', lambda e: e.tensor_tensor(scm.t[0:P, :, 0:P], psc.t[0:P, 0:NH * P].rearrange("p (h t) -> p h t", h=NH), maskU4.t[0:P, :, 0:P], ALU.mult),
                           (psc, maskU4), (scm,))
                    stages.append(q0)
                    for ch in range(nch):
                        def cA(st=st, ch=ch):
                            r0 = ch * C
                            t0 = st * P + ch * C
                            pkv = PSB[3]
                            for h in range(NH):
                                op('pe', lambda e, h=h: e.matmul(pkv.t[:, h * 128:(h + 1) * 128], khat.t[r0:r0 + C, st, h * 128:(h + 1) * 128], vv.t[r0:r0 + C, st, h * 128:(h + 1) * 128],
                                                                 start=True, stop=True), (khat, vv), (pkv,))
                            for h in range(NH):
                                ob = oTb[h // 2]
                                oc = (h % 2) * T + t0
                                op('pe', lambda e, h=h, ob=ob, oc=oc: e.matmul(ob.t[:, oc:oc + C], Sbf.t[:, h, :], qtT.t[:, h, t0:t0 + C], start=True, stop=False),
                                   (Sbf, qtT), (ob,))
                                op('pe', lambda e, h=h, ob=ob, oc=oc: e.matmul(ob.t[:, oc:oc + C], vv.t[r0:r0 + C, st, h * 128:(h + 1) * 128], scm.t[r0:r0 + C, h, r0:r0 + C],
                                                                            start=False, stop=True), (vv, scm), (ob,))
                            dec = eb.t[:, :, t0 + C - 1:t0 + C].to_broadcast([128, NH, 128])
                            op('dve', lambda e: e.tensor_tensor(Sst.t[:], Sst.t[:], dec, ALU.mult), (Sst, eb), (Sst,))

                        def cB(st=st, ch=ch):
                            pkv = PSB[3]
                            pk3 = pkv.t[:].rearrange("p (h v) -> p h v", h=NH)
                            op('dve', lambda e: e.tensor_tensor(Sbf.t[:], Sst.t[:], pk3, ALU.add), (Sst, pkv), (Sbf,))
                            op('dve', lambda e: e.tensor_tensor(Sst.t[:], Sst.t[:], pk3, ALU.add), (Sst, pkv), (Sst,))
                        stages += [cA, cB]
                return stages

            def G_chain():
                stages = []
                for hp in range(2):
                    sgb = sgX[hp]
                    sg3 = sgb.t[:, 0:2 * T].rearrange("p (h t) -> p h t", h=2)

                    def g0(hp=hp, sgb=sgb, sg3=sg3):
                        pg = PSB[4]
                        for hh in range(2):
                            h = hp * 2 + hh
                            for k in range(KC):
                                op('pe', lambda e, k=k, h=h, hh=hh: e.matmul(pg.t[:, hh * T:(hh + 1) * T], wg3[:, k, h * 128:(h + 1) * 128], hT.t[:, k, 0:T],
                                                                          start=(k == 0), stop=(k == KC - 1)),
                                   (hT, wb_g), (pg,))
                        pg3 = pg.t[:, 0:2 * T].rearrange("p (h t) -> p h t", h=2)
                        op('act', lambda e: e.activation(sg3, pg3, AF.Tanh, scale=0.5), (pg,), (sgb,))

                    def g1(hp=hp, sgb=sgb, sg3=sg3):
                        pg = PSB[4]
                        pg3 = pg.t[:, 0:2 * T].rearrange("p (h t) -> p h t", h=2)
                        op('dve', lambda e: e.scalar_tensor_tensor(sg3, sg3, 1.0, pg3, ALU.add, ALU.mult), (sgb, pg), (sgb,))
                    stages += [g0, g1]
                return stages

            def Q_chain():
                stages = []
                S = {}
                QB = rotator([PSB[5], PSB[6]])
                for hp in range(2):
                    def q0(hp=hp):
                        pq = QB()
                        for hh in range(2):
                            h = hp * 2 + hh
                            for kc in range(3):
                                op('pe', lambda e, kc=kc, h=h, hh=hh: e.matmul(pq.t[:, hh * T:(hh + 1) * T], wu3[:, kc, h * 192:h * 192 + 128], cqnT.t[:, kc, 0:T],
                                                                            start=(kc == 0), stop=(kc == 2)), (wb_u, cqnT), (pq,))
                        op('act', lambda e: e.activation(qnT.t[:, hp * 2:hp * 2 + 2, 0:T], pq.t[:, 0:2 * T].rearrange("p (h t) -> p h t", h=2), AF.Copy), (pq,), (qnT,))

                    def q1(hp=hp):
                        pp = S['pp'] = QB()
                        for hh in range(2):
                            h = hp * 2 + hh
                            for kc in range(3):
                                op('pe', lambda e, kc=kc, h=h, hh=hh: e.matmul(pp.t[0:64, hh * T:(hh + 1) * T], wu3[:, kc, h * 192 + 128:h * 192 + 192], cqnT.t[:, kc, 0:T],
                                                                            start=(kc == 0), stop=(kc == 2)), (wb_u, cqnT), (pp,))

                    def q2(hp=hp):
                        ps_ = S['ps'] = QB()
                        for hh in range(2):
                            h = hp * 2 + hh
                            for kc in range(3):
                                op('pe', lambda e, kc=kc, h=h, hh=hh: e.matmul(ps_.t[0:64, hh * T:(hh + 1) * T], wu3[:, kc, 768 + h * 64:768 + (h + 1) * 64], cqnT.t[:, kc, 0:T],
                                                                            start=(kc == 0), stop=(kc == 2)), (wb_u, cqnT), (ps_,))

                    def q3(hp=hp):
                        pp, ps_ = S['pp'], S['ps']
                        op('act', lambda e: e.activation(tB[0].t[0:64, 0:2 * T], pp.t[0:64, 0:2 * T], AF.Copy), (pp,), (tB[0],))
                        op('act', lambda e: e.activation(tB[1].t[0:64, 0:2 * T], ps_.t[0:64, 0:2 * T], AF.Copy), (ps_,), (tB[1],))

                    def q4(hp=hp):
                        cosb = ropeFs.t[:, 0:1, 0:T].to_broadcast([64, 2, T])
                        sinb = ropeFs.t[:, 1:2, 0:T].to_broadcast([64, 2, T])
                        t0v = tB[0].t[0:64, 0:2 * T].rearrange("p (h t) -> p h t", h=2)
                        t1v = tB[1].t[0:64, 0:2 * T].rearrange("p (h t) -> p h t", h=2)
                        op('pool', lambda e: e.tensor_tensor(t0v, t0v, cosb, ALU.mult), (tB[0], ropeFs), (tB[0],))
                        op('pool', lambda e: e.tensor_tensor(t1v, t1v, sinb, ALU.mult), (tB[1], ropeFs), (tB[1],))
                        op('pool', lambda e: e.tensor_tensor(qpeT.t[0:64, hp * 2:hp * 2 + 2, 0:T], t0v, t1v, ALU.add), (tB[0], tB[1]), (qpeT,))
                    stages += [q0, q1, q2, q3, q4]
                return stages

            def KV_chain():
                stages = []
                pb = PSB[7]
                for hp in range(2):
                    def k0(hp=hp):
                        for hh in range(2):
                            h = hp * 2 + hh
                            for kc in range(2):
                                op('pe', lambda e, h=h, hh=hh, kc=kc: e.matmul(pb.t[:, hh * T:(hh + 1) * T], wkv3[:, kc, h * 128:(h + 1) * 128], latT.t[:, kc, 0:T],
                                                                            start=(kc == 0), stop=(kc == 1)),
                                   (wb_kv, latT), (pb,))
                        for hh in range(2):
                            h = hp * 2 + hh
                            dst_ap, dst_buf = seq['KT_dst'](h, tok0, T)
                            op('act', lambda e, hh=hh, dst_ap=dst_ap: e.activation(dst_ap, pb.t[:, hh * T:(hh + 1) * T], AF.Copy), (pb,), (dst_buf,))
                    stages.append(k0)
                for st in range(nst):
                    def v0(st=st):
                        for kc in range(2):
                            op('pe', lambda e, kc=kc: e.matmul(pb.t[0:P, :], latT.t[:, kc, st * P:(st + 1) * P], wkv3[:, kc, 512:1024],
                                                             start=(kc == 0), stop=(kc == 1)),
                               (wb_kv, latT), (pb,))
                        dst_ap, dst_buf = seq['V_dst'](tok0, st, P)
                        op('dve', lambda e: e.tensor_copy(dst_ap, pb.t[0:P, :]), (pb,), (dst_buf,))
                    stages.append(v0)
                return stages

            run_chains([SEQ_chain(), G_chain(), Q_chain(), KV_chain()])
            if last:
                kb.dma('sp', seq['hg_out'].rearrange("h k v -> k h v"), Sst.t[:], (Sst,), (), semof=Sst, store=True)

            def R_chain(hp):
                ob = oTb[hp]
                ob3 = ob.t[:, 0:2 * T].rearrange("p (h t) -> p h t", h=2)
                sgb = sgX[hp]
                sg3 = sgb.t[:, 0:2 * T].rearrange("p (h t) -> p h t", h=2)
                rsb = rsX[hp]
                rs3 = rsb.t[:, 0:2 * T].rearrange("p (h t) -> p h t", h=2)
                sq3 = sqb.t[:, :, 0:T] if hp == 0 else khat.t[:, 0, 0:2 * T].rearrange("p (h t) -> p h t", h=2)
                sqbuf = sqb if hp == 0 else khat
                sq2 = sq3
                S = {}

                def r0():
                    op('act', lambda e: e.activation(sq3, ob3, AF.Square), (ob,), (sqbuf,))
                    pm = S['pm'] = RB()
                    op('pe', lambda e: e.matmul(pm.t[:, 0:2 * T], mean_bf.t[:], sq2, start=True, stop=True), (mean_bf, sqbuf), (pm,))

                def r1():
                    pm = S['pm']
                    pm3 = pm.t[:, 0:2 * T].rearrange("p (h t) -> p h t", h=2)
                    op('act', lambda e: e.activation(rs3, pm3, AF.Ln, bias=EPS), (pm,), (rsb,))

                def r2():
                    op('act', lambda e: e.activation(rs3, rs3, AF.Exp, scale=-0.5), (rsb,), (rsb,))

                def r3():
                    op('dve', lambda e: e.scalar_tensor_tensor(rs3, rs3, 0.5, ob3, ALU.mult, ALU.mult), (rsb, ob), (rsb,))

                def r4():
                    op('dve', lambda e: e.scalar_tensor_tensor(mixT.t[:, hp * 2:hp * 2 + 2, 0:T], rs3, colsb.t[:, HGG:HGG + 1], sg3, ALU.mult, ALU.mult),
                       (rsb, sgb, colsb), (mixT,))
                return [r0, r1, r2, r3, r4]

            run_chains([R_chain(0), R_chain(1)])
            attend(T, seq['keyblocks'](ti))
            wbs = {}

            wffn3 = wffn_b.rearrange("(j p) c -> p j c", p=128)
            dnbufs = [tA[0], tA[1], tC[0], tC[1], tD]
            wdns = {}

            def ffn_load_up(q):
                wb = wnext()
                kb.dma('sp', wb.t[:, 0:4096].rearrange("p (j c) -> p j c", j=2), wffn3[:, 2 * q:2 * q + 2, 0:2048], (wcvB,), (wb,), semof=wb)
                wbs[2 * q] = (wb, wb.t[:, 0:2048])
                wbs[2 * q + 1] = (wb, wb.t[:, 2048:4096])

            def ffn_load_dn(j):
                db = dnbufs[j % 5]
                dv = db.t[:, :].bitcast(BF16)
                kb.dma('sp', dv, wffn_b[j * 128:(j + 1) * 128, 2048:3072], (wcvB,), (db,), semof=db)
                wdns[j] = (db, dv)

            wos = []
            for half in range(2):
                wb_o = wload(wout_b[:, half * 4096:(half + 1) * 4096], 4096, wcvC)
                wos.append((wb_o, wb_o.t[:, 0:4096].rearrange("p (k c) -> p k c", k=KC)))
            ffn_load_up(0)
            ffn_load_up(1)
            for j in range(5):
                ffn_load_dn(j)
            for st in range(nst):
                for half in range(2):
                    wb_o, wo3 = wos[half]
                    po = pnext()
                    for c in range(KC):
                        op('pe', lambda e, c=c: e.matmul(po.t[0:P, :], mixT.t[:, c, st * P:(st + 1) * P], wo3[:, c, :], start=(c == 0), stop=(c == KC - 1)),
                           (mixT, wb_o), (po,))
                    tt = tB[st % 2]
                    op('dve', lambda e: e.tensor_tensor(tt.t[0:P, :], po.t[0:P, :], g1_bc.t[0:P, half * 512:(half + 1) * 512], ALU.mult), (po, g1_bc), (tt,))
                    op('pool', lambda e: e.tensor_tensor(xb.t[0:P, st, half * 512:(half + 1) * 512], xb.t[0:P, st, half * 512:(half + 1) * 512], tt.t[0:P, :], ALU.add),
                       (xps[st], tt), (xps[st],))
            for st in range(nst):
                norm_T_a(xps[st], st, P, scol=4 + st, xs_=(xsb if st == 0 else mixT))
            for st in range(nst):
                norm_T_b(st, P, G2c[r], modc[r], 2, hT, st * P, xs_=(xsb if st == 0 else mixT))
            acc = [[LB[st * 2 + half] for half in range(2)] for st in range(nst)]
            pavs = {}

            def ffn_up(j):
                wb, wv_ = wbs[j]
                wup = wv_.rearrange("p (k a c) -> p k a c", k=KC, a=2)
                pav = bank_of[('c', j)]
                for a_ in range(2):
                    for k in range(KC):
                        op('pe', lambda e, k=k, a_=a_: e.matmul(pav.t[:, a_ * T:(a_ + 1) * T], wup[:, k, a_, :], hT.t[:, k, 0:T], start=(k == 0), stop=(k == KC - 1)),
                           (wb, hT), (pav,))
                pavs[j] = pav

            def ffn_elem(j):
                pav = pavs[j]
                as_ = aS[j % 2]
                c1 = cc[j % 2]
                u = uT[j % 3]
                cwb = CW + j * 3
                op('pool', lambda e: e.tensor_copy(as_.t[:, 0:2], aprev.t[:, j, :]), (aprev,), (as_,))
                op('act', lambda e: e.activation(as_.t[:, 2:T + 2], pav.t[:, 0:T], AF.Copy), (pav,), (as_,))
                op('act', lambda e: e.activation(c1.t[:, 0:T], pav.t[:, 0:T], AF.Identity, scale=colsb.t[:, cwb + 2:cwb + 3], bias=colsb.t[:, CB + j:CB + j + 1]),
                   (pav, colsb), (c1,))
                op('pool', lambda e: e.tensor_copy(aprev.t[:, j, :], as_.t[:, T:T + 2]), (as_,), (aprev,))
                op('dve', lambda e: e.scalar_tensor_tensor(c1.t[:, 0:T], as_.t[:, 1:T + 1], colsb.t[:, cwb + 1:cwb + 2], c1.t[:, 0:T], ALU.mult, ALU.add),
                   (as_, colsb, c1), (c1,))
                op('dve', lambda e: e.scalar_tensor_tensor(c1.t[:, 0:T], as_.t[:, 0:T], colsb.t[:, cwb:cwb + 1], c1.t[:, 0:T], ALU.mult, ALU.add),
                   (as_, colsb, c1), (c1,))
                op('act', lambda e: e.activation(c1.t[:, 0:T], c1.t[:, 0:T], AF.Gelu), (c1,), (c1,))
                op('dve', lambda e: e.tensor_tensor(u.t[:, 0:T], c1.t[:, 0:T], pav.t[:, T:2 * T], ALU.mult), (c1, pav), (u,))

            def ffn_down(j):
                wbs.pop(j)
                wb, wdn = wdns.pop(j)
                u = uT[j % 3]
                for st in range(nst):
                    for half in range(2):
                        ab = acc[st][half]
                        op('pe', lambda e, st=st, half=half, ab=ab: e.matmul(ab.t[0:P, :], u.t[:, st * P:(st + 1) * P], wdn[:, half * 512:(half + 1) * 512],
                                                                          start=(j == 0), stop=(j == NJ - 1)), (u, wb), (ab,))

            ffn_load_up(2)
            ffn_load_up(3)
            items = [('c', j) for j in range(NJ)]
            post = {}
            if nxt is not None:
                nn = nxt[0]['nst']
                ins_at = [(8, 0)] + ([(16, 1)] if nn > 1 else [])
                for (cj, st_) in reversed(ins_at):
                    pos = items.index(('c', cj))
                    items[pos:pos] = [('p', st_, 0), ('p', st_, 1)]
                post[('c', 1)] = ('a', 0)
                if nn > 1:
                    post[('c', 8)] = ('a', 1)
                prefetch(nxt[0], nxt[1], g + 1, None, 'dma')
            bank_of = {it: PSB[i % 4] for i, it in enumerate(items)}
            AH = 3

            def stage1(it):
                if it[0] == 'c':
                    ffn_up(it[1])
                else:
                    prefetch(nxt[0], nxt[1], g + 1, bank_of[it], ('bh', it[1], it[2], 'pe'))

            def stage2(it):
                if it[0] == 'c':
                    ffn_elem(it[1])
                else:
                    prefetch(nxt[0], nxt[1], g + 1, bank_of[it], ('bh', it[1], it[2], 'ev'))

            for i in range(min(AH, len(items))):
                stage1(items[i])
            for i, it in enumerate(items):
                if it[0] == 'c':
                    j = it[1]
                    if j % 2 == 0 and j // 2 + 4 < NJ // 2:
                        ffn_load_up(j // 2 + 4)
                stage2(it)
                if it in post:
                    prefetch(nxt[0], nxt[1], g + 1, None, post[it])
                if i + AH < len(items):
                    stage1(items[i + AH])
                if it[0] == 'c':
                    ffn_down(it[1])
                    if it[1] + 5 < NJ:
                        ffn_load_dn(it[1] + 5)
            if nxt is not None:
                preloaded_w[g + 1] = (load_win(512, 1024), load_win(1024, 1536), load_win(2048, 2432), load_win(2432, 2752))
            def tail():
                if last:
                    kb.dma('sp', seq['cv_out'], aprev.t[:], (aprev,), (), semof=aprev, store=True)
                for st in range(nst):
                    for half in range(2):
                        ab = acc[st][half]
                        tt = tB[(st * 2 + half) % 2]
                        op('dve', lambda e: e.tensor_tensor(tt.t[0:P, :], ab.t[0:P, :], g2_bc.t[0:P, half * 512:(half + 1) * 512], ALU.mult), (ab, g2_bc), (tt,))
                        op('pool', lambda e: e.tensor_tensor(xb.t[0:P, st, half * 512:(half + 1) * 512], xb.t[0:P, st, half * 512:(half + 1) * 512], tt.t[0:P, :], ALU.add),
                           (xps[st], tt), (xps[st],))
                for st in range(nst):
                    xv = xb.t[0:P, st, :]
                    op('act', lambda e: e.activation(xsb.t[0:P, :], xv, AF.Square, accum_out=stat.t[0:P, 3:4]), (xps[st],), (xsb, stat))
                    rstd_from(3, P, D)
                    op('dve', lambda e: e.scalar_tensor_tensor(xv, xv, stat.t[0:P, 3:4], fg_bc.t[0:P, :], ALU.mult, ALU.mult), (xps[st], stat, fg_bc), (xps[st],))
                kb.dma('sp', seq['y_out'][tok0:tok0 + T, :].rearrange("(n p) d -> p n d", p=P), xb.t[0:P, 0:nst, :], tuple(xps[0:nst]), (), semof=xb, store=True)

            if (not last) and P == 128:
                pending_tail.append(tail)
            else:
                tail()

        kb.dma('sp', mixT.t[:, :, :].rearrange("p a b -> p (a b)"), wukv_b, (wcv0,), (mixT,), semof=mixT)
        wkv3c = mixT.t[:, :, :].rearrange("p a b -> p (a b)").rearrange("p (k c) -> p k c", k=2)
        msi = 0
        for g in range(PAST // TP):
            for _ in range(3):
                if msi < len(mod_steps):
                    mod_steps[msi]()
                    msi += 1
            cl = xparts[g % 2][0]
            clv = cl.t[:, 0, 0:640].rearrange("p (n c) -> p n c", n=2)
            kb.dma('sp', clv[:, :, 0:256], clat[g * TP:(g + 1) * TP, :].rearrange("(n p) c -> p n c", p=128), (), (cl,), semof=xbuf[g % 2])
            kb.dma('sp', clv[:, :, 256:320], ckpe[g * TP:(g + 1) * TP, :].rearrange("(n p) c -> p n c", p=128), (), (cl,), semof=xbuf[g % 2])
            for st in range(NST):
                pt2 = pnext()
                for kc in range(2):
                    op('pe', lambda e, kc=kc: e.transpose(pt2.t[:, kc * 128:(kc + 1) * 128], clv[:, st, kc * 128:(kc + 1) * 128], ident), (cl, cm), (pt2,))
                op('act', lambda e: e.activation(latT.t[:, :, st * 128:(st + 1) * 128], pt2.t[:, 0:256].rearrange("p (k t) -> p k t", k=2), AF.Copy), (pt2,), (latT,))
                pt3 = pnext()
                op('pe', lambda e: e.transpose(pt3.t[0:64, 0:128], clv[:, st, 256:320], ident), (cl, cm), (pt3,))
                c0 = g * TP + st * 128
                op('dve', lambda e: e.tensor_copy(kpeT.t[0:64, c0:c0 + 128], pt3.t[0:64, 0:128]), (pt3,), (kpeT,))
            kv_project(TP, wkv3c, mixT, lambda h, g=g: (KT.t[:, h, g * TP:(g + 1) * TP], KT),
                       lambda st, g=g: (Vr.t[:, g * NST + st, :], Vr), NST, 128)
        while msi < len(mod_steps):
            mod_steps[msi]()
            msi += 1
        mod_finish()
        kb.dma('pool', wuq_b.rearrange("p (a b) -> (p a) b", b=1024), w_uq_l.rearrange("p k c -> (p k) c"), reads=(modv,), semof=wcvC, nowait_w=True)
        kb.dma('pool', wout_b.rearrange("p (a b) -> (p a) b", b=2048), w_out_l.rearrange("p h k c -> p (h k c)").rearrange("p (a b) -> (p a) b", b=2048),
               reads=(modv,), semof=wcvC, nowait_w=True)
        wcvC.w = ('ls_wcvC', wcvC.lsem, wcvC.lcnt, 'dma')
        wsrc = w_ffn_l.rearrange("j p c -> (j p) c")
        RH = NJ * 128 // 2
        for r0 in range(0, NJ * 128, RH):
            kb.dma('pool', wffn_b[r0:r0 + RH, 0:2048], wsrc[r0:r0 + RH, 0:2048], reads=(modv,), semof=wcvB, nowait_w=True)
            kb.dma('pool', wffn_b[r0:r0 + RH, 2048:3072], wsrc[r0:r0 + RH, 2048:3072], reads=(modv,), semof=wcvB, nowait_w=True)
        wcvB.w = ('ls_wcvB', wcvB.lsem, wcvB.lcnt, 'dma')


        def full_block(kt):
            return (lambda h: (KT.t[:, h, kt * 128:(kt + 1) * 128], KT), kpeT.t[:, kt * 128:(kt + 1) * 128], kpeT,
                    lambda h: (Vr.t[:, kt, h * 128:(h + 1) * 128], Vr), 128)

        def sample_blocks(ti):
            bl = []
            for kt in range(PAST // 128):
                bl.append(full_block(kt) + (0, False))
            bl.append((lambda h: (KTs.t[:, h, :], KTs), kpeTs.t[:, :], kpeTs, lambda h: (Vs.t[:, h * 128:(h + 1) * 128], Vs), S_S, 0, False))
            return bl

        seq_s = dict(P=S_S, nst=1, C=S_S, r=1, ntiles=1, pos0=PAST, x=xs_in, y_out=y_s, lat_out=lat_s, kpe_out=kpe_s, hg_out=hg_s, cv_out=cv_s,
                     kpe_dst=lambda t0, P: (kpeTs.t[0:64, t0:t0 + P], kpeTs),
                     KT_dst=lambda h, t0, T: (KTs.t[:, h, t0:t0 + T], KTs),
                     V_dst=lambda t0, st, P: (Vs.t[0:P, :], Vs),
                     keyblocks=sample_blocks)
        def prompt_blocks(ti):
            bl = []
            for kt in range(ti * NST):
                bl.append(full_block(kt) + (0, False))
            for j in range(NST):
                bl.append(full_block(ti * NST + j) + (j * 128, True))
            return bl

        seq_p = dict(P=128, nst=NST, C=64, r=0, ntiles=S_P // TP, pos0=0, x=xp, y_out=y_p, lat_out=lat_p, kpe_out=kpe_p, hg_out=hg_p, cv_out=cv_p,
                     kpe_dst=lambda t0, P: (kpeT.t[0:64, t0:t0 + P], kpeT),
                     KT_dst=lambda h, t0, T: (KT.t[:, h, t0:t0 + T], KT),
                     V_dst=lambda t0, st, P: (Vr.t[:, t0 // 128 + st, :], Vr),
                     keyblocks=prompt_blocks)
        kb.dma('sp', Sst.t[:], shg.rearrange("h k v -> k h v"), (), (Sst,), semof=Sst)
        op('act', lambda e: e.activation(Sbf.t[:], Sst.t[:], AF.Copy), (Sst,), (Sbf,))
        kb.dma('sp', aprev.t[:], sconv, (), (aprev,), semof=aprev)
        load_gbc(1, S_S)
        run_tile(seq_s, 0, 0, (seq_p, 0))

        op('pool', lambda e: e.memset(Sst.t[:], 0.0), (), (Sst,))
        op('pool', lambda e: e.memset(Sbf.t[:], 0.0), (), (Sbf,))
        op('pool', lambda e: e.memset(aprev.t[:], 0.0), (), (aprev,))
        load_gbc(0, 128)

        for ti in range(seq_p['ntiles']):
            run_tile(seq_p, ti, 1 + ti, (seq_p, ti + 1) if ti + 1 < seq_p['ntiles'] else None)
        kb.finish()
        build_nc.stats = (kb.ninst, kb.nsem, dict(kb.cnt))
        build_nc.used = kb.used
        build_nc.sbuf_left = nc.sbuf_bytes_remaining
    return nc


def _rope_tables():
    half = 32
    inv_freq = (np.float32(10000.0) ** (-np.arange(half, dtype=np.float32) / np.float32(half))).astype(np.float32)
    pos = np.arange(S_P + S_S, dtype=np.float32)
    ang = (pos[:, None] * inv_freq[None, :]).astype(np.float32)
    cos = np.cos(ang).astype(np.float32)
    sin = np.sin(ang).astype(np.float32)
    ropeT = np.concatenate([cos, cos, sin, sin], axis=1).astype(np.float32)
    ropeF = np.stack([np.concatenate([cos.T, cos.T], 0), np.concatenate([-sin.T, sin.T], 0)], axis=1)
    return np.ascontiguousarray(ropeT), np.ascontiguousarray(ropeF.astype(np.float32))


def _const_mats():
    idx = np.arange(128)
    same = (idx[:, None] // 64) == (idx[None, :] // 64)
    U = (same & (idx[:, None] <= idx[None, :])).astype(np.float32)
    Ux = (same & (idx[:, None] > idx[None, :])).astype(np.float32)
    return np.ascontiguousarray(np.stack([np.eye(128, dtype=np.float32), U, Ux], axis=1))


def kernel(x_prompt, x_sample, c_prompt, c_sample, cache_kv_latent, cache_k_rope, state_hgrn,
           state_ffn_conv, w_ada, b_ada, norm_mix_gain, w_in, hg_lb_logits, hg_norm_gain,
           mla_q_norm_gain, mla_kv_norm_gain, w_uq, w_uk, w_uv, w_out, norm_ffn_gain, w_up,
           conv_w, conv_b, w_down, final_norm_gain):
    f = lambda a: np.ascontiguousarray(np.asarray(a, dtype=np.float32))
    x_prompt, x_sample, c_prompt, c_sample = f(x_prompt), f(x_sample), f(c_prompt), f(c_sample)
    n = 8
    ropeT, ropeF = _rope_tables()
    cmat = _const_mats()
    cols = np.zeros((128, 128), np.float32)
    cols[:, 0:8] = f(norm_mix_gain)[0].reshape(8, 128).T
    cols[:, 8:16] = f(norm_ffn_gain)[0].reshape(8, 128).T
    cols[:, 16:19] = f(mla_q_norm_gain)[0].reshape(3, 128).T
    cols[:, 19] = f(hg_norm_gain)[0]
    cw = f(conv_w)[0].reshape(3, NJ, 128)
    cols[:, 20:20 + 66] = cw.transpose(2, 1, 0).reshape(128, 66)
    cols[:, 86:86 + NJ] = f(conv_b)[0].reshape(NJ, 128).T
    bcs = np.zeros((4, 1024), np.float32)
    bcs[0] = f(final_norm_gain)
    bcs[1, 0:256] = f(mla_kv_norm_gain)[0]
    bcs[2, 0:512] = f(hg_lb_logits)[0]
    bcs[3, 0:512] = f(hg_lb_logits)[1]
    w_in_l = np.ascontiguousarray(f(w_in)[0].reshape(KC, 128, 2752).transpose(1, 0, 2))
    wq = f(w_uq)[0]
    sw = []
    for h in range(NH):
        sw.append(wq[:, h * 192 + 160:h * 192 + 192])
        sw.append(wq[:, h * 192 + 128:h * 192 + 160])
    wq_ext = np.concatenate([wq] + sw, axis=1)
    w_uq_l = np.ascontiguousarray(wq_ext.reshape(3, 128, 1024).transpose(1, 0, 2))
    wkv = np.concatenate([f(w_uk)[0].reshape(256, 512), f(w_uv)[0].reshape(256, 512)], axis=1)
    w_ukv_l = np.ascontiguousarray(wkv.reshape(2, 128, 1024).transpose(1, 0, 2))
    wo = f(w_out)[0]
    w_out_l = np.ascontiguousarray(wo.reshape(KC, 128, 2, 512).transpose(1, 2, 0, 3))
    wu = f(w_up)[0]
    wa = wu[:, :DFF].reshape(KC, 128, NJ, 128)
    wv = wu[:, DFF:].reshape(KC, 128, NJ, 128)
    wup_l = np.stack([wa, wv], axis=3)
    wup_l = wup_l.transpose(2, 1, 0, 3, 4).reshape(NJ, 128, 2048)
    wd_l = f(w_down)[0].reshape(NJ, 128, 1024)
    w_ffn_l = np.ascontiguousarray(np.concatenate([wup_l, wd_l], axis=2))
    b_ada2 = np.ascontiguousarray(np.broadcast_to(f(b_ada)[0][None, :], (2, 6 * D)))
    shared = dict(w_ada=f(w_ada)[0], b_ada2=b_ada2, cols=cols, bcs=bcs, cmat=cmat, ropeT=ropeT, ropeF=ropeF,
                  w_in_l=w_in_l, w_uq_l=w_uq_l, w_ukv_l=w_ukv_l, w_out_l=w_out_l, w_ffn_l=w_ffn_l)
    in_maps = []
    for b in range(n):
        c2 = np.stack([c_prompt[b], c_sample[b]], axis=0)
        c2l = np.ascontiguousarray(c2.reshape(2, KC, 128).transpose(2, 1, 0))
        sc = f(state_ffn_conv)[0, b]
        scl = np.ascontiguousarray(sc.reshape(2, NJ, 128).transpose(2, 1, 0))
        m = dict(shared)
        m.update(xp=x_prompt[b], xs=x_sample[b], c2=c2l, clat=f(cache_kv_latent)[0, b], ckpe=f(cache_k_rope)[0, b],
                 shg=f(state_hgrn)[0, b], sconv=scl)
        in_maps.append(m)
    build_nc()
    needed = {e: [] for e in ['pe', 'act', 'dve', 'pool']}
    for (e, i) in build_nc.used:
        needed[e].append(i)
    for e in needed:
        needed[e].sort()
    nc = build_nc(needed)
    res = run_bass_kernel_spmd(nc, in_maps, core_ids=list(range(n)))
    R = res.results

    def st(name):
        return np.stack([np.asarray(R[b][name], dtype=np.float32) for b in range(n)], axis=0)

    def cvfix(a):
        return np.ascontiguousarray(a.transpose(0, 3, 2, 1).reshape(n, 2, DFF))

    y_prompt = st("y_p")
    y_sample = st("y_s")
    return (y_prompt, y_sample, st("lat_p")[None], st("kpe_p")[None], st("hg_p")[None], cvfix(st("cv_p"))[None],
            st("lat_s")[None], st("kpe_s")[None], st("hg_s")[None], cvfix(st("cv_s"))[None])
```
